# Optimizing a Trainium2 kernel written in Bass

```python
import math
import jax, jax.numpy as jnp
from jax import lax
import numpy as np

D_MODEL = 1024
BATCH = 16
SEQ = 2048
DEPTH = 4

HEAD_DIM = 64
N_HEADS = D_MODEL // HEAD_DIM
N_KV_HEADS = 4
GROUP = N_HEADS // N_KV_HEADS
HQ = N_HEADS * HEAD_DIM
HKV = N_KV_HEADS * HEAD_DIM
N_A_LAYERS = DEPTH // 2
N_B_LAYERS = DEPTH - N_A_LAYERS
SWA_WINDOW = 128
Q_BLOCK = 128
N_BRANCH = 3
CMP_LEN = 32
CMP_STRIDE = 16
CMP_HIDDEN = 256
SEL_BLOCK = 64
SEL_TOP = 8
SEL_FORCE_LOCAL = 2
SEL_CHUNK = 32
NSA_WINDOW = 512
NUM_BUCKETS = 32
MAX_DISTANCE = 128
EPS = 1e-6
NEG = -1e30
FORCE_BONUS = 1e6
A_IN = 2 * HQ + 2 * HKV
B_IN = HQ + N_BRANCH * N_HEADS + N_BRANCH * HQ

kernel_name = "yoco_swa_sink_nsa_hybrid"


def rms_norm(x, g):
    xf = x.astype(jnp.float32)
    xf = xf * lax.rsqrt(jnp.mean(xf * xf, axis=-1, keepdims=True) + EPS)
    return (xf * g.astype(jnp.float32)).astype(x.dtype)


def t5_bucket(dist):
    d = jnp.maximum(dist, 0)
    max_exact = NUM_BUCKETS // 2
    large = max_exact + (jnp.log(jnp.maximum(d, 1).astype(jnp.float32) / max_exact)
                         / math.log(MAX_DISTANCE / max_exact)
                         * (NUM_BUCKETS - max_exact)).astype(jnp.int32)
    large = jnp.minimum(large, NUM_BUCKETS - 1)
    return jnp.where(d < max_exact, d, large)


def head_bias(table, dist):
    b = table[t5_bucket(dist)]
    b = jnp.moveaxis(b, -1, 0)
    return b.reshape(N_KV_HEADS, GROUP, *dist.shape)


def banded_attention(q, k, v, window, table, sink):
    B_, S = q.shape[0], q.shape[1]
    nb = S // Q_BLOCK
    n_prev = -(-(window - 1) // Q_BLOCK)
    pad = n_prev * Q_BLOCK
    span = pad + Q_BLOCK
    kp = jnp.pad(k, ((0, 0), (pad, 0), (0, 0), (0, 0)))
    vp = jnp.pad(v, ((0, 0), (pad, 0), (0, 0), (0, 0)))
    scale = HEAD_DIM ** -0.5

    def block(i):
        start = i * Q_BLOCK
        qb = lax.dynamic_slice_in_dim(q, start, Q_BLOCK, axis=1)
        kb = lax.dynamic_slice_in_dim(kp, start, span, axis=1)
        vb = lax.dynamic_slice_in_dim(vp, start, span, axis=1)
        qpos = start + jnp.arange(Q_BLOCK)
        kpos = start - pad + jnp.arange(span)
        dist = qpos[:, None] - kpos[None, :]
        mask = (dist >= 0) & (dist < window) & (kpos[None, :] >= 0)
        logits = jnp.einsum('bqhgd,bkhd->bhgqk', qb, kb,
                            preferred_element_type=jnp.float32) * scale
        logits = jnp.where(mask, logits + head_bias(table, dist), NEG)
        if sink is None:
            p = jax.nn.softmax(logits, axis=-1)
        else:
            s = sink.astype(jnp.float32)[None, :, :, None, None]
            m = jnp.maximum(jnp.max(logits, axis=-1, keepdims=True), s)
            e = jnp.exp(logits - m)
            p = e / (jnp.sum(e, axis=-1, keepdims=True) + jnp.exp(s - m))
        return jnp.einsum('bhgqk,bkhd->bqhgd', p.astype(v.dtype), vb)

    out = lax.map(block, jnp.arange(nb))
    return out.transpose(1, 0, 2, 3, 4, 5).reshape(q.shape)


def swa_layer(x, norm_g, w_in, q_gain, k_gain, sink, w_out, table):
    B_, S, _ = x.shape
    proj = rms_norm(x, norm_g) @ w_in
    q = rms_norm(proj[..., :HQ].reshape(B_, S, N_KV_HEADS, GROUP, HEAD_DIM), q_gain)
    k = rms_norm(proj[..., HQ:HQ + HKV].reshape(B_, S, N_KV_HEADS, HEAD_DIM), k_gain)
    v = proj[..., HQ + HKV:HQ + 2 * HKV].reshape(B_, S, N_KV_HEADS, HEAD_DIM)
    z = proj[..., HQ + 2 * HKV:]
    o = banded_attention(q, k, v, SWA_WINDOW, table, sink.reshape(N_KV_HEADS, GROUP))
    o = o.reshape(B_, S, HQ) * jax.nn.silu(z)
    return x + o @ w_out


def compress(t, pos_emb, w1, w2):
    B_, S = t.shape[0], t.shape[1]
    r = CMP_LEN // CMP_STRIDE
    chunks = t.reshape(B_, S // CMP_STRIDE, CMP_STRIDE, N_KV_HEADS, HEAD_DIM)
    n_cmp = S // CMP_STRIDE - r + 1
    blocks = jnp.concatenate([chunks[:, j:j + n_cmp] for j in range(r)], axis=2)
    blocks = blocks + pos_emb[:, None, :]
    flat = blocks.transpose(0, 1, 3, 2, 4).reshape(B_, n_cmp, N_KV_HEADS, CMP_LEN * HEAD_DIM)
    return jax.nn.silu(flat @ w1) @ w2


def nsa_shared_kv(x, norm_g, w_kv, k_gain, cmp_k_pos, cmp_k_w1, cmp_k_w2,
                  cmp_v_pos, cmp_v_w1, cmp_v_w2):
    B_, S, _ = x.shape
    kv = (rms_norm(x, norm_g) @ w_kv).reshape(B_, S, 2 * N_BRANCH, N_KV_HEADS, HEAD_DIM)
    k_cmp = rms_norm(compress(kv[:, :, 0], cmp_k_pos, cmp_k_w1, cmp_k_w2), k_gain[0])
    v_cmp = compress(kv[:, :, 1], cmp_v_pos, cmp_v_w1, cmp_v_w2)
    k_slc = rms_norm(kv[:, :, 2], k_gain[1])
    v_slc = kv[:, :, 3]
    k_win = rms_norm(kv[:, :, 4], k_gain[2])
    v_win = kv[:, :, 5]
    return k_cmp, v_cmp, k_slc, v_slc, k_win, v_win


def selected_attention(q, k, v, sel_idx, table):
    B_, S = q.shape[0], q.shape[1]
    nsel = S // SEL_BLOCK
    n_top = sel_idx.shape[-1]
    n_keys = n_top * SEL_BLOCK
    kb = k.reshape(B_, nsel, SEL_BLOCK, N_KV_HEADS, HEAD_DIM).transpose(0, 3, 1, 2, 4)
    kb = kb.reshape(B_, N_KV_HEADS, nsel, SEL_BLOCK * HEAD_DIM)
    vb = v.reshape(B_, nsel, SEL_BLOCK, N_KV_HEADS, HEAD_DIM).transpose(0, 3, 1, 2, 4)
    vb = vb.reshape(B_, N_KV_HEADS, nsel, SEL_BLOCK * HEAD_DIM)
    table_t = table.reshape(NUM_BUCKETS, N_KV_HEADS, GROUP).transpose(1, 0, 2)
    head_idx = jnp.arange(N_KV_HEADS)[None, :, None, None]
    offs = jnp.arange(SEL_BLOCK)
    scale = HEAD_DIM ** -0.5

    def chunk(c):
        start = c * SEL_CHUNK
        qc = lax.dynamic_slice_in_dim(q, start, SEL_CHUNK, axis=1)
        ic = lax.dynamic_slice_in_dim(sel_idx, start, SEL_CHUNK, axis=2)
        flat = ic.reshape(B_, N_KV_HEADS, SEL_CHUNK * n_top, 1)
        kg = jnp.take_along_axis(kb, flat, axis=2).reshape(B_, N_KV_HEADS, SEL_CHUNK, n_keys, HEAD_DIM)
        vg = jnp.take_along_axis(vb, flat, axis=2).reshape(B_, N_KV_HEADS, SEL_CHUNK, n_keys, HEAD_DIM)
        kpos = (ic[..., None] * SEL_BLOCK + offs).reshape(B_, N_KV_HEADS, SEL_CHUNK, n_keys)
        dist = (start + jnp.arange(SEL_CHUNK))[None, None, :, None] - kpos
        logits = jnp.einsum('bqhgd,bhqkd->bhgqk', qc, kg,
                            preferred_element_type=jnp.float32) * scale
        bias = jnp.moveaxis(table_t[head_idx, t5_bucket(dist)], -1, 2)
        logits = jnp.where((dist >= 0)[:, :, None], logits + bias, NEG)
        p = jax.nn.softmax(logits, axis=-1)
        return jnp.einsum('bhgqk,bhqkd->bqhgd', p.astype(v.dtype), vg)

    out = lax.map(chunk, jnp.arange(S // SEL_CHUNK))
    return out.transpose(1, 0, 2, 3, 4, 5).reshape(q.shape)


def nsa_layer(x, norm_g, w_in, q_gain, w_out, table, k_cmp, v_cmp, k_slc, v_slc, k_win, v_win):
    B_, S, _ = x.shape
    proj = rms_norm(x, norm_g) @ w_in
    q = rms_norm(proj[..., :HQ].reshape(B_, S, N_KV_HEADS, GROUP, HEAD_DIM), q_gain)
    gate_logits = proj[..., HQ:HQ + N_BRANCH * N_HEADS].reshape(B_, S, N_BRANCH, N_HEADS)
    z = proj[..., HQ + N_BRANCH * N_HEADS:].reshape(B_, S, N_BRANCH, N_HEADS, HEAD_DIM)
    pos = jnp.arange(S)
    scale = HEAD_DIM ** -0.5

    n_cmp = k_cmp.shape[1]
    cmp_start = jnp.arange(n_cmp) * CMP_STRIDE
    dist_c = pos[:, None] - (cmp_start + CMP_LEN - 1)[None, :]
    valid_c = dist_c >= 0
    logits = jnp.einsum('bshgd,bchd->bhgsc', q, k_cmp,
                        preferred_element_type=jnp.float32) * scale
    logits = jnp.where(valid_c, logits + head_bias(table, dist_c), NEG)
    p_cmp = jax.nn.softmax(logits, axis=-1) * valid_c
    o_cmp = jnp.einsum('bhgsc,bchd->bshgd', p_cmp.astype(v_cmp.dtype), v_cmp)

    nsel = S // SEL_BLOCK
    sel_start = jnp.arange(nsel) * SEL_BLOCK
    overlap = ((cmp_start[:, None] < sel_start[None, :] + SEL_BLOCK)
               & (cmp_start[:, None] + CMP_LEN > sel_start[None, :])).astype(jnp.float32)
    imp = jnp.einsum('bhsc,cj->bhsj', p_cmp.sum(axis=2), overlap)
    blk = jnp.arange(nsel)
    causal = sel_start[None, :] <= pos[:, None]
    rel_blk = (pos // SEL_BLOCK)[:, None] - blk[None, :]
    forced = (blk[None, :] == 0) | ((rel_blk >= 0) & (rel_blk < SEL_FORCE_LOCAL))
    score = jnp.where(causal, imp + jnp.where(forced, FORCE_BONUS, 0.0), NEG)
    _, sel_idx = lax.top_k(score, min(SEL_TOP, nsel))
    o_slc = selected_attention(q, k_slc, v_slc, sel_idx, table)

    o_win = banded_attention(q, k_win, v_win, NSA_WINDOW, table, None)

    o_all = jnp.stack([o_cmp, o_slc, o_win], axis=2).reshape(B_, S, N_BRANCH, N_HEADS, HEAD_DIM)
    o = jnp.einsum('bschd,bsch->bshd', o_all * jax.nn.silu(z), jax.nn.sigmoid(gate_logits))
    return x + o.reshape(B_, S, HQ) @ w_out


def setup_inputs(seed: int = 0) -> dict:
    key = jax.random.key(seed)
    ks = jax.random.split(key, 24)
    f32 = jnp.float32
    nrm = lambda k, shape, s: jax.random.normal(k, shape, f32) * s
    return {
        "x": nrm(ks[0], (BATCH, SEQ, D_MODEL), 1.0),
        "rel_table": nrm(ks[1], (NUM_BUCKETS, N_HEADS), 0.5),
        "a_norm": 1.0 + nrm(ks[2], (N_A_LAYERS, D_MODEL), 0.02),
        "a_w_in": nrm(ks[3], (N_A_LAYERS, D_MODEL, A_IN), D_MODEL ** -0.5),
        "a_q_gain": 1.0 + nrm(ks[4], (N_A_LAYERS, HEAD_DIM), 0.02),
        "a_k_gain": 1.0 + nrm(ks[5], (N_A_LAYERS, HEAD_DIM), 0.02),
        "a_sink": nrm(ks[6], (N_A_LAYERS, N_HEADS), 1.0),
        "a_w_out": nrm(ks[7], (N_A_LAYERS, HQ, D_MODEL), HQ ** -0.5),
        "kv_norm": 1.0 + nrm(ks[8], (D_MODEL,), 0.02),
        "kv_w": nrm(ks[9], (D_MODEL, 2 * N_BRANCH * HKV), D_MODEL ** -0.5),
        "kv_k_gain": 1.0 + nrm(ks[10], (N_BRANCH, HEAD_DIM), 0.02),
        "cmp_k_pos": nrm(ks[11], (CMP_LEN, HEAD_DIM), 0.1),
        "cmp_k_w1": nrm(ks[12], (CMP_LEN * HEAD_DIM, CMP_HIDDEN), (CMP_LEN * HEAD_DIM) ** -0.5),
        "cmp_k_w2": nrm(ks[13], (CMP_HIDDEN, HEAD_DIM), CMP_HIDDEN ** -0.5),
        "cmp_v_pos": nrm(ks[14], (CMP_LEN, HEAD_DIM), 0.1),
        "cmp_v_w1": nrm(ks[15], (CMP_LEN * HEAD_DIM, CMP_HIDDEN), (CMP_LEN * HEAD_DIM) ** -0.5),
        "cmp_v_w2": nrm(ks[16], (CMP_HIDDEN, HEAD_DIM), CMP_HIDDEN ** -0.5),
        "b_norm": 1.0 + nrm(ks[17], (N_B_LAYERS, D_MODEL), 0.02),
        "b_w_in": nrm(ks[18], (N_B_LAYERS, D_MODEL, B_IN), D_MODEL ** -0.5),
        "b_q_gain": 1.0 + nrm(ks[19], (N_B_LAYERS, HEAD_DIM), 0.02),
        "b_w_out": nrm(ks[20], (N_B_LAYERS, HQ, D_MODEL), HQ ** -0.5),
    }


def reference(x, rel_table, a_norm, a_w_in, a_q_gain, a_k_gain, a_sink, a_w_out,
              kv_norm, kv_w, kv_k_gain, cmp_k_pos, cmp_k_w1, cmp_k_w2,
              cmp_v_pos, cmp_v_w1, cmp_v_w2, b_norm, b_w_in, b_q_gain, b_w_out):
    shared = None
    for layer in range(DEPTH):
        if layer < N_A_LAYERS:
            x = swa_layer(x, a_norm[layer], a_w_in[layer], a_q_gain[layer], a_k_gain[layer],
                          a_sink[layer], a_w_out[layer], rel_table)
        else:
            if layer == N_A_LAYERS:
                shared = nsa_shared_kv(x, kv_norm, kv_w, kv_k_gain, cmp_k_pos, cmp_k_w1,
                                       cmp_k_w2, cmp_v_pos, cmp_v_w1, cmp_v_w2)
            j = layer - N_A_LAYERS
            x = nsa_layer(x, b_norm[j], b_w_in[j], b_q_gain[j], b_w_out[j], rel_table, *shared)
    return x
```

```python
import math
from contextlib import ExitStack
import numpy as np
import concourse.bass as bass
import concourse.mybir as mybir
from concourse.bass_utils import run_bass_kernel_spmd

F32 = mybir.dt.float32
BF16 = mybir.dt.bfloat16
AF = mybir.ActivationFunctionType
ALU = mybir.AluOpType

S = 2048
D = 1024
NCH = 4
EPS = 1e-6
NEGM = -30000.0
NOPREFETCH = False
HP = [0, 4, 1, 5, 2, 6, 3, 7, 8, 12, 9, 13, 10, 14, 11, 15]


class _Op:
    __slots__ = ("eng", "fn", "deps", "signal", "tok", "isdma", "idx")


class Prog:
    ENGS = ("pe", "act", "dve", "pool", "sp")

    def __init__(self, nc, es):
        self.nc, self.es = nc, es
        self.sems = {("eng", e): es.enter_context(nc.semaphore(f"sem_e_{e}")) for e in self.ENGS}
        self.bar = es.enter_context(nc.semaphore("sem_bar"))
        self.eng_cnt = {e: 0 for e in self.ENGS}
        self.dma_cnt = {}
        self.nflush = 0
        self._reset()

    def _reset(self):
        self.ops = {e: [] for e in self.ENGS}
        self.last_w = {}
        self.readers = {}
        self.n = 0

    def op(self, eng, fn, reads=(), writes=(), dma=None):
        o = _Op()
        o.eng, o.fn, o.signal, o.isdma = eng, fn, False, dma is not None
        o.idx = self.n
        self.n += 1
        deps = []
        for b in reads:
            deps.extend(self.last_w.get(b, {}).values())
        for b in writes:
            deps.extend(self.last_w.get(b, {}).values())
            deps.extend(self.readers.get(b, {}).values())
        best = {}
        for d in deps:
            if d.isdma:
                k = ("dma", d.tok[0])
            else:
                if d.eng == eng and eng == "pe":
                    continue
                k = ("eng", d.eng)
            if k not in best or best[k].idx < d.idx:
                best[k] = d
        o.deps = list(best.values())
        for d in o.deps:
            d.signal = True
        wk = ("dma", dma) if o.isdma else ("eng", eng)
        for b in reads:
            self.readers.setdefault(b, {})[wk] = o
        for b in writes:
            self.last_w.setdefault(b, {})[wk] = o
            self.readers[b] = {}
        if o.isdma:
            if dma not in self.dma_cnt:
                self.dma_cnt[dma] = 0
                self.sems[dma] = self.es.enter_context(self.nc.semaphore(f"sem_d{len(self.dma_cnt)}"))
            self.dma_cnt[dma] += 16
            o.tok = (dma, self.dma_cnt[dma])
        self.ops[eng].append(o)
        return o

    def flush(self):
        nc, sems = self.nc, self.sems
        for e in self.ENGS:
            c = self.eng_cnt[e]
            for o in self.ops[e]:
                if not o.isdma:
                    if o.signal:
                        c += 1
                    o.tok = (("eng", e), c)
            self.eng_cnt[e] = c
        ops = self.ops
        self.nflush += 1
        bar_target = 5 * self.nflush
        bar = self.bar
        dma_tot = dict(self.dma_cnt)

        def run(engname):
            def body(eng):
                waited = {}
                mydma = set()
                for o in ops[engname]:
                    for d in o.deps:
                        key, val = d.tok
                        if waited.get(key, 0) < val:
                            eng.wait_ge(sems[key], val)
                            waited[key] = val
                    if o.fn is None:
                        continue
                    ins = o.fn(eng)
                    if o.isdma:
                        ins.then_inc(sems[o.tok[0]], 16)
                        mydma.add(o.tok[0])
                    elif o.signal:
                        ins.then_inc(sems[o.tok[0]], 1)
                eng.drain()
                for k in mydma:
                    eng.wait_ge(sems[k], dma_tot[k])
                eng.sem_inc(bar, 1)
                eng.wait_ge(bar, bar_target)
            return body

        with nc.Block() as blk:
            blk.tensor(run("pe"))
            blk.scalar(run("act"))
            blk.vector(run("dve"))
            blk.gpsimd(run("pool"))
            blk.sync(run("sp"))
        self._reset()


def _t5_bucket(d):
    d = np.maximum(d, 0)
    large = 16 + (np.log(np.maximum(d, 1).astype(np.float32) / np.float32(16)) / np.float32(math.log(8.0))
                  * np.float32(16)).astype(np.int32)
    large = np.minimum(large, 31)
    return np.where(d < 16, d, large)


def _prep_shared(inp):
    f = np.float32
    tab = np.asarray(inp["rel_table"], f)
    k = np.arange(128)[:, None]
    q = np.arange(128)[None, :]
    d0 = q - k
    d1 = 128 + q - k
    dc = q - 16 * k + 1889
    G = np.empty((128, 3, 16, 128), f)
    G[:, 0] = np.transpose(tab[_t5_bucket(d0)], (0, 2, 1))
    G[:, 1] = np.transpose(tab[_t5_bucket(d1)], (0, 2, 1))
    G[:, 2] = np.transpose(tab[_t5_bucket(dc)], (0, 2, 1))
    NM = np.zeros((128, 3, 128), f)
    NM[:, 0] = np.where(d0 >= 0, 0.0, NEGM)
    NM[:, 2] = np.where(dc >= 0, 0.0, NEGM)
    NM4 = np.where(q < k, 0.0, NEGM).astype(f)
    ident = np.eye(128, dtype=f)
    bones = np.kron(np.eye(2, dtype=f), np.ones((64, 64), f))
    ind = np.zeros((32, 2048), f)
    ind[np.arange(2048) // 64, np.arange(2048)] = 1.0
    pos = (np.arange(16)[None, :, None] * 128 + np.arange(128)[:, None, None])
    j = np.arange(32)[None, None, :]
    pb = pos // 64
    causal = j <= pb
    forced = (j == 0) | ((pb - j >= 0) & (pb - j < 2))
    MF = np.where(causal, np.where(forced, 1e6, 0.0), -1e30).astype(f)
    cs = np.arange(128)[:, None] * 16
    ss = np.arange(32)[None, :] * 64
    ov = ((cs < ss + 64) & (cs + 32 > ss)).astype(f)
    ov[127] = 0.0
    ovx = np.concatenate([ov, np.ones((128, 1), f)], axis=1)
    qcols = np.concatenate([np.arange(h * 64, (h + 1) * 64) for h in HP])
    acols = np.concatenate([np.arange(1024, 1536), qcols[:512], 1536 + np.arange(0, 512),
                            qcols[512:], 1536 + np.arange(512, 1024)])
    a_w_in = np.ascontiguousarray(np.asarray(inp["a_w_in"], f)[:, :, acols])
    bcols = []
    for H in range(2):
        bcols.append(qcols[H * 512:(H + 1) * 512])
        for jj in range(4):
            for m in range(2):
                h = HP[2 * (4 * H + jj) + m]
                for c in range(3):
                    bcols.append(1072 + c * 1024 + h * 64 + np.arange(64))
    bcols.append(1024 + np.arange(48))
    bcols = np.concatenate(bcols)
    b_w_in = np.ascontiguousarray(np.asarray(inp["b_w_in"], f)[:, :, bcols])

    def colmajor(v):
        return np.ascontiguousarray(np.asarray(v, f).reshape(8, 128).T)
    ng = np.stack([colmajor(inp["a_norm"][0]), colmajor(inp["a_norm"][1]), colmajor(inp["kv_norm"]),
                   colmajor(inp["b_norm"][0]), colmajor(inp["b_norm"][1])], axis=1)
    hgl = [inp["a_q_gain"][0], inp["a_q_gain"][1], inp["a_k_gain"][0], inp["a_k_gain"][1],
           inp["kv_k_gain"][0], inp["kv_k_gain"][1], inp["kv_k_gain"][2], inp["b_q_gain"][0], inp["b_q_gain"][1]]
    hg = np.stack([np.tile(np.asarray(v, f), 2) for v in hgl], axis=1)
    sinkrep = np.ascontiguousarray(np.broadcast_to(np.asarray(inp["a_sink"], f)[None], (128, 2, 16)))
    b31rep = np.ascontiguousarray(np.broadcast_to(tab[31][None], (128, 16)))
    posk = np.ascontiguousarray(np.tile(np.asarray(inp["cmp_k_pos"], f).T, (2, 1)))
    posv = np.ascontiguousarray(np.tile(np.asarray(inp["cmp_v_pos"], f).T, (2, 1)))
    return dict(
        G=G, NM=NM, NM4=NM4, ident=ident, bones=bones, ind=ind, MF=MF, ovx=ovx,
        a_w_in=a_w_in, a_w_out=np.asarray(inp["a_w_out"], f), b_w_in=b_w_in,
        b_w_out=np.asarray(inp["b_w_out"], f), kv_w=np.asarray(inp["kv_w"], f),
        ng=np.ascontiguousarray(ng), hg=np.ascontiguousarray(hg), sinkrep=sinkrep, b31rep=b31rep,
        posk=posk, posv=posv,
        ck_w1=np.asarray(inp["cmp_k_w1"], f), ck_w2=np.asarray(inp["cmp_k_w2"], f),
        cv_w1=np.asarray(inp["cmp_v_w1"], f), cv_w2=np.asarray(inp["cmp_v_w2"], f),
    )


def build(nseq=2, n_a=2, do_kv=True, n_b=2, dbg=False):
    nc = bass.Bass("TRN2", target_bir_lowering=False)

    def din(name, shape):
        return nc.dram_tensor(name, list(shape), F32, kind="ExternalInput").ap()

    x_d = din("x", (nseq, S, D))
    G_d = din("G", (128, 3, 16, 128))
    NM_d = din("NM", (128, 3, 128))
    NM4_d = din("NM4", (128, 128))
    ident_d = din("ident", (128, 128))
    bones_d = din("bones", (128, 128))
    ind_d = din("ind", (32, 2048))
    MF_d = din("MF", (128, 16, 32))
    ovx_d = din("ovx", (128, 33))
    awin_d = din("a_w_in", (2, D, 2560))
    awout_d = din("a_w_out", (2, D, D))
    bwin_d = din("b_w_in", (2, D, 4144))
    bwout_d = din("b_w_out", (2, D, D))
    kvw_d = din("kv_w", (D, 1536))
    ng_d = din("ng", (128, 5, 8))
    hg_d = din("hg", (128, 9))
    sink_d = din("sinkrep", (128, 2, 16))
    b31_d = din("b31rep", (128, 16))
    posk_d = din("posk", (128, 32))
    posv_d = din("posv", (128, 32))
    ckw1_d = din("ck_w1", (2048, 256))
    ckw2_d = din("ck_w2", (256, 64))
    cvw1_d = din("cv_w1", (2048, 256))
    cvw2_d = din("cv_w2", (256, 64))
    y_d = nc.dram_tensor("y", [nseq, S, D], F32, kind="ExternalOutput").ap()
    BM_d = nc.dram_tensor("BM", [16, 128, 6, 128], BF16, kind="Internal").ap()

    es = ExitStack()
    with es:
        def sb(name, shape, dt):
            return es.enter_context(nc.sbuf_tensor("s_" + name, list(shape), dt))

        def ps(name, shape, dt):
            return es.enter_context(nc.psum_tensor(name, list(shape), dt))

        xT = sb("xT", (128, 8, S), F32)
        KT0 = sb("KT0", (128, 2, S), BF16)
        KT1 = sb("KT1", (128, 2, S), BF16)
        V0 = sb("V0", (128, 16, 4, 65), BF16)
        V1 = sb("V1", (128, 16, 4, 65), BF16)
        KcT = sb("KcT", (128, 2, 128), BF16)
        Vc = sb("Vc", (128, 4, 65), BF16)
        identb = sb("identb", (128, 128), BF16)
        identf = sb("identf", (128, 128), F32)
        bonesb = sb("bonesb", (128, 128), BF16)
        onesb = sb("onesb", (128, 128), BF16)
        indb = sb("indb", (32, 2048), BF16)
        NM4b = sb("NM4b", (128, 128), BF16)
        MF = sb("MF", (128, 16, 32), F32)
        ovxb = sb("ovxb", (128, 33), BF16)
        ng = sb("ng", (128, 5, 8), F32)
        hg = sb("hg", (128, 9), F32)
        esink = sb("esink", (128, 2, 16), F32)
        b31 = sb("b31", (128, 16), F32)
        banks = [ps(f"bk{i}", (128, 512), F32) for i in range(7)]
        bankT = ps("bkT", (128, 1024), BF16)
        bS = [("bk", 0), ("bk", 1)]
        bO = [("bk", 2), ("bk", 3)]
        bJ = [("bk", 4), ("bk", 5)]
        bM = ("bk", 6)

        bankTf = bankT.bitcast(F32)

        def bk(key):
            if key == "bkT":
                return bankTf
            return banks[key[1]]

        P = Prog(nc, es)

        with ExitStack() as ph:
            def tsb(name, shape, dt):
                return ph.enter_context(nc.sbuf_tensor("su_" + name, list(shape), dt))
            Gs = tsb("Gs", (128, 16, 128), F32)
            NMs = tsb("NMs", (128, 3, 128), F32)
            Dh = tsb("Dh", (128, 16, 128), BF16)
            Dl = tsb("Dl", (128, 16, 128), BF16)
            sinks = tsb("sinks", (128, 2, 16), F32)
            P.op("sp", lambda e: e.dma_start(out=identf[:], in_=ident_d), writes=["identf"], dma="c0")
            P.op("sp", lambda e: e.dma_start(out=MF[:], in_=MF_d), writes=["MF"], dma="c1")
            P.op("sp", lambda e: e.dma_start(out=ng[:], in_=ng_d), writes=["ng"], dma="c2")
            P.op("sp", lambda e: e.dma_start(out=hg[:], in_=hg_d), writes=["hg"], dma="c3")
            P.op("sp", lambda e: e.dma_start(out=sinks[:], in_=sink_d), writes=["sinks"], dma="c4")
            P.op("sp", lambda e: e.dma_start(out=b31[:], in_=b31_d), writes=["b31"], dma="c5")
            P.op("sp", lambda e: e.dma_start(out=NMs[:], in_=NM_d), writes=["NMs"], dma="c6")
            P.op("pool", lambda e: e.dma_start(out=identb[:], in_=ident_d), writes=["identb"], dma="p0")
            P.op("pool", lambda e: e.dma_start(out=bonesb[:], in_=bones_d), writes=["bonesb"], dma="p1")
            P.op("pool", lambda e: e.dma_start(out=indb[:], in_=ind_d), writes=["indb"], dma="p2")
            P.op("pool", lambda e: e.dma_start(out=NM4b[:], in_=NM4_d), writes=["NM4b"], dma="p3")
            P.op("pool", lambda e: e.dma_start(out=ovxb[:], in_=ovx_d), writes=["ovxb"], dma="p4")
            P.op("dve", lambda e: e.memset(onesb[:], 1.0), writes=["onesb"])
            P.op("dve", lambda e: e.memset(V0[:, :, :, 64:65], 1.0), writes=["V0o"])
            P.op("dve", lambda e: e.memset(V1[:, :, :, 64:65], 1.0), writes=["V1o"])
            P.op("dve", lambda e: e.memset(Vc[:, :, 0:64], 0.0), writes=["Vc"])
            P.op("dve", lambda e: e.memset(Vc[:, :, 64:65], 1.0), writes=["Vco"])
            P.op("dve", lambda e: e.memset(KcT[:], 0.0), writes=["KcT"])
            P.op("act", lambda e: e.activation(out=esink[:], in_=sinks[:], func=AF.Exp), reads=["sinks"], writes=["esink"])
            for t in range(3):
                P.op("sp", lambda e, t=t: e.dma_start(out=Gs[:], in_=G_d[:, t]), writes=["Gs"], dma="g")
                P.op("dve", lambda e: e.tensor_tensor(out=Gs[:], in0=Gs[:], in1=b31[:].unsqueeze(2).to_broadcast([128, 16, 128]),
                                                      op=ALU.subtract), reads=["Gs", "b31"], writes=["Gs"])
                if t != 1:
                    P.op("dve", lambda e, t=t: e.tensor_tensor(out=Gs[:], in0=Gs[:],
                                                               in1=NMs[:, t, :].unsqueeze(1).to_broadcast([128, 16, 128]),
                                                               op=ALU.add), reads=["Gs", "NMs"], writes=["Gs"])
                P.op("dve", lambda e: e.tensor_copy(out=Dh[:], in_=Gs[:]), reads=["Gs"], writes=["Dh"])
                P.op("dve", lambda e: e.tensor_tensor(out=Dl[:], in0=Gs[:], in1=Dh[:], op=ALU.subtract),
                     reads=["Gs", "Dh"], writes=["Dl"])
                P.op("sp", lambda e, t=t: e.dma_start(out=BM_d[:, :, 2 * t, :].rearrange("h k q -> k h q"), in_=Dh[:]),
                     reads=["Dh"], writes=["BM"], dma="bmw")
                P.op("sp", lambda e, t=t: e.dma_start(out=BM_d[:, :, 2 * t + 1, :].rearrange("h k q -> k h q"), in_=Dl[:]),
                     reads=["Dl"], writes=["BM"], dma="bmw")
            P.op("sp", None, reads=["BM"])
            P.flush()

        def load_x(P, W, s):
            for t in range(16):
                slot = t % 2
                P.op("sp", lambda e, t=t, slot=slot: e.dma_start(out=W["xs"][:, slot, :], in_=x_d[s, t * 128:(t + 1) * 128, :]),
                     writes=[("xs", slot)], dma=("xs", slot))
                for half in range(2):
                    b = bJ[half]
                    for kk in range(4):
                        k = half * 4 + kk
                        P.op("pe", lambda e, b=b, kk=kk, k=k, slot=slot: e.transpose(
                            out=bk(b)[:, kk * 128:(kk + 1) * 128], in_=W["xs"][:, slot, k * 128:(k + 1) * 128], identity=identf[:]),
                            reads=[("xs", slot), "identf"], writes=[b])
                    eng = "dve" if half == 0 else "act"
                    if eng == "dve":
                        P.op("dve", lambda e, b=b, half=half, t=t: e.tensor_copy(
                            out=xT[:, half * 4:half * 4 + 4, t * 128:(t + 1) * 128],
                            in_=bk(b)[:, :].rearrange("p (k q) -> p k q", k=4)), reads=[b], writes=[("xT", t // 4)])
                    else:
                        P.op("act", lambda e, b=b, half=half, t=t: e.activation(
                            out=xT[:, half * 4:half * 4 + 4, t * 128:(t + 1) * 128],
                            in_=bk(b)[:, :].rearrange("p (k q) -> p k q", k=4), func=AF.Copy), reads=[b], writes=[("xT", t // 4)])

        def store_x(P, W, s):
            for t in range(16):
                slot = t % 2
                for half in range(2):
                    b = bJ[half]
                    for kk in range(4):
                        k = half * 4 + kk
                        P.op("pe", lambda e, b=b, kk=kk, k=k, t=t: e.transpose(
                            out=bk(b)[:, kk * 128:(kk + 1) * 128], in_=xT[:, k, t * 128:(t + 1) * 128], identity=identf[:]),
                            reads=[("xT", t // 4), "identf"], writes=[b])
                    if half == 0:
                        P.op("dve", lambda e, b=b, half=half, slot=slot: e.tensor_copy(
                            out=W["xs"][:, slot, half * 512:(half + 1) * 512], in_=bk(b)[:, :]), reads=[b], writes=[("xs", slot)])
                    else:
                        P.op("act", lambda e, b=b, half=half, slot=slot: e.activation(
                            out=W["xs"][:, slot, half * 512:(half + 1) * 512], in_=bk(b)[:, :], func=AF.Copy),
                            reads=[b], writes=[("xs", slot)])
                P.op("sp", lambda e, t=t, slot=slot: e.dma_start(out=y_d[s, t * 128:(t + 1) * 128, :], in_=W["xs"][:, slot, :]),
                     reads=[("xs", slot)], writes=[("y", t)], dma=("ys", slot))
            P.op("sp", None, reads=[("y", t) for t in range(16)])

        def norm_chunk(P, W, c, gi):
            cs = slice(c * 512, (c + 1) * 512)
            for k in range(8):
                sl = k % 2
                P.op("act", lambda e, k=k, sl=sl: e.activation(out=W["sq"][:, sl, :], in_=xT[:, k, cs], func=AF.Square),
                     reads=[("xT", c)], writes=[("sq", sl)])
                P.op("pe", lambda e, k=k, sl=sl: e.matmul(bk(bM)[:, :], lhsT=onesb[:, :], rhs=W["sq"][:, sl, :],
                                                          start=(k == 0), stop=(k == 7)),
                     reads=[("sq", sl), "onesb"], writes=[bM])
            P.op("act", lambda e: e.activation(out=W["rs"][:, 0, :], in_=bk(bM)[:, :], func=AF.Ln, bias=EPS, scale=1.0 / D),
                 reads=[bM], writes=[("rs", 0)])
            P.op("act", lambda e: e.activation(out=W["rs"][:, 0, :], in_=W["rs"][:, 0, :], func=AF.Exp, scale=-0.5),
                 reads=[("rs", 0)], writes=[("rs", 0)])
            for k in range(8):
                P.op("dve", lambda e, k=k: e.scalar_tensor_tensor(out=W["hT"][:, k, :], in0=xT[:, k, cs], scalar=ng[:, gi, k:k + 1],
                                                                  in1=W["rs"][:, 0, :], op0=ALU.mult, op1=ALU.mult),
                     reads=[("xT", c), ("rs", 0), "ng"], writes=["hT"])

        wstate = {"n": 0}

        def _issue_w(P, W, i):
            src_ap, ncols = wstate["list"][i]
            slot = i % 2
            P.op("pool", lambda e, slot=slot, src_ap=src_ap, ncols=ncols: e.dma_start(
                out=W["w"][:, slot, :, 0:ncols], in_=src_ap.rearrange("(k p) n -> p k n", p=128)),
                writes=[("w", slot)], dma=("w", slot))

        def load_w(P, W, src_ap=None, ncols=None):
            i = wstate["n"]
            wstate["n"] += 1
            if NOPREFETCH:
                _issue_w(P, W, i)
                return i % 2
            if i == 0:
                _issue_w(P, W, 0)
            if i + 1 < len(wstate["list"]):
                _issue_w(P, W, i + 1)
            return i % 2

        def proj_fm(P, W, b, slot, co, ncols=128):
            for k in range(8):
                P.op("pe", lambda e, k=k: e.matmul(bk(b)[0:ncols, :], lhsT=W["w"][:, slot, k, co:co + ncols], rhs=W["hT"][:, k, :],
                                                   start=(k == 0), stop=(k == 7)),
                     reads=[("w", slot), "hT"], writes=[b])

        def headnorm(P, W, b, dest_fn, dest_key, gcol, scale):
            P.op("act", lambda e: e.activation(out=W["sqh"][:, :], in_=bk(b)[:, :], func=AF.Square), reads=[b], writes=["sqh"])
            P.op("pe", lambda e: e.matmul(bk(bM)[:, :], lhsT=bonesb[:, :], rhs=W["sqh"][:, :], start=True, stop=True),
                 reads=["sqh", "bonesb"], writes=[bM])
            P.op("act", lambda e: e.activation(out=W["rs"][:, 1, :], in_=bk(bM)[:, :], func=AF.Ln, bias=EPS, scale=1.0 / 64),
                 reads=[bM], writes=[("rs", 1)])
            P.op("act", lambda e: e.activation(out=W["rs"][:, 1, :], in_=W["rs"][:, 1, :], func=AF.Exp, scale=-0.5,
                                               bias=math.log(scale)), reads=[("rs", 1)], writes=[("rs", 1)])
            P.op("dve", lambda e: e.scalar_tensor_tensor(out=dest_fn(), in0=bk(b)[:, :], scalar=hg[:, gcol:gcol + 1],
                                                         in1=W["rs"][:, 1, :], op0=ALU.mult, op1=ALU.mult),
                 reads=[b, ("rs", 1), "hg"], writes=[dest_key])

        def proj_norm_seq(P, W, slot, items):
            n = len(items)
            proj_fm(P, W, bJ[0], slot, items[0][0])
            for i in range(n):
                if i + 1 < n:
                    proj_fm(P, W, bJ[(i + 1) % 2], slot, items[i + 1][0])
                co, dfn, dkey, gcol, sc = items[i]
                headnorm(P, W, bJ[i % 2], dfn, dkey, gcol, sc)

        def v_tm(P, W, slot, co, Vt, Vkey, c):
            for r in range(4):
                b = bJ[r % 2]
                for k in range(8):
                    P.op("pe", lambda e, k=k, r=r, b=b: e.matmul(bk(b)[:, 0:256], lhsT=W["hT"][:, k, r * 128:(r + 1) * 128],
                                                                 rhs=W["w"][:, slot, k, co:co + 256], start=(k == 0), stop=(k == 7)),
                         reads=[("w", slot), "hT"], writes=[b])
                P.op("act", lambda e, r=r, b=b: e.activation(out=Vt[:, 4 * c + r, :, 0:64],
                                                             in_=bk(b)[:, 0:256].rearrange("p (g d) -> p g d", g=4), func=AF.Copy),
                     reads=[b], writes=[(Vkey, c)])

        bmstate = {"n": 0}

        def load_bm(P, W, j):
            slot = bmstate["n"] % 2
            bmstate["n"] += 1
            for m in range(2):
                h = HP[2 * j + m]
                P.op("sp", lambda e, m=m, h=h, slot=slot: e.dma_start(out=W["bm"][:, slot, m, :, :], in_=BM_d[h]),
                     reads=["BM"], writes=[("bm", slot)], dma=("bm", slot))
            return slot

        ctr = {"S": 0, "O": 0, "PT": 0}

        bS3 = [("bk", 0), ("bk", 1), ("bk", 6)]
        bS4 = [("bk", 0), ("bk", 1), ("bk", 6), "bkT"]

        class Tile:
            __slots__ = ("A", "B", "C", "pre", "post")

            def __init__(self):
                self.pre = None
                self.post = None

        def run_pipeline(P, tiles, sb=None):
            sb = sb or bS4
            L = len(sb) - 1
            n = len(tiles)
            sks, pts = {}, {}
            for i in range(n + L):
                if i < n:
                    t = tiles[i]
                    if t.pre is not None:
                        t.pre()
                    sks[i] = sb[ctr["S"] % len(sb)]
                    ctr["S"] += 1
                    pts[i] = ctr["PT"] % 4
                    ctr["PT"] += 1
                    t.A(sks[i])
                    t.B(sks[i], pts[i])
                j = i - L
                if j >= 0:
                    t = tiles[j]
                    t.C(pts[j])
                    if t.post is not None:
                        t.post()

        def attn_tiles(P, W, h, pb, qsrc, qkey, Kt, Kkey, Vt, Vkey, kb, g, c, kts, rng_fn, bias_fn, bmslot, m, Okey,
                       extra_fn=None):
            tiles = []
            for ti_, kt in enumerate(kts):
                r0, r1 = rng_fn(kt)
                cols = slice(r0 * 128, (r1 + 1) * 128)
                adds = []
                for r in range(r0, r1 + 1):
                    for nm in bias_fn(4 * c + r - kt):
                        adds.append((r, nm))
                if extra_fn is not None:
                    adds = adds + [("x", None)]
                first = (ti_ == 0)
                T = Tile()

                def A(sk, kt=kt, cols=cols, adds=adds):
                    P.op("pe", lambda e, last=(len(adds) == 0): e.matmul(
                        bk(sk)[:, cols], lhsT=Kt[pb:pb + 64, kb, kt * 128:(kt + 1) * 128], rhs=qsrc[pb:pb + 64, cols],
                        start=True, stop=last, skip_group_check=True),
                        reads=[(Kkey, kt // 4), qkey], writes=[sk])
                    for i, (r, nm) in enumerate(adds):
                        last = (i == len(adds) - 1)
                        if r == "x":
                            extra_fn(P, sk, cols, kt, last)
                            continue
                        if nm == "NM4":
                            P.op("pe", lambda e, r=r, last=last: e.matmul(
                                bk(sk)[:, r * 128:(r + 1) * 128], lhsT=identb[:, :], rhs=NM4b[:, :], start=False, stop=last,
                                skip_group_check=True), reads=["identb", "NM4b"], writes=[sk])
                        else:
                            ti = {"D0": 0, "D1": 1}[nm]
                            for hl in range(2):
                                P.op("pe", lambda e, r=r, ti=ti, hl=hl, last=last: e.matmul(
                                    bk(sk)[:, r * 128:(r + 1) * 128], lhsT=identb[:, :], rhs=W["bm"][:, bmslot, m, 2 * ti + hl, :],
                                    start=False, stop=(last and hl == 1), skip_group_check=True),
                                    reads=["identb", ("bm", bmslot)], writes=[sk])

                def B(sk, pt, cols=cols):
                    P.op("act", lambda e: e.activation(out=W["PT"][:, pt, cols], in_=bk(sk)[:, cols], func=AF.Exp,
                                                       bias=b31[:, h:h + 1]),
                         reads=[sk, "b31"], writes=[("PT", pt)])

                def C(pt, kt=kt, r0=r0, r1=r1, first=first):
                    for r in range(r0, r1 + 1):
                        P.op("pe", lambda e, r=r, st=(first and r == r0): e.matmul(
                            bk(Okey)[:, r * 65:(r + 1) * 65], lhsT=W["PT"][:, pt, r * 128:(r + 1) * 128], rhs=Vt[:, kt, g, :],
                            start=st, stop=False, skip_group_check=True),
                            reads=[("PT", pt), (Vkey, kt // 4), Vkey + "o"], writes=[Okey])
                T.A, T.B, T.C = A, B, C
                tiles.append(T)
            return tiles

        def transpose_out(P, W, c, wout_d, l):
            for r in range(4):
                for k in range(8):
                    P.op("pe", lambda e, r=r, k=k: e.transpose(out=bankT[:, k * 128:(k + 1) * 128],
                                                               in_=W["acc"][:, r, k * 128:(k + 1) * 128], identity=identb[:]),
                         reads=["acc", "identb"], writes=["bkT"])
                P.op("dve", lambda e, r=r: e.tensor_copy(out=W["oT"][:, :, r * 128:(r + 1) * 128],
                                                         in_=bankT[:, :].rearrange("p (k q) -> p k q", k=8)),
                     reads=["bkT"], writes=["oT"])
            cs = slice(c * 512, (c + 1) * 512)
            for half in range(2):
                slot = load_w(P, W, wout_d[l][:, half * 512:(half + 1) * 512], 512)
                for nn in range(4):
                    n = half * 4 + nn
                    b = bJ[n % 2]
                    for k in range(8):
                        P.op("pe", lambda e, k=k, nn=nn, b=b, slot=slot: e.matmul(
                            bk(b)[:, :], lhsT=W["w"][:, slot, k, nn * 128:(nn + 1) * 128], rhs=W["oT"][:, k, :],
                            start=(k == 0), stop=(k == 7)), reads=[("w", slot), "oT"], writes=[b])
                    P.op("dve", lambda e, n=n, b=b: e.tensor_tensor(out=xT[:, n, cs], in0=bk(b)[:, :], in1=xT[:, n, cs], op=ALU.add),
                         reads=[b, ("xT", c)], writes=[("xT", c)])

        for s in range(nseq):
            for l in range(n_a):
                with ExitStack() as ph:
                    def tsb(name, shape, dt):
                        return ph.enter_context(nc.sbuf_tensor(f"t_{name}_a{s}{l}", list(shape), dt))
                    wstate["n"] = 0
                    wl = []
                    for c_ in range(NCH):
                        wl.append((awin_d[l][:, 0:512], 512))
                        for H_ in range(2):
                            wl.append((awin_d[l][:, 512 + H_ * 1024:1024 + H_ * 1024], 512))
                            wl.append((awin_d[l][:, 1024 + H_ * 1024:1536 + H_ * 1024], 512))
                        wl.append((awout_d[l][:, 0:512], 512))
                        wl.append((awout_d[l][:, 512:1024], 512))
                    wstate["list"] = wl
                    W = dict(
                        xs=tsb("xs", (128, 2, 1024), F32), sq=tsb("sq", (128, 2, 512), BF16), rs=tsb("rs", (128, 2, 512), F32),
                        hT=tsb("hT", (128, 8, 512), BF16), w=tsb("w", (128, 2, 8, 512), BF16), sqh=tsb("sqh", (128, 512), BF16),
                        QT=tsb("QT", (128, 4, 512), BF16), zs=tsb("zs", (128, 4, 512), BF16), PT=tsb("PT", (128, 4, 512), BF16),
                        bm=tsb("bm", (128, 2, 2, 6, 128), BF16), acc=tsb("acc", (128, 4, 1024), BF16),
                        oT=tsb("oT", (128, 8, 512), BF16), den=tsb("den", (128, 2, 4), F32), tt=tsb("tt", (128, 2, 4, 64), F32),
                    )
                    if l == 0:
                        load_x(P, W, s)
                    for c in range(NCH):
                        cs = slice(c * 512, (c + 1) * 512)
                        norm_chunk(P, W, c, l)
                        slot = load_w(P, W, awin_d[l][:, 0:512], 512)
                        proj_norm_seq(P, W, slot, [(blk * 128, (lambda blk=blk, cs=cs: KT0[:, blk, cs]), ("KT0", c), 2 + l, 1.0)
                                                   for blk in range(2)])
                        v_tm(P, W, slot, 256, V0, "V0", c)
                        for H in range(2):
                            slotq = load_w(P, W)
                            proj_norm_seq(P, W, slotq, [(jj * 128, (lambda jj=jj: W["QT"][:, jj, :]), ("QT", jj), l, 0.125)
                                                        for jj in range(4)])
                            slotz = load_w(P, W)
                            for r in range(4):
                                b = bJ[r % 2]
                                for k in range(8):
                                    P.op("pe", lambda e, k=k, r=r, b=b, slotz=slotz: e.matmul(
                                        bk(b)[:, :], lhsT=W["hT"][:, k, r * 128:(r + 1) * 128], rhs=W["w"][:, slotz, k, :],
                                        start=(k == 0), stop=(k == 7)), reads=[("w", slotz), "hT"], writes=[b])
                                P.op("act", lambda e, r=r, b=b: e.activation(out=W["zs"][:, r, :], in_=bk(b)[:, :], func=AF.Silu),
                                     reads=[b], writes=["zs"])
                            tiles = []
                            bmslots = {0: load_bm(P, W, 4 * H)}
                            for jj in range(4):
                                j = 4 * H + jj
                                ntiles0 = len(tiles)
                                for m in range(2):
                                    h = HP[2 * j + m]
                                    g = h // 4
                                    Okey = bO[ctr["O"] % 2]
                                    ctr["O"] += 1
                                    dsl = ctr["O"] % 2
                                    if jj not in bmslots:
                                        bmslots[jj] = (bmslots[jj - 1] + 1) % 2
                                    bmslot = bmslots[jj]
                                    tl = attn_tiles(P, W, h, 64 * m, W["QT"][:, jj, :], ("QT", jj), KT0, "KT0", V0, "V0", g // 2, g, c,
                                                    list(range(max(0, 4 * c - 1), 4 * c + 4)),
                                                    lambda kt, c=c: (max(0, kt - 4 * c), min(3, kt - 4 * c + 1)),
                                                    lambda dl: {0: ["D0"], 1: ["D1", "NM4"]}[dl], bmslot, m, Okey)

                                    def post(Okey=Okey, h=h, dsl=dsl, H=H, ll=l):
                                        Ov = bk(Okey)[:, 0:260].rearrange("p (r e) -> p r e", r=4)
                                        P.op("dve", lambda e: e.tensor_scalar(
                                            out=W["den"][:, dsl, :], in0=Ov[:, :, 64], scalar1=esink[:, ll, h:h + 1], scalar2=None, op0=ALU.add),
                                            reads=[Okey, "esink"], writes=[("den", dsl)])
                                        P.op("dve", lambda e: e.reciprocal(out=W["den"][:, dsl, :], in_=W["den"][:, dsl, :]),
                                             reads=[("den", dsl)], writes=[("den", dsl)])
                                        P.op("dve", lambda e: e.tensor_tensor(
                                            out=W["tt"][:, dsl, :, :], in0=Ov[:, :, 0:64],
                                            in1=W["den"][:, dsl, :].unsqueeze(2).to_broadcast([128, 4, 64]), op=ALU.mult),
                                            reads=[Okey, ("den", dsl)], writes=[("tt", dsl)])
                                        hh = h - 8 * H
                                        P.op("pool", lambda e: e.tensor_tensor(
                                            out=W["acc"][:, :, h * 64:(h + 1) * 64], in0=W["tt"][:, dsl, :, :],
                                            in1=W["zs"][:, :, hh * 64:(hh + 1) * 64], op=ALU.mult),
                                            reads=[("tt", dsl), "zs"], writes=["acc"])
                                    tl[-1].post = post
                                    tiles.extend(tl)
                                if jj + 1 < 4:
                                    def pre(jn=jj + 1, H=H):
                                        load_bm(P, W, 4 * H + jn)
                                    tiles[ntiles0].pre = pre
                            run_pipeline(P, tiles)
                        transpose_out(P, W, c, awout_d, l)
                    if dbg and l == n_a - 1 and not do_kv:
                        store_x(P, W, s)
                    P.flush()
            if not do_kv:
                continue
            with ExitStack() as ph:
                def tsb(name, shape, dt):
                    return ph.enter_context(nc.sbuf_tensor(f"t_{name}_kv{s}", list(shape), dt))
                wstate["n"] = 0
                wstate["list"] = [(kvw_d[:, i * 512:(i + 1) * 512], 512) for _ in range(NCH) for i in range(3)]
                W = dict(
                    sq=tsb("sq", (128, 2, 512), BF16), rs=tsb("rs", (128, 2, 512), F32),
                    hT=tsb("hT", (128, 8, 512), BF16), w=tsb("w", (128, 2, 8, 512), BF16), sqh=tsb("sqh", (128, 512), BF16),
                    KC=[tsb("KC0", (128, 2, S), BF16), tsb("VC0", (128, 2, S), BF16)],
                    w1X=tsb("w1X", (128, 32, 256), BF16), w2d=tsb("w2d", (128, 2, 128), BF16),
                    H1s=tsb("H1s", (128, 2, 128), BF16), posb=tsb("posb", (128, 2), F32), posT=tsb("posT", (128, 32), BF16),
                )
                for c in range(NCH):
                    cs = slice(c * 512, (c + 1) * 512)
                    norm_chunk(P, W, c, 2)
                    slot = load_w(P, W)
                    for kind in range(2):
                        for blk in range(2):
                            b = bJ[blk]
                            proj_fm(P, W, b, slot, kind * 256 + blk * 128)
                            P.op("act", lambda e, b=b, kind=kind, blk=blk, cs=cs: e.activation(
                                out=W["KC"][kind][:, blk, cs], in_=bk(b)[:, :], func=AF.Copy), reads=[b], writes=[("KC", kind)])
                    for br, (KTt, Kkey, Vt, Vkey, gcol) in enumerate([(KT0, "KT0", V0, "V0", 5), (KT1, "KT1", V1, "V1", 6)]):
                        slot = load_w(P, W)
                        proj_norm_seq(P, W, slot, [(blk * 128, (lambda blk=blk, cs=cs, KTt=KTt: KTt[:, blk, cs]), (Kkey, c), gcol, 1.0)
                                                   for blk in range(2)])
                        v_tm(P, W, slot, 256, Vt, Vkey, c)
                for kind, (w1_d, w2_d, pos_d) in enumerate([(ckw1_d, ckw2_d, posk_d), (cvw1_d, cvw2_d, posv_d)]):
                    SRC = W["KC"][kind]
                    for hf in range(2):
                        P.op("pool", lambda e, hf=hf, w1_d=w1_d: e.dma_start(
                            out=W["w1X"][hf * 64:(hf + 1) * 64, :, :], in_=w1_d.rearrange("(t d) n -> d t n", d=64)),
                            writes=["w1X"], dma="w1X")
                        P.op("pool", lambda e, hf=hf, w2_d=w2_d: e.dma_start(
                            out=W["w2d"][:, :, hf * 64:(hf + 1) * 64], in_=w2_d.rearrange("(hb p) n -> p hb n", p=128)),
                            writes=["w2d"], dma="w2d")
                    P.op("pool", lambda e, pos_d=pos_d: e.dma_start(out=W["posT"][:, :], in_=pos_d), writes=["posT"], dma="posT")
                    for hb in range(2):
                        for t in range(32):
                            P.op("pe", lambda e, hb=hb, t=t: e.matmul(
                                bk(bM)[:, hb:hb + 1], lhsT=W["w1X"][0:64, t, hb * 128:(hb + 1) * 128], rhs=W["posT"][0:64, t:t + 1],
                                start=(t == 0), stop=(t == 31)), reads=["w1X", "posT"], writes=[bM])
                    P.op("dve", lambda e: e.tensor_copy(out=W["posb"][:, :], in_=bk(bM)[:, 0:2]), reads=[bM], writes=["posb"])
                    for g in range(4):
                        pb = (g % 2) * 64
                        blk = g // 2
                        for hb in range(2):
                            b = bJ[hb]
                            for t in range(32):
                                P.op("pe", lambda e, hb=hb, t=t, b=b, pb=pb, blk=blk, SRC=SRC: e.matmul(
                                    bk(b)[:, 0:127], lhsT=W["w1X"][pb:pb + 64, t, hb * 128:(hb + 1) * 128],
                                    rhs=SRC[pb:pb + 64, blk, t:t + 2017:16], start=(t == 0), stop=(t == 31)),
                                    reads=["w1X", ("KC", kind)], writes=[b])
                            P.op("act", lambda e, hb=hb, b=b: e.activation(out=W["H1s"][:, hb, 0:127], in_=bk(b)[:, 0:127], func=AF.Silu,
                                                                           bias=W["posb"][:, hb:hb + 1]),
                                 reads=[b, "posb"], writes=["H1s"])
                        sk = bS[0]
                        if kind == 0:
                            for hb in range(2):
                                P.op("pe", lambda e, hb=hb, sk=sk: e.matmul(bk(sk)[:, 0:127], lhsT=W["w2d"][:, hb, :], rhs=W["H1s"][:, hb, 0:127],
                                                                          start=(hb == 0), stop=(hb == 1)),
                                     reads=["w2d", "H1s"], writes=[sk])
                            P.op("act", lambda e, sk=sk: e.activation(out=W["sqh"][:, 0:127], in_=bk(sk)[:, 0:127], func=AF.Square),
                                 reads=[sk], writes=["sqh"])
                            P.op("pe", lambda e: e.matmul(bk(bM)[:, 0:127], lhsT=bonesb[:, :], rhs=W["sqh"][:, 0:127], start=True, stop=True),
                                 reads=["sqh", "bonesb"], writes=[bM])
                            P.op("act", lambda e: e.activation(out=W["rs"][:, 1, 0:127], in_=bk(bM)[:, 0:127], func=AF.Ln, bias=EPS,
                                                               scale=1.0 / 64), reads=[bM], writes=[("rs", 1)])
                            P.op("act", lambda e: e.activation(out=W["rs"][:, 1, 0:127], in_=W["rs"][:, 1, 0:127], func=AF.Exp, scale=-0.5),
                                 reads=[("rs", 1)], writes=[("rs", 1)])
                            P.op("dve", lambda e, sk=sk, pb=pb, blk=blk: e.scalar_tensor_tensor(
                                out=KcT[pb:pb + 64, blk, 0:127], in0=bk(sk)[pb:pb + 64, 0:127], scalar=hg[pb:pb + 64, 4:5],
                                in1=W["rs"][pb:pb + 64, 1, 0:127], op0=ALU.mult, op1=ALU.mult),
                                reads=[sk, ("rs", 1), "hg"], writes=["KcT"])
                        else:
                            for hb in range(2):
                                P.op("pe", lambda e, hb=hb, sk=sk: e.matmul(bk(sk)[0:127, 0:64], lhsT=W["H1s"][:, hb, 0:127],
                                                                          rhs=W["w2d"][:, hb, 0:64], start=(hb == 0), stop=(hb == 1)),
                                     reads=["w2d", "H1s"], writes=[sk])
                            P.op("act", lambda e, sk=sk, g=g: e.activation(out=Vc[0:127, g, 0:64], in_=bk(sk)[0:127, 0:64], func=AF.Copy),
                                 reads=[sk], writes=["Vc"])
                P.flush()

            for l in range(n_b):
                with ExitStack() as ph:
                    def tsb(name, shape, dt):
                        return ph.enter_context(nc.sbuf_tensor(f"t_{name}_b{s}{l}", list(shape), dt))
                    wstate["n"] = 0
                    wl = []
                    for c_ in range(NCH):
                        for H_ in range(2):
                            wl.append((bwin_d[l][:, H_ * 2048:H_ * 2048 + 512], 512))
                            for jj_ in range(4):
                                o_ = H_ * 2048 + 512 + jj_ * 384
                                wl.append((bwin_d[l][:, o_:o_ + 384], 384))
                        wl.append((bwout_d[l][:, 0:512], 512))
                        wl.append((bwout_d[l][:, 512:1024], 512))
                    wstate["list"] = wl
                    W = dict(
                        xs=tsb("xs", (128, 2, 1024), F32), sq=tsb("sq", (128, 2, 512), BF16), rs=tsb("rs", (128, 2, 512), F32),
                        hT=tsb("hT", (128, 8, 512), BF16), w=tsb("w", (128, 2, 8, 512), BF16), sqh=tsb("sqh", (128, 512), BF16),
                        QT=tsb("QT", (128, 4, 512), BF16), zs=tsb("zs", (128, 2, 4, 384), BF16), PT=tsb("PT", (128, 4, 512), BF16),
                        bm=tsb("bm", (128, 2, 2, 6, 128), BF16), acc=tsb("acc", (128, 4, 1024), BF16),
                        oT=tsb("oT", (128, 8, 512), BF16), den=tsb("den", (128, 2, 4), F32), den2=tsb("den2", (128, 2, 4), F32),
                        tt=tsb("tt", (128, 2, 4, 64), F32), U=tsb("U", (128, 3, 4, 64), F32),
                        Wg=tsb("Wg", (128, 8, 48), BF16), sig=tsb("sig", (128, 4, 48), F32), ogc=tsb("ogc", (128, 8, 4, 64), F32),
                        impg=tsb("impg", (128, 2, 4, 32), F32), imt=tsb("imt", (128, 4, 32), F32), score=tsb("score", (128, 4, 32), F32),
                        top8=tsb("top8", (128, 8), F32), negsel=tsb("negsel", (128, 4, 32), BF16),
                        negselT=tsb("negselT", (32, 2, 512), BF16),
                    )
                    P.op("pool", lambda e, l=l: e.dma_start(out=W["Wg"][:, :, :],
                                                            in_=bwin_d[l][:, 4096:4144].rearrange("(k p) n -> p k n", p=128)),
                         writes=["Wg"], dma="Wg")
                    for c in range(NCH):
                        norm_chunk(P, W, c, 3 + l)
                        for r in range(4):
                            for k in range(8):
                                P.op("pe", lambda e, r=r, k=k: e.matmul(bk(bM)[:, r * 48:(r + 1) * 48], lhsT=W["hT"][:, k, r * 128:(r + 1) * 128],
                                                                        rhs=W["Wg"][:, k, :], start=(k == 0), stop=(k == 7), skip_group_check=True),
                                     reads=["hT", "Wg"], writes=[bM])
                        P.op("act", lambda e: e.activation(out=W["sig"][:, :, :], in_=bk(bM)[:, 0:192].rearrange("p (r n) -> p r n", r=4),
                                                           func=AF.Sigmoid), reads=[bM], writes=["sig"])
                        for H in range(2):
                            slotq = load_w(P, W)
                            proj_norm_seq(P, W, slotq, [(jj * 128, (lambda jj=jj: W["QT"][:, jj, :]), ("QT", jj), 7 + l, 0.125)
                                                        for jj in range(4)])
                            seen = set()
                            tiles = []
                            bmslots = {0: load_bm(P, W, 4 * H)}
                            for jj in range(4):
                                j = 4 * H + jj
                                ntiles0 = len(tiles)
                                if jj not in bmslots:
                                    bmslots[jj] = (bmslots[jj - 1] + 1) % 2
                                bmslot = bmslots[jj]
                                for m in range(2):
                                    h = HP[2 * j + m]
                                    g = h // 4
                                    gl = g - 2 * H
                                    hl = jj * 2 + m
                                    pb = 64 * m
                                    Okey = bO[ctr["O"] % 2]
                                    ctr["O"] += 1
                                    dsl = ctr["O"] % 2
                                    Ikey = bJ[hl % 2]
                                    for r in range(4):
                                        qt = 4 * c + r
                                        ncq = 8 * qt + 8
                                        off = 120 - 8 * qt
                                        T = Tile()

                                        def A(sk, ncq=ncq, off=off, pb=pb, g=g, jj=jj, r=r, bmslot=bmslot, m=m):
                                            P.op("pe", lambda e: e.matmul(
                                                bk(sk)[0:ncq, 0:128], lhsT=KcT[pb:pb + 64, g // 2, 0:ncq],
                                                rhs=W["QT"][pb:pb + 64, jj, r * 128:(r + 1) * 128],
                                                start=True, stop=False, skip_group_check=True), reads=["KcT", ("QT", jj)], writes=[sk])
                                            for hl2 in range(2):
                                                P.op("pe", lambda e, hl2=hl2: e.matmul(
                                                    bk(sk)[0:ncq, 0:128], lhsT=identb[:, off:off + ncq], rhs=W["bm"][:, bmslot, m, 4 + hl2, :],
                                                    start=False, stop=(hl2 == 1), skip_group_check=True),
                                                    reads=["identb", ("bm", bmslot)], writes=[sk])

                                        def B(sk, pt, ncq=ncq, h=h):
                                            P.op("act", lambda e: e.activation(
                                                out=W["PT"][0:ncq, pt, 0:128], in_=bk(sk)[0:ncq, 0:128], func=AF.Exp, bias=b31[0:ncq, h:h + 1]),
                                                reads=[sk, "b31"], writes=[("PT", pt)])

                                        def C(pt, ncq=ncq, Okey=Okey, Ikey=Ikey, g=g, r=r):
                                            P.op("pe", lambda e: e.matmul(
                                                bk(Okey)[:, r * 65:(r + 1) * 65], lhsT=W["PT"][0:ncq, pt, 0:128], rhs=Vc[0:ncq, g, :],
                                                start=(r == 0), stop=False, skip_group_check=True), reads=[("PT", pt), "Vc"], writes=[Okey])
                                            P.op("pe", lambda e: e.matmul(
                                                bk(Ikey)[:, r * 33:(r + 1) * 33], lhsT=W["PT"][0:ncq, pt, 0:128], rhs=ovxb[0:ncq, :],
                                                start=(r == 0), stop=False, skip_group_check=True), reads=[("PT", pt), "ovxb"], writes=[Ikey])
                                        T.A, T.B, T.C = A, B, C
                                        tiles.append(T)
                                    first_g = gl not in seen
                                    seen.add(gl)

                                    def post(Okey=Okey, Ikey=Ikey, dsl=dsl, first_g=first_g, gl=gl, h=h, hl=hl):
                                        Ov = bk(Okey)[:, 0:260].rearrange("p (r e) -> p r e", r=4)
                                        Iv = bk(Ikey)[:, 0:132].rearrange("p (r e) -> p r e", r=4)
                                        P.op("dve", lambda e: e.tensor_scalar(
                                            out=W["den"][:, dsl, :], in0=Ov[:, :, 64], scalar1=1e-30, scalar2=None, op0=ALU.max),
                                            reads=[Okey], writes=[("den", dsl)])
                                        P.op("dve", lambda e: e.reciprocal(out=W["den"][:, dsl, :], in_=W["den"][:, dsl, :]),
                                             reads=[("den", dsl)], writes=[("den", dsl)])
                                        dst = W["impg"][:, gl, :, :] if first_g else W["imt"][:, :, :]
                                        P.op("dve", lambda e: e.tensor_tensor(
                                            out=dst, in0=Iv[:, :, 0:32], in1=W["den"][:, dsl, :].unsqueeze(2).to_broadcast([128, 4, 32]), op=ALU.mult),
                                            reads=[Ikey, ("den", dsl)], writes=[("impg", gl) if first_g else "imt"])
                                        if not first_g:
                                            P.op("dve", lambda e: e.tensor_tensor(out=W["impg"][:, gl, :, :], in0=W["impg"][:, gl, :, :],
                                                                                  in1=W["imt"][:, :, :], op=ALU.add),
                                                 reads=["imt", ("impg", gl)], writes=[("impg", gl)])
                                        P.op("dve", lambda e: e.tensor_tensor(
                                            out=W["den2"][:, dsl, :], in0=W["den"][:, dsl, :], in1=W["sig"][:, :, h], op=ALU.mult),
                                            reads=[("den", dsl), "sig"], writes=[("den2", dsl)])
                                        P.op("dve", lambda e: e.tensor_tensor(
                                            out=W["ogc"][:, hl, :, :], in0=Ov[:, :, 0:64],
                                            in1=W["den2"][:, dsl, :].unsqueeze(2).to_broadcast([128, 4, 64]), op=ALU.mult),
                                            reads=[Okey, ("den2", dsl)], writes=[("ogc", hl)])
                                    tiles[-1].post = post
                                if jj + 1 < 4:
                                    def pre(jn=jj + 1, H=H):
                                        load_bm(P, W, 4 * H + jn)
                                    tiles[ntiles0].pre = pre
                            run_pipeline(P, tiles, bS3)
                            def emit_z(jj, H=H):
                                slotz = load_w(P, W)
                                zsl = jj % 2
                                for r in range(4):
                                    b = bJ[r % 2]
                                    for k in range(8):
                                        P.op("pe", lambda e, k=k, r=r, b=b, slotz=slotz: e.matmul(
                                            bk(b)[:, 0:384], lhsT=W["hT"][:, k, r * 128:(r + 1) * 128], rhs=W["w"][:, slotz, k, 0:384],
                                            start=(k == 0), stop=(k == 7)), reads=[("w", slotz), "hT"], writes=[b])
                                    P.op("act", lambda e, r=r, b=b, zsl=zsl: e.activation(out=W["zs"][:, zsl, r, :], in_=bk(b)[:, 0:384], func=AF.Silu),
                                         reads=[b], writes=[("zs", zsl)])
                            emit_z(0)
                            for gl in range(2):
                                P.op("dve", lambda e, gl=gl, c=c: e.tensor_tensor(out=W["score"][:, :, :], in0=W["impg"][:, gl, :, :],
                                                                                in1=MF[:, 4 * c:4 * c + 4, :], op=ALU.add),
                                     reads=[("impg", gl), "MF"], writes=["score"])
                                for r in range(4):
                                    P.op("dve", lambda e, r=r: e.max(out=W["top8"][:, :], in_=W["score"][:, r, :]), reads=["score"], writes=["top8"])
                                    P.op("dve", lambda e, r=r: e.tensor_scalar(out=W["negsel"][:, r, :], in0=W["score"][:, r, :],
                                                                               scalar1=W["top8"][:, 7:8], scalar2=NEGM, op0=ALU.is_lt, op1=ALU.mult),
                                         reads=["score", "top8"], writes=["negsel"])
                                for r in range(4):
                                    P.op("pe", lambda e, r=r: e.matmul(bk(bM)[0:32, r * 128:(r + 1) * 128], lhsT=W["negsel"][:, r, :], rhs=identb[:, :],
                                                                       start=True, stop=True, skip_group_check=True),
                                         reads=["negsel", "identb"], writes=[bM])
                                P.op("act", lambda e, gl=gl: e.activation(out=W["negselT"][0:32, gl, :], in_=bk(bM)[0:32, :], func=AF.Copy),
                                     reads=[bM], writes=[("negselT", gl)])
                            bmslots = {0: load_bm(P, W, 4 * H)}
                            alltiles = []
                            for jj in range(4):
                                j = 4 * H + jj
                                zsl = jj % 2
                                if jj not in bmslots:
                                    bmslots[jj] = (bmslots[jj - 1] + 1) % 2
                                bmslot = bmslots[jj]
                                tiles = []
                                for m in range(2):
                                    h = HP[2 * j + m]
                                    g = h // 4
                                    gl = g - 2 * H
                                    hl = jj * 2 + m
                                    for br in range(2):
                                        Okey = bO[ctr["O"] % 2]
                                        ctr["O"] += 1
                                        dsl = ctr["O"] % 2
                                        if br == 0:
                                            def extra(P_, sk, cols, kt, last, gl=gl):
                                                P_.op("pe", lambda e: e.matmul(
                                                    bk(sk)[:, cols], lhsT=indb[0:32, kt * 128:(kt + 1) * 128], rhs=W["negselT"][0:32, gl, cols],
                                                    start=False, stop=last, skip_group_check=True),
                                                    reads=["indb", ("negselT", gl)], writes=[sk])
                                            tl = attn_tiles(P, W, h, 64 * m, W["QT"][:, jj, :], ("QT", jj), KT0, "KT0", V0, "V0", g // 2, g, c,
                                                            list(range(0, 4 * c + 4)), lambda kt, c=c: (max(0, kt - 4 * c), 3),
                                                            lambda dl: {0: ["D0"], 1: ["D1"]}.get(dl, []), bmslot, m, Okey, extra_fn=extra)
                                        else:
                                            tl = attn_tiles(P, W, h, 64 * m, W["QT"][:, jj, :], ("QT", jj), KT1, "KT1", V1, "V1", g // 2, g, c,
                                                            list(range(max(0, 4 * c - 4), 4 * c + 4)),
                                                            lambda kt, c=c: (max(0, kt - 4 * c), min(3, kt - 4 * c + 4)),
                                                            lambda dl: {0: ["D0"], 1: ["D1"], 4: ["NM4"]}.get(dl, []), bmslot, m, Okey)

                                        def post(Okey=Okey, dsl=dsl, h=h, br=br, m=m, zsl=zsl, hl=hl):
                                            Ov = bk(Okey)[:, 0:260].rearrange("p (r e) -> p r e", r=4)
                                            P.op("dve", lambda e: e.reciprocal(out=W["den"][:, dsl, :], in_=Ov[:, :, 64]),
                                                 reads=[Okey], writes=[("den", dsl)])
                                            P.op("dve", lambda e: e.tensor_tensor(
                                                out=W["den2"][:, dsl, :], in0=W["den"][:, dsl, :], in1=W["sig"][:, :, 16 * (br + 1) + h], op=ALU.mult),
                                                reads=[("den", dsl), "sig"], writes=[("den2", dsl)])
                                            P.op("dve", lambda e: e.tensor_tensor(
                                                out=W["tt"][:, dsl, :, :], in0=Ov[:, :, 0:64],
                                                in1=W["den2"][:, dsl, :].unsqueeze(2).to_broadcast([128, 4, 64]), op=ALU.mult),
                                                reads=[Okey, ("den2", dsl)], writes=[("tt", dsl)])
                                            zo = m * 192 + (br + 1) * 64
                                            P.op("pool", lambda e: e.tensor_tensor(
                                                out=W["U"][:, br, :, :], in0=W["tt"][:, dsl, :, :], in1=W["zs"][:, zsl, :, zo:zo + 64], op=ALU.mult),
                                                reads=[("tt", dsl), ("zs", zsl)], writes=[("U", br)])
                                            if br == 1:
                                                zo0 = m * 192
                                                P.op("pool", lambda e: e.tensor_tensor(
                                                    out=W["U"][:, 2, :, :], in0=W["ogc"][:, hl, :, :], in1=W["zs"][:, zsl, :, zo0:zo0 + 64], op=ALU.mult),
                                                    reads=[("ogc", hl), ("zs", zsl)], writes=[("U", 2)])
                                                P.op("pool", lambda e: e.tensor_tensor(out=W["U"][:, 0, :, :], in0=W["U"][:, 0, :, :],
                                                                                       in1=W["U"][:, 1, :, :], op=ALU.add),
                                                     reads=[("U", 0), ("U", 1)], writes=[("U", 0)])
                                                P.op("pool", lambda e: e.tensor_tensor(out=W["acc"][:, :, h * 64:(h + 1) * 64], in0=W["U"][:, 0, :, :],
                                                                                       in1=W["U"][:, 2, :, :], op=ALU.add),
                                                     reads=[("U", 0), ("U", 2)], writes=["acc"])
                                        tl[-1].post = post
                                        tiles.extend(tl)
                                if jj + 1 < 4:
                                    def pre1(jn=jj + 1, H=H):
                                        load_bm(P, W, 4 * H + jn)
                                    tiles[0].pre = pre1

                                    def pre2(jn=jj + 1):
                                        emit_z(jn)
                                    tiles[len(tiles) // 2].pre = pre2
                                alltiles.extend(tiles)
                            run_pipeline(P, alltiles, bS4)
                        transpose_out(P, W, c, bwout_d, l)
                    if l == n_b - 1:
                        store_x(P, W, s)
                    P.flush()
    return nc


_CACHE = {}


def kernel(**inputs):
    sh = _prep_shared(inputs)
    x = np.asarray(inputs["x"], np.float32)
    nc = build()
    in_maps = []
    for i in range(8):
        m = dict(sh)
        m["x"] = np.ascontiguousarray(x[2 * i:2 * i + 2])
        in_maps.append(m)
    res = run_bass_kernel_spmd(nc, in_maps, core_ids=list(range(8)))
    return np.concatenate([r["y"] for r in res.results], axis=0).astype(np.float32)
```

```python
import math
from contextlib import ExitStack
import numpy as np
import concourse.bass as bass
import concourse.mybir as mybir
from concourse.bass_utils import run_bass_kernel_spmd

F32 = mybir.dt.float32
BF16 = mybir.dt.bfloat16
AF = mybir.ActivationFunctionType
ALU = mybir.AluOpType

S = 2048
D = 1024
NCH = 4
EPS = 1e-6
NEGM = -30000.0
NOPREFETCH = False
HP = [0, 4, 1, 5, 2, 6, 3, 7, 8, 12, 9, 13, 10, 14, 11, 15]


class _Op:
    __slots__ = ("eng", "fn", "deps", "signal", "tok", "isdma", "idx")


class Prog:
    ENGS = ("pe", "act", "dve", "pool", "sp")

    def __init__(self, nc, es):
        self.nc, self.es = nc, es
        self.sems = {("eng", e): es.enter_context(nc.semaphore(f"sem_e_{e}")) for e in self.ENGS}
        self.bar = es.enter_context(nc.semaphore("sem_bar"))
        self.eng_cnt = {e: 0 for e in self.ENGS}
        self.dma_cnt = {}
        self.nflush = 0
        self._reset()

    def _reset(self):
        self.ops = {e: [] for e in self.ENGS}
        self.last_w = {}
        self.readers = {}
        self.n = 0

    def op(self, eng, fn, reads=(), writes=(), dma=None):
        o = _Op()
        o.eng, o.fn, o.signal, o.isdma = eng, fn, False, dma is not None
        o.idx = self.n
        self.n += 1
        deps = []
        for b in reads:
            deps.extend(self.last_w.get(b, {}).values())
        for b in writes:
            deps.extend(self.last_w.get(b, {}).values())
            deps.extend(self.readers.get(b, {}).values())
        best = {}
        for d in deps:
            if d.isdma:
                k = ("dma", d.tok[0])
            else:
                if d.eng == eng and eng == "pe":
                    continue
                k = ("eng", d.eng)
            if k not in best or best[k].idx < d.idx:
                best[k] = d
        o.deps = list(best.values())
        for d in o.deps:
            d.signal = True
        wk = ("dma", dma) if o.isdma else ("eng", eng)
        for b in reads:
            self.readers.setdefault(b, {})[wk] = o
        for b in writes:
            self.last_w.setdefault(b, {})[wk] = o
            self.readers[b] = {}
        if o.isdma:
            if dma not in self.dma_cnt:
                self.dma_cnt[dma] = 0
                self.sems[dma] = self.es.enter_context(self.nc.semaphore(f"sem_d{len(self.dma_cnt)}"))
            self.dma_cnt[dma] += 16
            o.tok = (dma, self.dma_cnt[dma])
        self.ops[eng].append(o)
        return o

    def flush(self):
        nc, sems = self.nc, self.sems
        for e in self.ENGS:
            c = self.eng_cnt[e]
            for o in self.ops[e]:
                if not o.isdma:
                    if o.signal:
                        c += 1
                    o.tok = (("eng", e), c)
            self.eng_cnt[e] = c
        ops = self.ops
        self.nflush += 1
        bar_target = 5 * self.nflush
        bar = self.bar
        dma_tot = dict(self.dma_cnt)

        def run(engname):
            def body(eng):
                waited = {}
                mydma = set()
                for o in ops[engname]:
                    for d in o.deps:
                        key, val = d.tok
                        if waited.get(key, 0) < val:
                            eng.wait_ge(sems[key], val)
                            waited[key] = val
                    if o.fn is None:
                        continue
                    ins = o.fn(eng)
                    if o.isdma:
                        ins.then_inc(sems[o.tok[0]], 16)
                        mydma.add(o.tok[0])
                    elif o.signal:
                        ins.then_inc(sems[o.tok[0]], 1)
                eng.drain()
                for k in mydma:
                    eng.wait_ge(sems[k], dma_tot[k])
                eng.sem_inc(bar, 1)
                eng.wait_ge(bar, bar_target)
            return body

        with nc.Block() as blk:
            blk.tensor(run("pe"))
            blk.scalar(run("act"))
            blk.vector(run("dve"))
            blk.gpsimd(run("pool"))
            blk.sync(run("sp"))
        self._reset()


def _t5_bucket(d):
    d = np.maximum(d, 0)
    large = 16 + (np.log(np.maximum(d, 1).astype(np.float32) / np.float32(16)) / np.float32(math.log(8.0))
                  * np.float32(16)).astype(np.int32)
    large = np.minimum(large, 31)
    return np.where(d < 16, d, large)


def _prep_shared(inp):
    f = np.float32
    tab = np.asarray(inp["rel_table"], f)
    k = np.arange(128)[:, None]
    q = np.arange(128)[None, :]
    d0 = q - k
    d1 = 128 + q - k
    dc = q - 16 * k + 1889
    G = np.empty((128, 3, 16, 128), f)
    G[:, 0] = np.transpose(tab[_t5_bucket(d0)], (0, 2, 1))
    G[:, 1] = np.transpose(tab[_t5_bucket(d1)], (0, 2, 1))
    G[:, 2] = np.transpose(tab[_t5_bucket(dc)], (0, 2, 1))
    NM = np.zeros((128, 3, 128), f)
    NM[:, 0] = np.where(d0 >= 0, 0.0, NEGM)
    NM[:, 2] = np.where(dc >= 0, 0.0, NEGM)
    NM4 = np.where(q < k, 0.0, NEGM).astype(f)
    ident = np.eye(128, dtype=f)
    bones = np.kron(np.eye(2, dtype=f), np.ones((64, 64), f))
    ind = np.zeros((32, 2048), f)
    ind[np.arange(2048) // 64, np.arange(2048)] = 1.0
    pos = (np.arange(16)[None, :, None] * 128 + np.arange(128)[:, None, None])
    j = np.arange(32)[None, None, :]
    pb = pos // 64
    causal = j <= pb
    forced = (j == 0) | ((pb - j >= 0) & (pb - j < 2))
    MF = np.where(causal, np.where(forced, 1e6, 0.0), -1e30).astype(f)
    cs = np.arange(128)[:, None] * 16
    ss = np.arange(32)[None, :] * 64
    ov = ((cs < ss + 64) & (cs + 32 > ss)).astype(f)
    ov[127] = 0.0
    ovx = np.concatenate([ov, np.ones((128, 1), f)], axis=1)
    qcols = np.concatenate([np.arange(h * 64, (h + 1) * 64) for h in HP])
    acols = np.concatenate([np.arange(1024, 1536), qcols[:512], 1536 + np.arange(0, 512),
                            qcols[512:], 1536 + np.arange(512, 1024)])
    a_w_in = np.ascontiguousarray(np.asarray(inp["a_w_in"], f)[:, :, acols])
    bcols = []
    for H in range(2):
        bcols.append(qcols[H * 512:(H + 1) * 512])
        for jj in range(4):
            for m in range(2):
                h = HP[2 * (4 * H + jj) + m]
                for c in range(3):
                    bcols.append(1072 + c * 1024 + h * 64 + np.arange(64))
    bcols.append(1024 + np.arange(48))
    bcols = np.concatenate(bcols)
    b_w_in = np.ascontiguousarray(np.asarray(inp["b_w_in"], f)[:, :, bcols])

    def colmajor(v):
        return np.ascontiguousarray(np.asarray(v, f).reshape(8, 128).T)
    ng = np.stack([colmajor(inp["a_norm"][0]), colmajor(inp["a_norm"][1]), colmajor(inp["kv_norm"]),
                   colmajor(inp["b_norm"][0]), colmajor(inp["b_norm"][1])], axis=1)
    hgl = [inp["a_q_gain"][0], inp["a_q_gain"][1], inp["a_k_gain"][0], inp["a_k_gain"][1],
           inp["kv_k_gain"][0], inp["kv_k_gain"][1], inp["kv_k_gain"][2], inp["b_q_gain"][0], inp["b_q_gain"][1]]
    hg = np.stack([np.tile(np.asarray(v, f), 2) for v in hgl], axis=1)
    sinkrep = np.ascontiguousarray(np.broadcast_to(np.asarray(inp["a_sink"], f)[None], (128, 2, 16)))
    b31rep = np.ascontiguousarray(np.broadcast_to(tab[31][None], (128, 16)))
    posk = np.ascontiguousarray(np.tile(np.asarray(inp["cmp_k_pos"], f).T, (2, 1)))
    posv = np.ascontiguousarray(np.tile(np.asarray(inp["cmp_v_pos"], f).T, (2, 1)))
    return dict(
        G=G, NM=NM, NM4=NM4, ident=ident, bones=bones, ind=ind, MF=MF, ovx=ovx,
        a_w_in=a_w_in, a_w_out=np.asarray(inp["a_w_out"], f), b_w_in=b_w_in,
        b_w_out=np.asarray(inp["b_w_out"], f), kv_w=np.asarray(inp["kv_w"], f),
        ng=np.ascontiguousarray(ng), hg=np.ascontiguousarray(hg), sinkrep=sinkrep, b31rep=b31rep,
        posk=posk, posv=posv,
        ck_w1=np.asarray(inp["cmp_k_w1"], f), ck_w2=np.asarray(inp["cmp_k_w2"], f),
        cv_w1=np.asarray(inp["cmp_v_w1"], f), cv_w2=np.asarray(inp["cmp_v_w2"], f),
    )


def build(nseq=2, n_a=2, do_kv=True, n_b=2, dbg=False):
    nc = bass.Bass("TRN2", target_bir_lowering=False)

    def din(name, shape):
        return nc.dram_tensor(name, list(shape), F32, kind="ExternalInput").ap()

    x_d = din("x", (nseq, S, D))
    G_d = din("G", (128, 3, 16, 128))
    NM_d = din("NM", (128, 3, 128))
    NM4_d = din("NM4", (128, 128))
    ident_d = din("ident", (128, 128))
    bones_d = din("bones", (128, 128))
    ind_d = din("ind", (32, 2048))
    MF_d = din("MF", (128, 16, 32))
    ovx_d = din("ovx", (128, 33))
    awin_d = din("a_w_in", (2, D, 2560))
    awout_d = din("a_w_out", (2, D, D))
    bwin_d = din("b_w_in", (2, D, 4144))
    bwout_d = din("b_w_out", (2, D, D))
    kvw_d = din("kv_w", (D, 1536))
    ng_d = din("ng", (128, 5, 8))
    hg_d = din("hg", (128, 9))
    sink_d = din("sinkrep", (128, 2, 16))
    b31_d = din("b31rep", (128, 16))
    posk_d = din("posk", (128, 32))
    posv_d = din("posv", (128, 32))
    ckw1_d = din("ck_w1", (2048, 256))
    ckw2_d = din("ck_w2", (256, 64))
    cvw1_d = din("cv_w1", (2048, 256))
    cvw2_d = din("cv_w2", (256, 64))
    y_d = nc.dram_tensor("y", [nseq, S, D], F32, kind="ExternalOutput").ap()
    BM_d = nc.dram_tensor("BM", [16, 128, 6, 128], BF16, kind="Internal").ap()

    es = ExitStack()
    with es:
        def sb(name, shape, dt):
            return es.enter_context(nc.sbuf_tensor("s_" + name, list(shape), dt))

        def ps(name, shape, dt):
            return es.enter_context(nc.psum_tensor(name, list(shape), dt))

        xT = sb("xT", (128, 8, S), F32)
        KT0 = sb("KT0", (128, 2, S), BF16)
        KT1 = sb("KT1", (128, 2, S), BF16)
        V0 = sb("V0", (128, 16, 4, 65), BF16)
        V1 = sb("V1", (128, 16, 4, 65), BF16)
        KcT = sb("KcT", (128, 2, 128), BF16)
        Vc = sb("Vc", (128, 4, 65), BF16)
        identb = sb("identb", (128, 128), BF16)
        identf = sb("identf", (128, 128), F32)
        bonesb = sb("bonesb", (128, 128), BF16)
        onesb = sb("onesb", (128, 128), BF16)
        indb = sb("indb", (64, 2048), BF16)
        NM4b = sb("NM4b", (128, 128), BF16)
        MF = sb("MF", (128, 16, 32), F32)
        ovxb = sb("ovxb", (128, 33), BF16)
        ng = sb("ng", (128, 5, 8), F32)
        hg = sb("hg", (128, 9), F32)
        esink = sb("esink", (128, 2, 16), F32)
        b31 = sb("b31", (128, 16), F32)
        banks = [ps(f"bk{i}", (128, 512), F32) for i in range(7)]
        bankT = ps("bkT", (128, 1024), BF16)
        bS = [("bk", 0), ("bk", 1)]
        bO = [("bk", 2), ("bk", 3)]
        bJ = [("bk", 4), ("bk", 5)]
        bM = ("bk", 6)

        bankTf = bankT.bitcast(F32)

        def bk(key):
            if key == "bkT":
                return bankTf
            return banks[key[1]]

        P = Prog(nc, es)

        with ExitStack() as ph:
            def tsb(name, shape, dt):
                return ph.enter_context(nc.sbuf_tensor("su_" + name, list(shape), dt))
            Gs = tsb("Gs", (128, 16, 128), F32)
            NMs = tsb("NMs", (128, 3, 128), F32)
            Dh = tsb("Dh", (128, 16, 128), BF16)
            Dl = tsb("Dl", (128, 16, 128), BF16)
            sinks = tsb("sinks", (128, 2, 16), F32)
            P.op("sp", lambda e: e.dma_start(out=identf[:], in_=ident_d), writes=["identf"], dma="c0")
            P.op("sp", lambda e: e.dma_start(out=MF[:], in_=MF_d), writes=["MF"], dma="c1")
            P.op("sp", lambda e: e.dma_start(out=ng[:], in_=ng_d), writes=["ng"], dma="c2")
            P.op("sp", lambda e: e.dma_start(out=hg[:], in_=hg_d), writes=["hg"], dma="c3")
            P.op("sp", lambda e: e.dma_start(out=sinks[:], in_=sink_d), writes=["sinks"], dma="c4")
            P.op("sp", lambda e: e.dma_start(out=b31[:], in_=b31_d), writes=["b31"], dma="c5")
            P.op("sp", lambda e: e.dma_start(out=NMs[:], in_=NM_d), writes=["NMs"], dma="c6")
            P.op("pool", lambda e: e.dma_start(out=identb[:], in_=ident_d), writes=["identb"], dma="p0")
            P.op("pool", lambda e: e.dma_start(out=bonesb[:], in_=bones_d), writes=["bonesb"], dma="p1")
            P.op("pool", lambda e: e.dma_start(out=indb[0:32, :], in_=ind_d), writes=["indb"], dma="p2")
            P.op("pool", lambda e: e.dma_start(out=indb[32:64, :], in_=ind_d), writes=["indb"], dma="p2")
            P.op("pool", lambda e: e.dma_start(out=NM4b[:], in_=NM4_d), writes=["NM4b"], dma="p3")
            P.op("pool", lambda e: e.dma_start(out=ovxb[:], in_=ovx_d), writes=["ovxb"], dma="p4")
            P.op("dve", lambda e: e.memset(onesb[:], 1.0), writes=["onesb"])
            P.op("dve", lambda e: e.memset(V0[:, :, :, 64:65], 1.0), writes=["V0o"])
            P.op("dve", lambda e: e.memset(V1[:, :, :, 64:65], 1.0), writes=["V1o"])
            P.op("dve", lambda e: e.memset(Vc[:, :, 0:64], 0.0), writes=["Vc"])
            P.op("dve", lambda e: e.memset(Vc[:, :, 64:65], 1.0), writes=["Vco"])
            P.op("dve", lambda e: e.memset(KcT[:], 0.0), writes=["KcT"])
            P.op("act", lambda e: e.activation(out=esink[:], in_=sinks[:], func=AF.Exp), reads=["sinks"], writes=["esink"])
            for t in range(3):
                P.op("sp", lambda e, t=t: e.dma_start(out=Gs[:], in_=G_d[:, t]), writes=["Gs"], dma="g")
                P.op("dve", lambda e: e.tensor_tensor(out=Gs[:], in0=Gs[:], in1=b31[:].unsqueeze(2).to_broadcast([128, 16, 128]),
                                                      op=ALU.subtract), reads=["Gs", "b31"], writes=["Gs"])
                if t != 1:
                    P.op("dve", lambda e, t=t: e.tensor_tensor(out=Gs[:], in0=Gs[:],
                                                               in1=NMs[:, t, :].unsqueeze(1).to_broadcast([128, 16, 128]),
                                                               op=ALU.add), reads=["Gs", "NMs"], writes=["Gs"])
                P.op("dve", lambda e: e.tensor_copy(out=Dh[:], in_=Gs[:]), reads=["Gs"], writes=["Dh"])
                P.op("dve", lambda e: e.tensor_tensor(out=Dl[:], in0=Gs[:], in1=Dh[:], op=ALU.subtract),
                     reads=["Gs", "Dh"], writes=["Dl"])
                P.op("sp", lambda e, t=t: e.dma_start(out=BM_d[:, :, 2 * t, :].rearrange("h k q -> k h q"), in_=Dh[:]),
                     reads=["Dh"], writes=["BM"], dma="bmw")
                P.op("sp", lambda e, t=t: e.dma_start(out=BM_d[:, :, 2 * t + 1, :].rearrange("h k q -> k h q"), in_=Dl[:]),
                     reads=["Dl"], writes=["BM"], dma="bmw")
            P.op("sp", None, reads=["BM"])
            P.flush()

        def load_x(P, W, s):
            for t in range(16):
                slot = t % 2
                P.op("sp", lambda e, t=t, slot=slot: e.dma_start(out=W["xs"][:, slot, :], in_=x_d[s, t * 128:(t + 1) * 128, :]),
                     writes=[("xs", slot)], dma=("xs", slot))
                for half in range(2):
                    b = bJ[half]
                    for kk in range(4):
                        k = half * 4 + kk
                        P.op("pe", lambda e, b=b, kk=kk, k=k, slot=slot: e.transpose(
                            out=bk(b)[:, kk * 128:(kk + 1) * 128], in_=W["xs"][:, slot, k * 128:(k + 1) * 128], identity=identf[:]),
                            reads=[("xs", slot), "identf"], writes=[b])
                    eng = "dve" if half == 0 else "act"
                    if eng == "dve":
                        P.op("dve", lambda e, b=b, half=half, t=t: e.tensor_copy(
                            out=xT[:, half * 4:half * 4 + 4, t * 128:(t + 1) * 128],
                            in_=bk(b)[:, :].rearrange("p (k q) -> p k q", k=4)), reads=[b], writes=[("xT", t // 4)])
                    else:
                        P.op("act", lambda e, b=b, half=half, t=t: e.activation(
                            out=xT[:, half * 4:half * 4 + 4, t * 128:(t + 1) * 128],
                            in_=bk(b)[:, :].rearrange("p (k q) -> p k q", k=4), func=AF.Copy), reads=[b], writes=[("xT", t // 4)])

        def store_x(P, W, s):
            for t in range(16):
                slot = t % 2
                for half in range(2):
                    b = bJ[half]
                    for kk in range(4):
                        k = half * 4 + kk
                        P.op("pe", lambda e, b=b, kk=kk, k=k, t=t: e.transpose(
                            out=bk(b)[:, kk * 128:(kk + 1) * 128], in_=xT[:, k, t * 128:(t + 1) * 128], identity=identf[:]),
                            reads=[("xT", t // 4), "identf"], writes=[b])
                    if half == 0:
                        P.op("dve", lambda e, b=b, half=half, slot=slot: e.tensor_copy(
                            out=W["xs"][:, slot, half * 512:(half + 1) * 512], in_=bk(b)[:, :]), reads=[b], writes=[("xs", slot)])
                    else:
                        P.op("act", lambda e, b=b, half=half, slot=slot: e.activation(
                            out=W["xs"][:, slot, half * 512:(half + 1) * 512], in_=bk(b)[:, :], func=AF.Copy),
                            reads=[b], writes=[("xs", slot)])
                P.op("sp", lambda e, t=t, slot=slot: e.dma_start(out=y_d[s, t * 128:(t + 1) * 128, :], in_=W["xs"][:, slot, :]),
                     reads=[("xs", slot)], writes=[("y", t)], dma=("ys", slot))
            P.op("sp", None, reads=[("y", t) for t in range(16)])

        def norm_chunk(P, W, c, gi):
            cs = slice(c * 512, (c + 1) * 512)
            for k in range(8):
                sl = k % 2
                P.op("act", lambda e, k=k, sl=sl: e.activation(out=W["sq"][:, sl, :], in_=xT[:, k, cs], func=AF.Square),
                     reads=[("xT", c)], writes=[("sq", sl)])
                P.op("pe", lambda e, k=k, sl=sl: e.matmul(bk(bM)[:, :], lhsT=onesb[:, :], rhs=W["sq"][:, sl, :],
                                                          start=(k == 0), stop=(k == 7)),
                     reads=[("sq", sl), "onesb"], writes=[bM])
            P.op("act", lambda e: e.activation(out=W["rs"][:, 0, :], in_=bk(bM)[:, :], func=AF.Ln, bias=EPS, scale=1.0 / D),
                 reads=[bM], writes=[("rs", 0)])
            P.op("act", lambda e: e.activation(out=W["rs"][:, 0, :], in_=W["rs"][:, 0, :], func=AF.Exp, scale=-0.5),
                 reads=[("rs", 0)], writes=[("rs", 0)])
            for k in range(8):
                P.op("dve", lambda e, k=k: e.scalar_tensor_tensor(out=W["hT"][:, k, :], in0=xT[:, k, cs], scalar=ng[:, gi, k:k + 1],
                                                                  in1=W["rs"][:, 0, :], op0=ALU.mult, op1=ALU.mult),
                     reads=[("xT", c), ("rs", 0), "ng"], writes=["hT"])

        wstate = {"n": 0}

        def _issue_w(P, W, i):
            src_ap, ncols = wstate["list"][i]
            slot = i % 2
            P.op("pool", lambda e, slot=slot, src_ap=src_ap, ncols=ncols: e.dma_start(
                out=W["w"][:, slot, :, 0:ncols], in_=src_ap.rearrange("(k p) n -> p k n", p=128)),
                writes=[("w", slot)], dma=("w", slot))

        def load_w(P, W, src_ap=None, ncols=None):
            i = wstate["n"]
            wstate["n"] += 1
            if NOPREFETCH:
                _issue_w(P, W, i)
                return i % 2
            if i == 0:
                _issue_w(P, W, 0)
            if i + 1 < len(wstate["list"]):
                _issue_w(P, W, i + 1)
            return i % 2

        def proj_fm(P, W, b, slot, co, ncols=128):
            for k in range(8):
                P.op("pe", lambda e, k=k: e.matmul(bk(b)[0:ncols, :], lhsT=W["w"][:, slot, k, co:co + ncols], rhs=W["hT"][:, k, :],
                                                   start=(k == 0), stop=(k == 7)),
                     reads=[("w", slot), "hT"], writes=[b])

        def headnorm(P, W, b, dest_fn, dest_key, gcol, scale):
            P.op("act", lambda e: e.activation(out=W["sqh"][:, :], in_=bk(b)[:, :], func=AF.Square), reads=[b], writes=["sqh"])
            P.op("pe", lambda e: e.matmul(bk(bM)[:, :], lhsT=bonesb[:, :], rhs=W["sqh"][:, :], start=True, stop=True),
                 reads=["sqh", "bonesb"], writes=[bM])
            P.op("act", lambda e: e.activation(out=W["rs"][:, 1, :], in_=bk(bM)[:, :], func=AF.Ln, bias=EPS, scale=1.0 / 64),
                 reads=[bM], writes=[("rs", 1)])
            P.op("act", lambda e: e.activation(out=W["rs"][:, 1, :], in_=W["rs"][:, 1, :], func=AF.Exp, scale=-0.5,
                                               bias=math.log(scale)), reads=[("rs", 1)], writes=[("rs", 1)])
            P.op("dve", lambda e: e.scalar_tensor_tensor(out=dest_fn(), in0=bk(b)[:, :], scalar=hg[:, gcol:gcol + 1],
                                                         in1=W["rs"][:, 1, :], op0=ALU.mult, op1=ALU.mult),
                 reads=[b, ("rs", 1), "hg"], writes=[dest_key])

        def proj_norm_seq(P, W, slot, items):
            n = len(items)
            proj_fm(P, W, bJ[0], slot, items[0][0])
            for i in range(n):
                if i + 1 < n:
                    proj_fm(P, W, bJ[(i + 1) % 2], slot, items[i + 1][0])
                co, dfn, dkey, gcol, sc = items[i]
                headnorm(P, W, bJ[i % 2], dfn, dkey, gcol, sc)

        def v_tm(P, W, slot, co, Vt, Vkey, c):
            for r in range(4):
                b = bJ[r % 2]
                for k in range(8):
                    P.op("pe", lambda e, k=k, r=r, b=b: e.matmul(bk(b)[:, 0:256], lhsT=W["hT"][:, k, r * 128:(r + 1) * 128],
                                                                 rhs=W["w"][:, slot, k, co:co + 256], start=(k == 0), stop=(k == 7)),
                         reads=[("w", slot), "hT"], writes=[b])
                P.op("act", lambda e, r=r, b=b: e.activation(out=Vt[:, 4 * c + r, :, 0:64],
                                                             in_=bk(b)[:, 0:256].rearrange("p (g d) -> p g d", g=4), func=AF.Copy),
                     reads=[b], writes=[(Vkey, c)])

        bmstate = {"n": 0}

        def load_bm(P, W, j):
            slot = bmstate["n"] % 2
            bmstate["n"] += 1
            for m in range(2):
                h = HP[2 * j + m]
                P.op("sp", lambda e, m=m, h=h, slot=slot: e.dma_start(out=W["bm"][:, slot, m, :, :], in_=BM_d[h]),
                     reads=["BM"], writes=[("bm", slot)], dma=("bm", slot))
            return slot

        ctr = {"S": 0, "O": 0, "PT": 0}

        bS3 = [("bk", 0), ("bk", 1), ("bk", 6)]
        bS4 = [("bk", 0), ("bk", 1), ("bk", 6), "bkT"]

        class Tile:
            __slots__ = ("A0", "A1", "B", "C", "pre", "post", "mate")

            def __init__(self):
                self.pre = None
                self.post = None
                self.mate = False

        def interleave(l1, l2):
            out = []
            for i in range(max(len(l1), len(l2))):
                if i < len(l1):
                    out.append(l1[i])
                    l1[i].mate = i < len(l2)
                if i < len(l2):
                    out.append(l2[i])
            return out

        def run_pipeline(P, tiles, sb=None):
            sb = bS4
            groups = []
            i = 0
            while i < len(tiles):
                if tiles[i].mate and i + 1 < len(tiles):
                    groups.append([tiles[i], tiles[i + 1]])
                    i += 2
                else:
                    groups.append([tiles[i]])
                    i += 1
            prev = None
            for grp in groups + [None]:
                cur = None
                if grp is not None:
                    base = (ctr["S"] % 2) * 2
                    ctr["S"] += 1
                    cur = []
                    for ti, t in enumerate(grp):
                        if t.pre is not None:
                            t.pre()
                        cur.append((t, sb[base + ti], base + ti))
                    for t, sk, pt in cur:
                        t.A0(sk)
                    for t, sk, pt in cur:
                        t.A1(sk)
                    for t, sk, pt in cur:
                        t.B(sk, pt)
                if prev is not None:
                    for t, sk, pt in prev:
                        t.C(pt)
                        if t.post is not None:
                            t.post()
                prev = cur

        def attn_tiles(P, W, h, pb, qsrc, qkey, Kt, Kkey, Vt, Vkey, kb, g, c, kts, rng_fn, bias_fn, bmslot, m, Okey,
                       extra_fn=None):
            tiles = []
            for ti_, kt in enumerate(kts):
                r0, r1 = rng_fn(kt)
                cols = slice(r0 * 128, (r1 + 1) * 128)
                adds = []
                for r in range(r0, r1 + 1):
                    for nm in bias_fn(4 * c + r - kt):
                        adds.append((r, nm))
                if extra_fn is not None:
                    adds = adds + [("x", None)]
                first = (ti_ == 0)
                T = Tile()

                def A0(sk, kt=kt, cols=cols, adds=adds):
                    P.op("pe", lambda e, last=(len(adds) == 0): e.matmul(
                        bk(sk)[:, cols], lhsT=Kt[pb:pb + 64, kb, kt * 128:(kt + 1) * 128], rhs=qsrc[pb:pb + 64, cols],
                        start=True, stop=last, skip_group_check=True),
                        reads=[(Kkey, kt // 4), qkey], writes=[sk])

                def A1(sk, kt=kt, cols=cols, adds=adds):
                    for i, (r, nm) in enumerate(adds):
                        last = (i == len(adds) - 1)
                        if r == "x":
                            extra_fn(P, sk, cols, kt, last)
                            continue
                        if nm == "NM4":
                            P.op("pe", lambda e, r=r, last=last: e.matmul(
                                bk(sk)[:, r * 128:(r + 1) * 128], lhsT=identb[:, :], rhs=NM4b[:, :], start=False, stop=last,
                                skip_group_check=True), reads=["identb", "NM4b"], writes=[sk])
                        else:
                            ti = {"D0": 0, "D1": 1}[nm]
                            for hl in range(2):
                                P.op("pe", lambda e, r=r, ti=ti, hl=hl, last=last: e.matmul(
                                    bk(sk)[:, r * 128:(r + 1) * 128], lhsT=identb[:, :], rhs=W["bm"][:, bmslot, m, 2 * ti + hl, :],
                                    start=False, stop=(last and hl == 1), skip_group_check=True),
                                    reads=["identb", ("bm", bmslot)], writes=[sk])

                def B(sk, pt, cols=cols):
                    P.op("act", lambda e: e.activation(out=W["PT"][:, pt, cols], in_=bk(sk)[:, cols], func=AF.Exp,
                                                       bias=b31[:, h:h + 1]),
                         reads=[sk, "b31"], writes=[("PT", pt)])

                def C(pt, kt=kt, r0=r0, r1=r1, first=first):
                    for r in range(r0, r1 + 1):
                        P.op("pe", lambda e, r=r, st=(first and r == r0): e.matmul(
                            bk(Okey)[:, r * 65:(r + 1) * 65], lhsT=W["PT"][:, pt, r * 128:(r + 1) * 128], rhs=Vt[:, kt, g, :],
                            start=st, stop=False, skip_group_check=True),
                            reads=[("PT", pt), (Vkey, kt // 4), Vkey + "o"], writes=[Okey])
                T.A0, T.A1, T.B, T.C = A0, A1, B, C
                tiles.append(T)
            return tiles

        def transpose_out(P, W, c, wout_d, l):
            for r in range(4):
                for k in range(8):
                    P.op("pe", lambda e, r=r, k=k: e.transpose(out=bankT[:, k * 128:(k + 1) * 128],
                                                               in_=W["acc"][:, r, k * 128:(k + 1) * 128], identity=identb[:]),
                         reads=["acc", "identb"], writes=["bkT"])
                P.op("dve", lambda e, r=r: e.tensor_copy(out=W["oT"][:, :, r * 128:(r + 1) * 128],
                                                         in_=bankT[:, :].rearrange("p (k q) -> p k q", k=8)),
                     reads=["bkT"], writes=["oT"])
            cs = slice(c * 512, (c + 1) * 512)
            for half in range(2):
                slot = load_w(P, W, wout_d[l][:, half * 512:(half + 1) * 512], 512)
                for nn in range(4):
                    n = half * 4 + nn
                    b = bJ[n % 2]
                    for k in range(8):
                        P.op("pe", lambda e, k=k, nn=nn, b=b, slot=slot: e.matmul(
                            bk(b)[:, :], lhsT=W["w"][:, slot, k, nn * 128:(nn + 1) * 128], rhs=W["oT"][:, k, :],
                            start=(k == 0), stop=(k == 7)), reads=[("w", slot), "oT"], writes=[b])
                    P.op("dve", lambda e, n=n, b=b: e.tensor_tensor(out=xT[:, n, cs], in0=bk(b)[:, :], in1=xT[:, n, cs], op=ALU.add),
                         reads=[b, ("xT", c)], writes=[("xT", c)])

        for s in range(nseq):
            for l in range(n_a):
                with ExitStack() as ph:
                    def tsb(name, shape, dt):
                        return ph.enter_context(nc.sbuf_tensor(f"t_{name}_a{s}{l}", list(shape), dt))
                    wstate["n"] = 0
                    wl = []
                    for c_ in range(NCH):
                        wl.append((awin_d[l][:, 0:512], 512))
                        for H_ in range(2):
                            wl.append((awin_d[l][:, 512 + H_ * 1024:1024 + H_ * 1024], 512))
                            wl.append((awin_d[l][:, 1024 + H_ * 1024:1536 + H_ * 1024], 512))
                        wl.append((awout_d[l][:, 0:512], 512))
                        wl.append((awout_d[l][:, 512:1024], 512))
                    wstate["list"] = wl
                    W = dict(
                        xs=tsb("xs", (128, 2, 1024), F32), sq=tsb("sq", (128, 2, 512), BF16), rs=tsb("rs", (128, 2, 512), F32),
                        hT=tsb("hT", (128, 8, 512), BF16), w=tsb("w", (128, 2, 8, 512), BF16), sqh=tsb("sqh", (128, 512), BF16),
                        QT=tsb("QT", (128, 4, 512), BF16), zs=tsb("zs", (128, 4, 512), BF16), PT=tsb("PT", (128, 4, 512), BF16),
                        bm=tsb("bm", (128, 2, 2, 6, 128), BF16), acc=tsb("acc", (128, 4, 1024), BF16),
                        oT=tsb("oT", (128, 8, 512), BF16), den=tsb("den", (128, 2, 4), F32), tt=tsb("tt", (128, 2, 4, 64), F32),
                    )
                    if l == 0:
                        load_x(P, W, s)
                    for c in range(NCH):
                        cs = slice(c * 512, (c + 1) * 512)
                        norm_chunk(P, W, c, l)
                        slot = load_w(P, W, awin_d[l][:, 0:512], 512)
                        proj_norm_seq(P, W, slot, [(blk * 128, (lambda blk=blk, cs=cs: KT0[:, blk, cs]), ("KT0", c), 2 + l, 1.0)
                                                   for blk in range(2)])
                        v_tm(P, W, slot, 256, V0, "V0", c)
                        for H in range(2):
                            slotq = load_w(P, W)
                            proj_norm_seq(P, W, slotq, [(jj * 128, (lambda jj=jj: W["QT"][:, jj, :]), ("QT", jj), l, 0.125)
                                                        for jj in range(4)])
                            slotz = load_w(P, W)
                            for r in range(4):
                                b = bJ[r % 2]
                                for k in range(8):
                                    P.op("pe", lambda e, k=k, r=r, b=b, slotz=slotz: e.matmul(
                                        bk(b)[:, :], lhsT=W["hT"][:, k, r * 128:(r + 1) * 128], rhs=W["w"][:, slotz, k, :],
                                        start=(k == 0), stop=(k == 7)), reads=[("w", slotz), "hT"], writes=[b])
                                P.op("act", lambda e, r=r, b=b: e.activation(out=W["zs"][:, r, :], in_=bk(b)[:, :], func=AF.Silu),
                                     reads=[b], writes=["zs"])
                            tiles = []
                            bmslots = {0: load_bm(P, W, 4 * H)}
                            for jj in range(4):
                                j = 4 * H + jj
                                ntiles0 = len(tiles)
                                tls = []
                                for m in range(2):
                                    h = HP[2 * j + m]
                                    g = h // 4
                                    Okey = bO[ctr["O"] % 2]
                                    ctr["O"] += 1
                                    dsl = ctr["O"] % 2
                                    if jj not in bmslots:
                                        bmslots[jj] = (bmslots[jj - 1] + 1) % 2
                                    bmslot = bmslots[jj]
                                    tl = attn_tiles(P, W, h, 64 * m, W["QT"][:, jj, :], ("QT", jj), KT0, "KT0", V0, "V0", g // 2, g, c,
                                                    list(range(max(0, 4 * c - 1), 4 * c + 4)),
                                                    lambda kt, c=c: (max(0, kt - 4 * c), min(3, kt - 4 * c + 1)),
                                                    lambda dl: {0: ["D0"], 1: ["D1", "NM4"]}[dl], bmslot, m, Okey)

                                    def post(Okey=Okey, h=h, dsl=dsl, H=H, ll=l):
                                        Ov = bk(Okey)[:, 0:260].rearrange("p (r e) -> p r e", r=4)
                                        P.op("dve", lambda e: e.tensor_scalar(
                                            out=W["den"][:, dsl, :], in0=Ov[:, :, 64], scalar1=esink[:, ll, h:h + 1], scalar2=None, op0=ALU.add),
                                            reads=[Okey, "esink"], writes=[("den", dsl)])
                                        P.op("dve", lambda e: e.reciprocal(out=W["den"][:, dsl, :], in_=W["den"][:, dsl, :]),
                                             reads=[("den", dsl)], writes=[("den", dsl)])
                                        P.op("dve", lambda e: e.tensor_tensor(
                                            out=W["tt"][:, dsl, :, :], in0=Ov[:, :, 0:64],
                                            in1=W["den"][:, dsl, :].unsqueeze(2).to_broadcast([128, 4, 64]), op=ALU.mult),
                                            reads=[Okey, ("den", dsl)], writes=[("tt", dsl)])
                                        hh = h - 8 * H
                                        P.op("pool", lambda e: e.tensor_tensor(
                                            out=W["acc"][:, :, h * 64:(h + 1) * 64], in0=W["tt"][:, dsl, :, :],
                                            in1=W["zs"][:, :, hh * 64:(hh + 1) * 64], op=ALU.mult),
                                            reads=[("tt", dsl), "zs"], writes=["acc"])
                                    tl[-1].post = post
                                    tls.append(tl)
                                tiles.extend(interleave(tls[0], tls[1]))
                                if jj + 1 < 4:
                                    def pre(jn=jj + 1, H=H):
                                        load_bm(P, W, 4 * H + jn)
                                    tiles[ntiles0].pre = pre
                            run_pipeline(P, tiles)
                        transpose_out(P, W, c, awout_d, l)
                    if dbg and l == n_a - 1 and not do_kv:
                        store_x(P, W, s)
                    P.flush()
            if not do_kv:
                continue
            with ExitStack() as ph:
                def tsb(name, shape, dt):
                    return ph.enter_context(nc.sbuf_tensor(f"t_{name}_kv{s}", list(shape), dt))
                wstate["n"] = 0
                wstate["list"] = [(kvw_d[:, i * 512:(i + 1) * 512], 512) for _ in range(NCH) for i in range(3)]
                W = dict(
                    sq=tsb("sq", (128, 2, 512), BF16), rs=tsb("rs", (128, 2, 512), F32),
                    hT=tsb("hT", (128, 8, 512), BF16), w=tsb("w", (128, 2, 8, 512), BF16), sqh=tsb("sqh", (128, 512), BF16),
                    KC=[tsb("KC0", (128, 2, S), BF16), tsb("VC0", (128, 2, S), BF16)],
                    w1X=tsb("w1X", (128, 32, 256), BF16), w2d=tsb("w2d", (128, 2, 128), BF16),
                    H1s=tsb("H1s", (128, 2, 128), BF16), posb=tsb("posb", (128, 2), F32), posT=tsb("posT", (128, 32), BF16),
                )
                for c in range(NCH):
                    cs = slice(c * 512, (c + 1) * 512)
                    norm_chunk(P, W, c, 2)
                    slot = load_w(P, W)
                    for kind in range(2):
                        for blk in range(2):
                            b = bJ[blk]
                            proj_fm(P, W, b, slot, kind * 256 + blk * 128)
                            P.op("act", lambda e, b=b, kind=kind, blk=blk, cs=cs: e.activation(
                                out=W["KC"][kind][:, blk, cs], in_=bk(b)[:, :], func=AF.Copy), reads=[b], writes=[("KC", kind)])
                    for br, (KTt, Kkey, Vt, Vkey, gcol) in enumerate([(KT0, "KT0", V0, "V0", 5), (KT1, "KT1", V1, "V1", 6)]):
                        slot = load_w(P, W)
                        proj_norm_seq(P, W, slot, [(blk * 128, (lambda blk=blk, cs=cs, KTt=KTt: KTt[:, blk, cs]), (Kkey, c), gcol, 1.0)
                                                   for blk in range(2)])
                        v_tm(P, W, slot, 256, Vt, Vkey, c)
                for kind, (w1_d, w2_d, pos_d) in enumerate([(ckw1_d, ckw2_d, posk_d), (cvw1_d, cvw2_d, posv_d)]):
                    SRC = W["KC"][kind]
                    for hf in range(2):
                        P.op("pool", lambda e, hf=hf, w1_d=w1_d: e.dma_start(
                            out=W["w1X"][hf * 64:(hf + 1) * 64, :, :], in_=w1_d.rearrange("(t d) n -> d t n", d=64)),
                            writes=["w1X"], dma="w1X")
                        P.op("pool", lambda e, hf=hf, w2_d=w2_d: e.dma_start(
                            out=W["w2d"][:, :, hf * 64:(hf + 1) * 64], in_=w2_d.rearrange("(hb p) n -> p hb n", p=128)),
                            writes=["w2d"], dma="w2d")
                    P.op("pool", lambda e, pos_d=pos_d: e.dma_start(out=W["posT"][:, :], in_=pos_d), writes=["posT"], dma="posT")
                    for hb in range(2):
                        for t in range(32):
                            P.op("pe", lambda e, hb=hb, t=t: e.matmul(
                                bk(bM)[:, hb:hb + 1], lhsT=W["w1X"][0:64, t, hb * 128:(hb + 1) * 128], rhs=W["posT"][0:64, t:t + 1],
                                start=(t == 0), stop=(t == 31)), reads=["w1X", "posT"], writes=[bM])
                    P.op("dve", lambda e: e.tensor_copy(out=W["posb"][:, :], in_=bk(bM)[:, 0:2]), reads=[bM], writes=["posb"])
                    for g in range(4):
                        pb = (g % 2) * 64
                        blk = g // 2
                        for hb in range(2):
                            b = bJ[hb]
                            for t in range(32):
                                P.op("pe", lambda e, hb=hb, t=t, b=b, pb=pb, blk=blk, SRC=SRC: e.matmul(
                                    bk(b)[:, 0:127], lhsT=W["w1X"][pb:pb + 64, t, hb * 128:(hb + 1) * 128],
                                    rhs=SRC[pb:pb + 64, blk, t:t + 2017:16], start=(t == 0), stop=(t == 31)),
                                    reads=["w1X", ("KC", kind)], writes=[b])
                            P.op("act", lambda e, hb=hb, b=b: e.activation(out=W["H1s"][:, hb, 0:127], in_=bk(b)[:, 0:127], func=AF.Silu,
                                                                           bias=W["posb"][:, hb:hb + 1]),
                                 reads=[b, "posb"], writes=["H1s"])
                        sk = bS[0]
                        if kind == 0:
                            for hb in range(2):
                                P.op("pe", lambda e, hb=hb, sk=sk: e.matmul(bk(sk)[:, 0:127], lhsT=W["w2d"][:, hb, :], rhs=W["H1s"][:, hb, 0:127],
                                                                          start=(hb == 0), stop=(hb == 1)),
                                     reads=["w2d", "H1s"], writes=[sk])
                            P.op("act", lambda e, sk=sk: e.activation(out=W["sqh"][:, 0:127], in_=bk(sk)[:, 0:127], func=AF.Square),
                                 reads=[sk], writes=["sqh"])
                            P.op("pe", lambda e: e.matmul(bk(bM)[:, 0:127], lhsT=bonesb[:, :], rhs=W["sqh"][:, 0:127], start=True, stop=True),
                                 reads=["sqh", "bonesb"], writes=[bM])
                            P.op("act", lambda e: e.activation(out=W["rs"][:, 1, 0:127], in_=bk(bM)[:, 0:127], func=AF.Ln, bias=EPS,
                                                               scale=1.0 / 64), reads=[bM], writes=[("rs", 1)])
                            P.op("act", lambda e: e.activation(out=W["rs"][:, 1, 0:127], in_=W["rs"][:, 1, 0:127], func=AF.Exp, scale=-0.5),
                                 reads=[("rs", 1)], writes=[("rs", 1)])
                            P.op("dve", lambda e, sk=sk, pb=pb, blk=blk: e.scalar_tensor_tensor(
                                out=KcT[pb:pb + 64, blk, 0:127], in0=bk(sk)[pb:pb + 64, 0:127], scalar=hg[pb:pb + 64, 4:5],
                                in1=W["rs"][pb:pb + 64, 1, 0:127], op0=ALU.mult, op1=ALU.mult),
                                reads=[sk, ("rs", 1), "hg"], writes=["KcT"])
                        else:
                            for hb in range(2):
                                P.op("pe", lambda e, hb=hb, sk=sk: e.matmul(bk(sk)[0:127, 0:64], lhsT=W["H1s"][:, hb, 0:127],
                                                                          rhs=W["w2d"][:, hb, 0:64], start=(hb == 0), stop=(hb == 1)),
                                     reads=["w2d", "H1s"], writes=[sk])
                            P.op("act", lambda e, sk=sk, g=g: e.activation(out=Vc[0:127, g, 0:64], in_=bk(sk)[0:127, 0:64], func=AF.Copy),
                                 reads=[sk], writes=["Vc"])
                P.flush()

            for l in range(n_b):
                with ExitStack() as ph:
                    def tsb(name, shape, dt):
                        return ph.enter_context(nc.sbuf_tensor(f"t_{name}_b{s}{l}", list(shape), dt))
                    wstate["n"] = 0
                    wl = []
                    for c_ in range(NCH):
                        for H_ in range(2):
                            wl.append((bwin_d[l][:, H_ * 2048:H_ * 2048 + 512], 512))
                            for jj_ in range(4):
                                o_ = H_ * 2048 + 512 + jj_ * 384
                                wl.append((bwin_d[l][:, o_:o_ + 384], 384))
                        wl.append((bwout_d[l][:, 0:512], 512))
                        wl.append((bwout_d[l][:, 512:1024], 512))
                    wstate["list"] = wl
                    W = dict(
                        xs=tsb("xs", (128, 2, 1024), F32), sq=tsb("sq", (128, 2, 512), BF16), rs=tsb("rs", (128, 2, 512), F32),
                        hT=tsb("hT", (128, 8, 512), BF16), w=tsb("w", (128, 2, 8, 512), BF16), sqh=tsb("sqh", (128, 512), BF16),
                        QT=tsb("QT", (128, 4, 512), BF16), zs=tsb("zs", (128, 2, 4, 384), BF16), PT=tsb("PT", (128, 4, 512), BF16),
                        bm=tsb("bm", (128, 2, 2, 6, 128), BF16), acc=tsb("acc", (128, 4, 1024), BF16),
                        oT=tsb("oT", (128, 8, 512), BF16), den=tsb("den", (128, 4, 4), F32), den2=tsb("den2", (128, 4, 4), F32),
                        tt=tsb("tt", (128, 4, 4, 64), F32), U=tsb("U", (128, 2, 3, 4, 64), F32),
                        Wg=tsb("Wg", (128, 8, 48), BF16), sig=tsb("sig", (128, 4, 48), F32), ogc=tsb("ogc", (128, 8, 4, 64), F32),
                        impg=tsb("impg", (128, 2, 4, 32), F32), imt=tsb("imt", (128, 4, 32), F32), score=tsb("score", (128, 4, 32), F32),
                        top8=tsb("top8", (128, 8), F32), negsel=tsb("negsel", (128, 4, 32), BF16),
                        negselT=tsb("negselT", (64, 512), BF16),
                    )
                    P.op("pool", lambda e, l=l: e.dma_start(out=W["Wg"][:, :, :],
                                                            in_=bwin_d[l][:, 4096:4144].rearrange("(k p) n -> p k n", p=128)),
                         writes=["Wg"], dma="Wg")
                    for c in range(NCH):
                        norm_chunk(P, W, c, 3 + l)
                        for r in range(4):
                            for k in range(8):
                                P.op("pe", lambda e, r=r, k=k: e.matmul(bk(bM)[:, r * 48:(r + 1) * 48], lhsT=W["hT"][:, k, r * 128:(r + 1) * 128],
                                                                        rhs=W["Wg"][:, k, :], start=(k == 0), stop=(k == 7), skip_group_check=True),
                                     reads=["hT", "Wg"], writes=[bM])
                        P.op("act", lambda e: e.activation(out=W["sig"][:, :, :], in_=bk(bM)[:, 0:192].rearrange("p (r n) -> p r n", r=4),
                                                           func=AF.Sigmoid), reads=[bM], writes=["sig"])
                        for H in range(2):
                            slotq = load_w(P, W)
                            proj_norm_seq(P, W, slotq, [(jj * 128, (lambda jj=jj: W["QT"][:, jj, :]), ("QT", jj), 7 + l, 0.125)
                                                        for jj in range(4)])
                            seen = set()
                            tiles = []
                            bmslots = {0: load_bm(P, W, 4 * H)}
                            for jj in range(4):
                                j = 4 * H + jj
                                ntiles0 = len(tiles)
                                if jj not in bmslots:
                                    bmslots[jj] = (bmslots[jj - 1] + 1) % 2
                                bmslot = bmslots[jj]
                                tls = []
                                for m in range(2):
                                    h = HP[2 * j + m]
                                    g = h // 4
                                    gl = g - 2 * H
                                    hl = jj * 2 + m
                                    pb = 64 * m
                                    Okey = bO[ctr["O"] % 2]
                                    ctr["O"] += 1
                                    dsl = ctr["O"] % 2
                                    Ikey = bJ[hl % 2]
                                    tiles_h = []
                                    for r in range(4):
                                        qt = 4 * c + r
                                        ncq = 8 * qt + 8
                                        off = 120 - 8 * qt
                                        T = Tile()

                                        def A0(sk, ncq=ncq, off=off, pb=pb, g=g, jj=jj, r=r, bmslot=bmslot, m=m):
                                            P.op("pe", lambda e: e.matmul(
                                                bk(sk)[0:ncq, 0:128], lhsT=KcT[pb:pb + 64, g // 2, 0:ncq],
                                                rhs=W["QT"][pb:pb + 64, jj, r * 128:(r + 1) * 128],
                                                start=True, stop=False, skip_group_check=True), reads=["KcT", ("QT", jj)], writes=[sk])

                                        def A1(sk, ncq=ncq, off=off, pb=pb, g=g, jj=jj, r=r, bmslot=bmslot, m=m):
                                            for hl2 in range(2):
                                                P.op("pe", lambda e, hl2=hl2: e.matmul(
                                                    bk(sk)[0:ncq, 0:128], lhsT=identb[:, off:off + ncq], rhs=W["bm"][:, bmslot, m, 4 + hl2, :],
                                                    start=False, stop=(hl2 == 1), skip_group_check=True),
                                                    reads=["identb", ("bm", bmslot)], writes=[sk])

                                        def B(sk, pt, ncq=ncq, h=h):
                                            P.op("act", lambda e: e.activation(
                                                out=W["PT"][0:ncq, pt, 0:128], in_=bk(sk)[0:ncq, 0:128], func=AF.Exp, bias=b31[0:ncq, h:h + 1]),
                                                reads=[sk, "b31"], writes=[("PT", pt)])

                                        def C(pt, ncq=ncq, Okey=Okey, Ikey=Ikey, g=g, r=r):
                                            P.op("pe", lambda e: e.matmul(
                                                bk(Okey)[:, r * 65:(r + 1) * 65], lhsT=W["PT"][0:ncq, pt, 0:128], rhs=Vc[0:ncq, g, :],
                                                start=(r == 0), stop=False, skip_group_check=True), reads=[("PT", pt), "Vc"], writes=[Okey])
                                            P.op("pe", lambda e: e.matmul(
                                                bk(Ikey)[:, r * 33:(r + 1) * 33], lhsT=W["PT"][0:ncq, pt, 0:128], rhs=ovxb[0:ncq, :],
                                                start=(r == 0), stop=False, skip_group_check=True), reads=[("PT", pt), "ovxb"], writes=[Ikey])
                                        T.A0, T.A1, T.B, T.C = A0, A1, B, C
                                        tiles_h.append(T)
                                    first_g = gl not in seen
                                    seen.add(gl)

                                    def post(Okey=Okey, Ikey=Ikey, dsl=dsl, first_g=first_g, gl=gl, h=h, hl=hl):
                                        Ov = bk(Okey)[:, 0:260].rearrange("p (r e) -> p r e", r=4)
                                        Iv = bk(Ikey)[:, 0:132].rearrange("p (r e) -> p r e", r=4)
                                        P.op("dve", lambda e: e.tensor_scalar(
                                            out=W["den"][:, dsl, :], in0=Ov[:, :, 64], scalar1=1e-30, scalar2=None, op0=ALU.max),
                                            reads=[Okey], writes=[("den", dsl)])
                                        P.op("dve", lambda e: e.reciprocal(out=W["den"][:, dsl, :], in_=W["den"][:, dsl, :]),
                                             reads=[("den", dsl)], writes=[("den", dsl)])
                                        dst = W["impg"][:, gl, :, :] if first_g else W["imt"][:, :, :]
                                        P.op("dve", lambda e: e.tensor_tensor(
                                            out=dst, in0=Iv[:, :, 0:32], in1=W["den"][:, dsl, :].unsqueeze(2).to_broadcast([128, 4, 32]), op=ALU.mult),
                                            reads=[Ikey, ("den", dsl)], writes=[("impg", gl) if first_g else "imt"])
                                        if not first_g:
                                            P.op("dve", lambda e: e.tensor_tensor(out=W["impg"][:, gl, :, :], in0=W["impg"][:, gl, :, :],
                                                                                  in1=W["imt"][:, :, :], op=ALU.add),
                                                 reads=["imt", ("impg", gl)], writes=[("impg", gl)])
                                        P.op("dve", lambda e: e.tensor_tensor(
                                            out=W["den2"][:, dsl, :], in0=W["den"][:, dsl, :], in1=W["sig"][:, :, h], op=ALU.mult),
                                            reads=[("den", dsl), "sig"], writes=[("den2", dsl)])
                                        P.op("dve", lambda e: e.tensor_tensor(
                                            out=W["ogc"][:, hl, :, :], in0=Ov[:, :, 0:64],
                                            in1=W["den2"][:, dsl, :].unsqueeze(2).to_broadcast([128, 4, 64]), op=ALU.mult),
                                            reads=[Okey, ("den2", dsl)], writes=[("ogc", hl)])
                                    tiles_h[-1].post = post
                                    tls.append(tiles_h)
                                tiles.extend(interleave(tls[0], tls[1]))
                                if jj + 1 < 4:
                                    def pre(jn=jj + 1, H=H):
                                        load_bm(P, W, 4 * H + jn)
                                    tiles[ntiles0].pre = pre
                            run_pipeline(P, tiles, bS3)
                            def emit_z(jj, H=H):
                                slotz = load_w(P, W)
                                zsl = jj % 2
                                for r in range(4):
                                    b = bJ[r % 2]
                                    for k in range(8):
                                        P.op("pe", lambda e, k=k, r=r, b=b, slotz=slotz: e.matmul(
                                            bk(b)[:, 0:384], lhsT=W["hT"][:, k, r * 128:(r + 1) * 128], rhs=W["w"][:, slotz, k, 0:384],
                                            start=(k == 0), stop=(k == 7)), reads=[("w", slotz), "hT"], writes=[b])
                                    P.op("act", lambda e, r=r, b=b, zsl=zsl: e.activation(out=W["zs"][:, zsl, r, :], in_=bk(b)[:, 0:384], func=AF.Silu),
                                         reads=[b], writes=[("zs", zsl)])
                            emit_z(0)
                            for gl in range(2):
                                P.op("dve", lambda e, gl=gl, c=c: e.tensor_tensor(out=W["score"][:, :, :], in0=W["impg"][:, gl, :, :],
                                                                                in1=MF[:, 4 * c:4 * c + 4, :], op=ALU.add),
                                     reads=[("impg", gl), "MF"], writes=["score"])
                                for r in range(4):
                                    P.op("dve", lambda e, r=r: e.max(out=W["top8"][:, :], in_=W["score"][:, r, :]), reads=["score"], writes=["top8"])
                                    P.op("dve", lambda e, r=r: e.tensor_scalar(out=W["negsel"][:, r, :], in0=W["score"][:, r, :],
                                                                               scalar1=W["top8"][:, 7:8], scalar2=NEGM, op0=ALU.is_lt, op1=ALU.mult),
                                         reads=["score", "top8"], writes=["negsel"])
                                for r in range(4):
                                    P.op("pe", lambda e, r=r, gl=gl: e.matmul(bk(bM)[32 * gl:32 * gl + 32, r * 128:(r + 1) * 128], lhsT=W["negsel"][:, r, :], rhs=identb[:, :],
                                                                       start=True, stop=True, skip_group_check=True),
                                         reads=["negsel", "identb"], writes=[bM])
                                P.op("act", lambda e, gl=gl: e.activation(out=W["negselT"][32 * gl:32 * gl + 32, :], in_=bk(bM)[32 * gl:32 * gl + 32, :], func=AF.Copy),
                                     reads=[bM], writes=[("negselT", gl)])
                            bmslots = {0: load_bm(P, W, 4 * H)}
                            alltiles = []
                            for jj in range(4):
                                j = 4 * H + jj
                                zsl = jj % 2
                                if jj not in bmslots:
                                    bmslots[jj] = (bmslots[jj - 1] + 1) % 2
                                bmslot = bmslots[jj]
                                tiles = []
                                tlmb = {}
                                for m in range(2):
                                    h = HP[2 * j + m]
                                    g = h // 4
                                    gl = g - 2 * H
                                    hl = jj * 2 + m
                                    for br in range(2):
                                        Okey = bO[m]
                                        ctr["O"] += 1
                                        dsl = ctr["O"] % 4
                                        if br == 0:
                                            def extra(P_, sk, cols, kt, last, gl=gl):
                                                P_.op("pe", lambda e: e.matmul(
                                                    bk(sk)[:, cols], lhsT=indb[32 * gl:32 * gl + 32, kt * 128:(kt + 1) * 128], rhs=W["negselT"][32 * gl:32 * gl + 32, cols],
                                                    start=False, stop=last, skip_group_check=True),
                                                    reads=["indb", ("negselT", gl)], writes=[sk])
                                            tl = attn_tiles(P, W, h, 64 * m, W["QT"][:, jj, :], ("QT", jj), KT0, "KT0", V0, "V0", g // 2, g, c,
                                                            list(range(0, 4 * c + 4)), lambda kt, c=c: (max(0, kt - 4 * c), 3),
                                                            lambda dl: {0: ["D0"], 1: ["D1"]}.get(dl, []), bmslot, m, Okey, extra_fn=extra)
                                        else:
                                            tl = attn_tiles(P, W, h, 64 * m, W["QT"][:, jj, :], ("QT", jj), KT1, "KT1", V1, "V1", g // 2, g, c,
                                                            list(range(max(0, 4 * c - 4), 4 * c + 4)),
                                                            lambda kt, c=c: (max(0, kt - 4 * c), min(3, kt - 4 * c + 4)),
                                                            lambda dl: {0: ["D0"], 1: ["D1"], 4: ["NM4"]}.get(dl, []), bmslot, m, Okey)

                                        def post(Okey=Okey, dsl=dsl, h=h, br=br, m=m, zsl=zsl, hl=hl):
                                            Ov = bk(Okey)[:, 0:260].rearrange("p (r e) -> p r e", r=4)
                                            P.op("dve", lambda e: e.reciprocal(out=W["den"][:, dsl, :], in_=Ov[:, :, 64]),
                                                 reads=[Okey], writes=[("den", dsl)])
                                            P.op("dve", lambda e: e.tensor_tensor(
                                                out=W["den2"][:, dsl, :], in0=W["den"][:, dsl, :], in1=W["sig"][:, :, 16 * (br + 1) + h], op=ALU.mult),
                                                reads=[("den", dsl), "sig"], writes=[("den2", dsl)])
                                            P.op("dve", lambda e: e.tensor_tensor(
                                                out=W["tt"][:, dsl, :, :], in0=Ov[:, :, 0:64],
                                                in1=W["den2"][:, dsl, :].unsqueeze(2).to_broadcast([128, 4, 64]), op=ALU.mult),
                                                reads=[Okey, ("den2", dsl)], writes=[("tt", dsl)])
                                            zo = m * 192 + (br + 1) * 64
                                            P.op("pool", lambda e: e.tensor_tensor(
                                                out=W["U"][:, m, br, :, :], in0=W["tt"][:, dsl, :, :], in1=W["zs"][:, zsl, :, zo:zo + 64], op=ALU.mult),
                                                reads=[("tt", dsl), ("zs", zsl)], writes=[("U", m, br)])
                                            if br == 1:
                                                zo0 = m * 192
                                                P.op("pool", lambda e: e.tensor_tensor(
                                                    out=W["U"][:, m, 2, :, :], in0=W["ogc"][:, hl, :, :], in1=W["zs"][:, zsl, :, zo0:zo0 + 64], op=ALU.mult),
                                                    reads=[("ogc", hl), ("zs", zsl)], writes=[("U", m, 2)])
                                                P.op("pool", lambda e: e.tensor_tensor(out=W["U"][:, m, 0, :, :], in0=W["U"][:, m, 0, :, :],
                                                                                       in1=W["U"][:, m, 1, :, :], op=ALU.add),
                                                     reads=[("U", m, 0), ("U", m, 1)], writes=[("U", m, 0)])
                                                P.op("pool", lambda e: e.tensor_tensor(out=W["acc"][:, :, h * 64:(h + 1) * 64], in0=W["U"][:, m, 0, :, :],
                                                                                       in1=W["U"][:, m, 2, :, :], op=ALU.add),
                                                     reads=[("U", m, 0), ("U", m, 2)], writes=["acc"])
                                        tl[-1].post = post
                                        tlmb[(m, br)] = tl
                                tiles = interleave(tlmb[(0, 0)], tlmb[(1, 0)]) + interleave(tlmb[(0, 1)], tlmb[(1, 1)])
                                if jj + 1 < 4:
                                    def pre1(jn=jj + 1, H=H):
                                        load_bm(P, W, 4 * H + jn)
                                    tiles[0].pre = pre1

                                    def pre2(jn=jj + 1):
                                        emit_z(jn)
                                    tiles[len(tiles) // 2].pre = pre2
                                alltiles.extend(tiles)
                            run_pipeline(P, alltiles, bS4)
                        transpose_out(P, W, c, bwout_d, l)
                    if l == n_b - 1:
                        store_x(P, W, s)
                    P.flush()
    return nc


_CACHE = {}


def kernel(**inputs):
    sh = _prep_shared(inputs)
    x = np.asarray(inputs["x"], np.float32)
    nc = build()
    in_maps = []
    for i in range(8):
        m = dict(sh)
        m["x"] = np.ascontiguousarray(x[2 * i:2 * i + 2])
        in_maps.append(m)
    res = run_bass_kernel_spmd(nc, in_maps, core_ids=list(range(8)))
    return np.concatenate([r["y"] for r in res.results], axis=0).astype(np.float32)
```

```python
import math
from contextlib import ExitStack
import numpy as np
import concourse.bass as bass
import concourse.mybir as mybir
from concourse.bass_utils import run_bass_kernel_spmd

F32 = mybir.dt.float32
BF16 = mybir.dt.bfloat16
AF = mybir.ActivationFunctionType
ALU = mybir.AluOpType

S = 2048
D = 1024
NCH = 4
EPS = 1e-6
NEGM = -30000.0
NOPREFETCH = False
NBIAS = 2
HP = [0, 4, 1, 5, 2, 6, 3, 7, 8, 12, 9, 13, 10, 14, 11, 15]


class _Op:
    __slots__ = ("eng", "fn", "deps", "signal", "tok", "isdma", "idx")


class Prog:
    ENGS = ("pe", "act", "dve", "pool", "sp")

    def __init__(self, nc, es):
        self.nc, self.es = nc, es
        self.sems = {("eng", e): es.enter_context(nc.semaphore(f"sem_e_{e}")) for e in self.ENGS}
        self.bar = es.enter_context(nc.semaphore("sem_bar"))
        self.eng_cnt = {e: 0 for e in self.ENGS}
        self.dma_cnt = {}
        self.nflush = 0
        self._reset()

    def _reset(self):
        self.ops = {e: [] for e in self.ENGS}
        self.last_w = {}
        self.readers = {}
        self.n = 0

    def op(self, eng, fn, reads=(), writes=(), dma=None):
        o = _Op()
        o.eng, o.fn, o.signal, o.isdma = eng, fn, False, dma is not None
        o.idx = self.n
        self.n += 1
        deps = []
        for b in reads:
            deps.extend(self.last_w.get(b, {}).values())
        for b in writes:
            deps.extend(self.last_w.get(b, {}).values())
            deps.extend(self.readers.get(b, {}).values())
        best = {}
        for d in deps:
            if d.isdma:
                k = ("dma", d.tok[0])
            else:
                if d.eng == eng and eng == "pe":
                    continue
                k = ("eng", d.eng)
            if k not in best or best[k].idx < d.idx:
                best[k] = d
        o.deps = list(best.values())
        for d in o.deps:
            d.signal = True
        wk = ("dma", dma) if o.isdma else ("eng", eng)
        for b in reads:
            self.readers.setdefault(b, {})[wk] = o
        for b in writes:
            self.last_w.setdefault(b, {})[wk] = o
            self.readers[b] = {}
        if o.isdma:
            if dma not in self.dma_cnt:
                self.dma_cnt[dma] = 0
                self.sems[dma] = self.es.enter_context(self.nc.semaphore(f"sem_d{len(self.dma_cnt)}"))
            self.dma_cnt[dma] += 16
            o.tok = (dma, self.dma_cnt[dma])
        self.ops[eng].append(o)
        return o

    def flush(self):
        nc, sems = self.nc, self.sems
        for e in self.ENGS:
            c = self.eng_cnt[e]
            for o in self.ops[e]:
                if not o.isdma:
                    if o.signal:
                        c += 1
                    o.tok = (("eng", e), c)
            self.eng_cnt[e] = c
        ops = self.ops
        self.nflush += 1
        bar_target = 5 * self.nflush
        bar = self.bar
        dma_tot = dict(self.dma_cnt)

        def run(engname):
            def body(eng):
                waited = {}
                mydma = set()
                for o in ops[engname]:
                    for d in o.deps:
                        key, val = d.tok
                        if waited.get(key, 0) < val:
                            eng.wait_ge(sems[key], val)
                            waited[key] = val
                    if o.fn is None:
                        continue
                    ins = o.fn(eng)
                    if o.isdma:
                        ins.then_inc(sems[o.tok[0]], 16)
                        mydma.add(o.tok[0])
                    elif o.signal:
                        ins.then_inc(sems[o.tok[0]], 1)
                eng.drain()
                for k in mydma:
                    eng.wait_ge(sems[k], dma_tot[k])
                eng.sem_inc(bar, 1)
                eng.wait_ge(bar, bar_target)
            return body

        with nc.Block() as blk:
            blk.tensor(run("pe"))
            blk.scalar(run("act"))
            blk.vector(run("dve"))
            blk.gpsimd(run("pool"))
            blk.sync(run("sp"))
        self._reset()


def _t5_bucket(d):
    d = np.maximum(d, 0)
    large = 16 + (np.log(np.maximum(d, 1).astype(np.float32) / np.float32(16)) / np.float32(math.log(8.0))
                  * np.float32(16)).astype(np.int32)
    large = np.minimum(large, 31)
    return np.where(d < 16, d, large)


def _prep_shared(inp):
    f = np.float32
    tab = np.asarray(inp["rel_table"], f)
    k = np.arange(128)[:, None]
    q = np.arange(128)[None, :]
    d0 = q - k
    d1 = 128 + q - k
    dc = q - 16 * k + 1889
    G = np.empty((128, 3, 16, 128), f)
    G[:, 0] = np.transpose(tab[_t5_bucket(d0)], (0, 2, 1))
    G[:, 1] = np.transpose(tab[_t5_bucket(d1)], (0, 2, 1))
    G[:, 2] = np.transpose(tab[_t5_bucket(dc)], (0, 2, 1))
    NM = np.zeros((128, 3, 128), f)
    NM[:, 0] = np.where(d0 >= 0, 0.0, NEGM)
    NM[:, 2] = np.where(dc >= 0, 0.0, NEGM)
    NM4 = np.where(q < k, 0.0, NEGM).astype(f)
    ident = np.eye(128, dtype=f)
    bones = np.kron(np.eye(2, dtype=f), np.ones((64, 64), f))
    ind = np.zeros((32, 2048), f)
    ind[np.arange(2048) // 64, np.arange(2048)] = 1.0
    pos = (np.arange(16)[None, :, None] * 128 + np.arange(128)[:, None, None])
    j = np.arange(32)[None, None, :]
    pb = pos // 64
    causal = j <= pb
    forced = (j == 0) | ((pb - j >= 0) & (pb - j < 2))
    MF = np.where(causal, np.where(forced, 1e6, 0.0), -1e30).astype(f)
    cs = np.arange(128)[:, None] * 16
    ss = np.arange(32)[None, :] * 64
    ov = ((cs < ss + 64) & (cs + 32 > ss)).astype(f)
    ov[127] = 0.0
    ovx = np.concatenate([ov, np.ones((128, 1), f)], axis=1)
    qcols = np.concatenate([np.arange(h * 64, (h + 1) * 64) for h in HP])
    acols = np.concatenate([np.arange(1024, 1536), qcols[:512], 1536 + np.arange(0, 512),
                            qcols[512:], 1536 + np.arange(512, 1024)])
    a_w_in = np.ascontiguousarray(np.asarray(inp["a_w_in"], f)[:, :, acols])
    bcols = []
    for H in range(2):
        bcols.append(qcols[H * 512:(H + 1) * 512])
        for jj in range(4):
            for m in range(2):
                h = HP[2 * (4 * H + jj) + m]
                for c in range(3):
                    bcols.append(1072 + c * 1024 + h * 64 + np.arange(64))
    bcols.append(1024 + np.arange(48))
    bcols = np.concatenate(bcols)
    b_w_in = np.ascontiguousarray(np.asarray(inp["b_w_in"], f)[:, :, bcols])

    def colmajor(v):
        return np.ascontiguousarray(np.asarray(v, f).reshape(8, 128).T)
    ng = np.stack([colmajor(inp["a_norm"][0]), colmajor(inp["a_norm"][1]), colmajor(inp["kv_norm"]),
                   colmajor(inp["b_norm"][0]), colmajor(inp["b_norm"][1])], axis=1)
    hgl = [inp["a_q_gain"][0], inp["a_q_gain"][1], inp["a_k_gain"][0], inp["a_k_gain"][1],
           inp["kv_k_gain"][0], inp["kv_k_gain"][1], inp["kv_k_gain"][2], inp["b_q_gain"][0], inp["b_q_gain"][1]]
    hg = np.stack([np.tile(np.asarray(v, f), 2) for v in hgl], axis=1)
    sinkrep = np.ascontiguousarray(np.broadcast_to(np.asarray(inp["a_sink"], f)[None], (128, 2, 16)))
    b31rep = np.ascontiguousarray(np.broadcast_to(tab[31][None], (128, 16)))
    posk = np.ascontiguousarray(np.tile(np.asarray(inp["cmp_k_pos"], f).T, (2, 1)))
    posv = np.ascontiguousarray(np.tile(np.asarray(inp["cmp_v_pos"], f).T, (2, 1)))
    return dict(
        G=G, NM=NM, NM4=NM4, ident=ident, bones=bones, ind=ind, MF=MF, ovx=ovx,
        a_w_in=a_w_in, a_w_out=np.asarray(inp["a_w_out"], f), b_w_in=b_w_in,
        b_w_out=np.asarray(inp["b_w_out"], f), kv_w=np.asarray(inp["kv_w"], f),
        ng=np.ascontiguousarray(ng), hg=np.ascontiguousarray(hg), sinkrep=sinkrep, b31rep=b31rep,
        posk=posk, posv=posv,
        ck_w1=np.asarray(inp["cmp_k_w1"], f), ck_w2=np.asarray(inp["cmp_k_w2"], f),
        cv_w1=np.asarray(inp["cmp_v_w1"], f), cv_w2=np.asarray(inp["cmp_v_w2"], f),
    )


def build(nseq=2, n_a=2, do_kv=True, n_b=2, dbg=False):
    nc = bass.Bass("TRN2", target_bir_lowering=False)

    def din(name, shape):
        return nc.dram_tensor(name, list(shape), F32, kind="ExternalInput").ap()

    x_d = din("x", (nseq, S, D))
    G_d = din("G", (128, 3, 16, 128))
    NM_d = din("NM", (128, 3, 128))
    NM4_d = din("NM4", (128, 128))
    ident_d = din("ident", (128, 128))
    bones_d = din("bones", (128, 128))
    ind_d = din("ind", (32, 2048))
    MF_d = din("MF", (128, 16, 32))
    ovx_d = din("ovx", (128, 33))
    awin_d = din("a_w_in", (2, D, 2560))
    awout_d = din("a_w_out", (2, D, D))
    bwin_d = din("b_w_in", (2, D, 4144))
    bwout_d = din("b_w_out", (2, D, D))
    kvw_d = din("kv_w", (D, 1536))
    ng_d = din("ng", (128, 5, 8))
    hg_d = din("hg", (128, 9))
    sink_d = din("sinkrep", (128, 2, 16))
    b31_d = din("b31rep", (128, 16))
    posk_d = din("posk", (128, 32))
    posv_d = din("posv", (128, 32))
    ckw1_d = din("ck_w1", (2048, 256))
    ckw2_d = din("ck_w2", (256, 64))
    cvw1_d = din("cv_w1", (2048, 256))
    cvw2_d = din("cv_w2", (256, 64))
    y_d = nc.dram_tensor("y", [nseq, S, D], F32, kind="ExternalOutput").ap()
    BM_d = nc.dram_tensor("BM", [16, 128, 6, 128], BF16, kind="Internal").ap()

    es = ExitStack()
    with es:
        def sb(name, shape, dt):
            return es.enter_context(nc.sbuf_tensor("s_" + name, list(shape), dt))

        def ps(name, shape, dt):
            return es.enter_context(nc.psum_tensor(name, list(shape), dt))

        xT = sb("xT", (128, 8, S), F32)
        KT0 = sb("KT0", (128, 2, S), BF16)
        KT1 = sb("KT1", (128, 2, S), BF16)
        V0 = sb("V0", (128, 16, 4, 65), BF16)
        V1 = sb("V1", (128, 16, 4, 65), BF16)
        KcT = sb("KcT", (128, 2, 128), BF16)
        Vc = sb("Vc", (128, 4, 65), BF16)
        identb = sb("identb", (128, 128), BF16)
        identf = sb("identf", (128, 128), F32)
        bonesb = sb("bonesb", (128, 128), BF16)
        onesb = sb("onesb", (128, 128), BF16)
        indb = sb("indb", (64, 2048), BF16)
        NM4b = sb("NM4b", (128, 128), BF16)
        MF = sb("MF", (128, 16, 32), F32)
        ovxb = sb("ovxb", (128, 33), BF16)
        ng = sb("ng", (128, 5, 8), F32)
        hg = sb("hg", (128, 9), F32)
        esink = sb("esink", (128, 2, 16), F32)
        b31 = sb("b31", (128, 16), F32)
        banks = [ps(f"bk{i}", (128, 512), F32) for i in range(7)]
        bankT = ps("bkT", (128, 1024), BF16)
        bS = [("bk", 0), ("bk", 1)]
        bO = [("bk", 2), ("bk", 3)]
        bJ = [("bk", 4), ("bk", 5)]
        bM = ("bk", 6)

        bankTf = bankT.bitcast(F32)

        def bk(key):
            if key == "bkT":
                return bankTf
            return banks[key[1]]

        P = Prog(nc, es)

        with ExitStack() as ph:
            def tsb(name, shape, dt):
                return ph.enter_context(nc.sbuf_tensor("su_" + name, list(shape), dt))
            Gs = tsb("Gs", (128, 16, 128), F32)
            NMs = tsb("NMs", (128, 3, 128), F32)
            Dh = tsb("Dh", (128, 16, 128), BF16)
            Dl = tsb("Dl", (128, 16, 128), BF16)
            sinks = tsb("sinks", (128, 2, 16), F32)
            P.op("sp", lambda e: e.dma_start(out=identf[:], in_=ident_d), writes=["identf"], dma="c0")
            P.op("sp", lambda e: e.dma_start(out=MF[:], in_=MF_d), writes=["MF"], dma="c1")
            P.op("sp", lambda e: e.dma_start(out=ng[:], in_=ng_d), writes=["ng"], dma="c2")
            P.op("sp", lambda e: e.dma_start(out=hg[:], in_=hg_d), writes=["hg"], dma="c3")
            P.op("sp", lambda e: e.dma_start(out=sinks[:], in_=sink_d), writes=["sinks"], dma="c4")
            P.op("sp", lambda e: e.dma_start(out=b31[:], in_=b31_d), writes=["b31"], dma="c5")
            P.op("sp", lambda e: e.dma_start(out=NMs[:], in_=NM_d), writes=["NMs"], dma="c6")
            P.op("pool", lambda e: e.dma_start(out=identb[:], in_=ident_d), writes=["identb"], dma="p0")
            P.op("pool", lambda e: e.dma_start(out=bonesb[:], in_=bones_d), writes=["bonesb"], dma="p1")
            P.op("pool", lambda e: e.dma_start(out=indb[0:32, :], in_=ind_d), writes=["indb"], dma="p2")
            P.op("pool", lambda e: e.dma_start(out=indb[32:64, :], in_=ind_d), writes=["indb"], dma="p2")
            P.op("pool", lambda e: e.dma_start(out=NM4b[:], in_=NM4_d), writes=["NM4b"], dma="p3")
            P.op("pool", lambda e: e.dma_start(out=ovxb[:], in_=ovx_d), writes=["ovxb"], dma="p4")
            P.op("dve", lambda e: e.memset(onesb[:], 1.0), writes=["onesb"])
            P.op("dve", lambda e: e.memset(V0[:, :, :, 64:65], 1.0), writes=["V0o"])
            P.op("dve", lambda e: e.memset(V1[:, :, :, 64:65], 1.0), writes=["V1o"])
            P.op("dve", lambda e: e.memset(Vc[:, :, 0:64], 0.0), writes=["Vc"])
            P.op("dve", lambda e: e.memset(Vc[:, :, 64:65], 1.0), writes=["Vco"])
            P.op("dve", lambda e: e.memset(KcT[:], 0.0), writes=["KcT"])
            P.op("act", lambda e: e.activation(out=esink[:], in_=sinks[:], func=AF.Exp), reads=["sinks"], writes=["esink"])
            for t in range(3):
                P.op("sp", lambda e, t=t: e.dma_start(out=Gs[:], in_=G_d[:, t]), writes=["Gs"], dma="g")
                P.op("dve", lambda e: e.tensor_tensor(out=Gs[:], in0=Gs[:], in1=b31[:].unsqueeze(2).to_broadcast([128, 16, 128]),
                                                      op=ALU.subtract), reads=["Gs", "b31"], writes=["Gs"])
                if t != 1:
                    P.op("dve", lambda e, t=t: e.tensor_tensor(out=Gs[:], in0=Gs[:],
                                                               in1=NMs[:, t, :].unsqueeze(1).to_broadcast([128, 16, 128]),
                                                               op=ALU.add), reads=["Gs", "NMs"], writes=["Gs"])
                P.op("dve", lambda e: e.tensor_copy(out=Dh[:], in_=Gs[:]), reads=["Gs"], writes=["Dh"])
                P.op("dve", lambda e: e.tensor_tensor(out=Dl[:], in0=Gs[:], in1=Dh[:], op=ALU.subtract),
                     reads=["Gs", "Dh"], writes=["Dl"])
                P.op("sp", lambda e, t=t: e.dma_start(out=BM_d[:, :, 2 * t, :].rearrange("h k q -> k h q"), in_=Dh[:]),
                     reads=["Dh"], writes=["BM"], dma="bmw")
                P.op("sp", lambda e, t=t: e.dma_start(out=BM_d[:, :, 2 * t + 1, :].rearrange("h k q -> k h q"), in_=Dl[:]),
                     reads=["Dl"], writes=["BM"], dma="bmw")
            P.op("sp", None, reads=["BM"])
            P.flush()

        def load_x(P, W, s):
            for t in range(16):
                slot = t % 2
                P.op("sp", lambda e, t=t, slot=slot: e.dma_start(out=W["xs"][:, slot, :], in_=x_d[s, t * 128:(t + 1) * 128, :]),
                     writes=[("xs", slot)], dma=("xs", slot))
                for half in range(2):
                    b = bJ[half]
                    for kk in range(4):
                        k = half * 4 + kk
                        P.op("pe", lambda e, b=b, kk=kk, k=k, slot=slot: e.transpose(
                            out=bk(b)[:, kk * 128:(kk + 1) * 128], in_=W["xs"][:, slot, k * 128:(k + 1) * 128], identity=identf[:]),
                            reads=[("xs", slot), "identf"], writes=[b])
                    eng = "dve" if half == 0 else "act"
                    if eng == "dve":
                        P.op("dve", lambda e, b=b, half=half, t=t: e.tensor_copy(
                            out=xT[:, half * 4:half * 4 + 4, t * 128:(t + 1) * 128],
                            in_=bk(b)[:, :].rearrange("p (k q) -> p k q", k=4)), reads=[b], writes=[("xT", t // 4)])
                    else:
                        P.op("act", lambda e, b=b, half=half, t=t: e.activation(
                            out=xT[:, half * 4:half * 4 + 4, t * 128:(t + 1) * 128],
                            in_=bk(b)[:, :].rearrange("p (k q) -> p k q", k=4), func=AF.Copy), reads=[b], writes=[("xT", t // 4)])

        def store_x(P, W, s):
            nxs = W["nxs"]
            for t in range(16):
                slot = t % nxs
                for half in range(2):
                    b = bJ[half]
                    for kk in range(4):
                        k = half * 4 + kk
                        P.op("pe", lambda e, b=b, kk=kk, k=k, t=t: e.transpose(
                            out=bk(b)[:, kk * 128:(kk + 1) * 128], in_=xT[:, k, t * 128:(t + 1) * 128], identity=identf[:]),
                            reads=[("xT", t // 4), "identf"], writes=[b])
                    if half == 0:
                        P.op("dve", lambda e, b=b, half=half, slot=slot: e.tensor_copy(
                            out=W["xs"][:, slot, half * 512:(half + 1) * 512], in_=bk(b)[:, :]), reads=[b], writes=[("xs", slot)])
                    else:
                        P.op("act", lambda e, b=b, half=half, slot=slot: e.activation(
                            out=W["xs"][:, slot, half * 512:(half + 1) * 512], in_=bk(b)[:, :], func=AF.Copy),
                            reads=[b], writes=[("xs", slot)])
                P.op("sp", lambda e, t=t, slot=slot: e.dma_start(out=y_d[s, t * 128:(t + 1) * 128, :], in_=W["xs"][:, slot, :]),
                     reads=[("xs", slot)], writes=[("y", t)], dma=("ys", slot))
            P.op("sp", None, reads=[("y", t) for t in range(16)])

        def norm_chunk(P, W, c, gi, sbank=None):
            sbank = sbank or bM
            hp = c % 2
            cs = slice(c * 512, (c + 1) * 512)
            for k in range(8):
                sl = k % 2
                P.op("act", lambda e, k=k, sl=sl: e.activation(out=W["sq"][:, sl, :], in_=xT[:, k, cs], func=AF.Square),
                     reads=[("xT", c)], writes=[("sq", sl)])
                P.op("pe", lambda e, k=k, sl=sl: e.matmul(bk(sbank)[:, :], lhsT=onesb[:, :], rhs=W["sq"][:, sl, :],
                                                          start=(k == 0), stop=(k == 7)),
                     reads=[("sq", sl), "onesb"], writes=[sbank])
            P.op("act", lambda e: e.activation(out=W["rs"][:, 0, :], in_=bk(sbank)[:, :], func=AF.Ln, bias=EPS, scale=1.0 / D),
                 reads=[sbank], writes=[("rs", 0)])
            P.op("act", lambda e: e.activation(out=W["rs"][:, 0, :], in_=W["rs"][:, 0, :], func=AF.Exp, scale=-0.5),
                 reads=[("rs", 0)], writes=[("rs", 0)])
            for k in range(8):
                P.op("dve", lambda e, k=k: e.scalar_tensor_tensor(out=W["hT"][:, hp, k, :], in0=xT[:, k, cs], scalar=ng[:, gi, k:k + 1],
                                                                  in1=W["rs"][:, 0, :], op0=ALU.mult, op1=ALU.mult),
                     reads=[("xT", c), ("rs", 0), "ng"], writes=[("hT", hp, k)])

        wstate = {"n": 0}

        def _issue_w(P, W, i):
            src_ap, ncols = wstate["list"][i]
            slot = i % 2
            P.op("pool", lambda e, slot=slot, src_ap=src_ap, ncols=ncols: e.dma_start(
                out=W["w"][:, slot, :, 0:ncols], in_=src_ap.rearrange("(k p) n -> p k n", p=128)),
                writes=[("w", slot)], dma=("w", slot))

        def load_w(P, W, src_ap=None, ncols=None):
            i = wstate["n"]
            wstate["n"] += 1
            if NOPREFETCH:
                _issue_w(P, W, i)
                return i % 2
            if i == 0:
                _issue_w(P, W, 0)
            if i + 1 < len(wstate["list"]):
                _issue_w(P, W, i + 1)
            return i % 2

        def proj_fm(P, W, b, slot, co, ncols=128):
            hs = W["hsel"]
            for k in range(8):
                P.op("pe", lambda e, k=k: e.matmul(bk(b)[0:ncols, :], lhsT=W["w"][:, slot, k, co:co + ncols], rhs=W["hT"][:, hs, k, :],
                                                   start=(k == 0), stop=(k == 7)),
                     reads=[("w", slot), ("hT", hs, k)], writes=[b])

        def headnorm(P, W, b, dest_fn, dest_key, gcol, scale, idx=0):
            sl = idx % 2
            stb = [bM, "bkT"][sl]
            P.op("act", lambda e: e.activation(out=W["sqh"][:, sl, :], in_=bk(b)[:, :], func=AF.Square), reads=[b], writes=[("sqh", sl)])
            P.op("pe", lambda e: e.matmul(bk(stb)[:, :], lhsT=bonesb[:, :], rhs=W["sqh"][:, sl, :], start=True, stop=True),
                 reads=[("sqh", sl), "bonesb"], writes=[stb])
            P.op("act", lambda e: e.activation(out=W["rs"][:, 1 + sl, :], in_=bk(stb)[:, :], func=AF.Ln, bias=EPS, scale=1.0 / 64),
                 reads=[stb], writes=[("rs", 1 + sl)])
            P.op("act", lambda e: e.activation(out=W["rs"][:, 1 + sl, :], in_=W["rs"][:, 1 + sl, :], func=AF.Exp, scale=-0.5,
                                               bias=math.log(scale)), reads=[("rs", 1 + sl)], writes=[("rs", 1 + sl)])
            P.op("dve", lambda e: e.scalar_tensor_tensor(out=dest_fn(), in0=bk(b)[:, :], scalar=hg[:, gcol:gcol + 1],
                                                         in1=W["rs"][:, 1 + sl, :], op0=ALU.mult, op1=ALU.mult),
                 reads=[b, ("rs", 1 + sl), "hg"], writes=[dest_key])

        def proj_norm_seq(P, W, slot, items):
            pbanks = [bJ[0], bJ[1], ("bk", 0), ("bk", 1)]
            n = len(items)
            for i in range(min(2, n)):
                proj_fm(P, W, pbanks[i % 4], slot, items[i][0])
            for i in range(n):
                if i + 2 < n:
                    proj_fm(P, W, pbanks[(i + 2) % 4], slot, items[i + 2][0])
                co, dfn, dkey, gcol, sc = items[i]
                headnorm(P, W, pbanks[i % 4], dfn, dkey, gcol, sc, idx=i)

        def v_tm(P, W, slot, co, Vt, Vkey, c):
            for r in range(4):
                b = bJ[r % 2]
                for k in range(8):
                    P.op("pe", lambda e, k=k, r=r, b=b, hs=W["hsel"]: e.matmul(bk(b)[:, 0:256], lhsT=W["hT"][:, hs, k, r * 128:(r + 1) * 128],
                                                                 rhs=W["w"][:, slot, k, co:co + 256], start=(k == 0), stop=(k == 7)),
                         reads=[("w", slot), ("hT", W["hsel"], k)], writes=[b])
                P.op("act", lambda e, r=r, b=b: e.activation(out=Vt[:, 4 * c + r, :, 0:64],
                                                             in_=bk(b)[:, 0:256].rearrange("p (g d) -> p g d", g=4), func=AF.Copy),
                     reads=[b], writes=[(Vkey, c)])

        bmstate = {"n": 0}

        def load_bm(P, W, j):
            slot = bmstate["n"] % 2
            bmstate["n"] += 1
            for m in range(2):
                h = HP[2 * j + m]
                P.op("sp", lambda e, m=m, h=h, slot=slot: e.dma_start(out=W["bm"][:, slot, m, :, :], in_=BM_d[h]),
                     reads=["BM"], writes=[("bm", slot)], dma=("bm", slot))
            return slot

        ctr = {"S": 0, "O": 0, "PT": 0}

        bS3 = [("bk", 0), ("bk", 1), ("bk", 6)]
        bS4 = [("bk", 0), ("bk", 1), ("bk", 6), "bkT"]

        class Tile:
            __slots__ = ("A0", "A1", "B", "C", "pre", "post", "mate")

            def __init__(self):
                self.pre = None
                self.post = None
                self.mate = False

        def interleave(l1, l2):
            out = []
            for i in range(max(len(l1), len(l2))):
                if i < len(l1):
                    out.append(l1[i])
                    l1[i].mate = i < len(l2)
                if i < len(l2):
                    out.append(l2[i])
            return out

        def run_pipeline(P, tiles, sb=None):
            sb = bS4
            groups = []
            i = 0
            while i < len(tiles):
                if tiles[i].mate and i + 1 < len(tiles):
                    groups.append([tiles[i], tiles[i + 1]])
                    i += 2
                else:
                    groups.append([tiles[i]])
                    i += 1
            prev = None
            for grp in groups + [None]:
                cur = None
                if grp is not None:
                    base = (ctr["S"] % 2) * 2
                    ctr["S"] += 1
                    cur = []
                    for ti, t in enumerate(grp):
                        if t.pre is not None:
                            t.pre()
                        cur.append((t, sb[base + ti], base + ti))
                    for t, sk, pt in cur:
                        t.A0(sk)
                    for t, sk, pt in cur:
                        t.A1(sk)
                    for t, sk, pt in cur:
                        t.B(sk, pt)
                if prev is not None:
                    for t, sk, pt in prev:
                        t.C(pt)
                        if t.post is not None:
                            t.post()
                prev = cur

        def attn_tiles(P, W, h, pb, qsrc, qkey, Kt, Kkey, Vt, Vkey, kb, g, c, kts, rng_fn, bias_fn, bmslot, m, Okey,
                       extra_fn=None):
            tiles = []
            for ti_, kt in enumerate(kts):
                r0, r1 = rng_fn(kt)
                cols = slice(r0 * 128, (r1 + 1) * 128)
                adds = []
                for r in range(r0, r1 + 1):
                    for nm in bias_fn(4 * c + r - kt):
                        adds.append((r, nm))
                if extra_fn is not None:
                    adds = adds + [("x", None)]
                first = (ti_ == 0)
                T = Tile()

                def A0(sk, kt=kt, cols=cols, adds=adds):
                    P.op("pe", lambda e, last=(len(adds) == 0): e.matmul(
                        bk(sk)[:, cols], lhsT=Kt[pb:pb + 64, kb, kt * 128:(kt + 1) * 128], rhs=qsrc[pb:pb + 64, cols],
                        start=True, stop=last, skip_group_check=True),
                        reads=[(Kkey, kt // 4), qkey], writes=[sk])

                def A1(sk, kt=kt, cols=cols, adds=adds):
                    for i, (r, nm) in enumerate(adds):
                        last = (i == len(adds) - 1)
                        if r == "x":
                            extra_fn(P, sk, cols, kt, last)
                            continue
                        if nm == "NM4":
                            P.op("pe", lambda e, r=r, last=last: e.matmul(
                                bk(sk)[:, r * 128:(r + 1) * 128], lhsT=identb[:, :], rhs=NM4b[:, :], start=False, stop=last,
                                skip_group_check=True), reads=["identb", "NM4b"], writes=[sk])
                        else:
                            ti = {"D0": 0, "D1": 1}[nm]
                            for hl in range(NBIAS):
                                P.op("pe", lambda e, r=r, ti=ti, hl=hl, last=last: e.matmul(
                                    bk(sk)[:, r * 128:(r + 1) * 128], lhsT=identb[:, :], rhs=W["bm"][:, bmslot, m, 2 * ti + hl, :],
                                    start=False, stop=(last and hl == NBIAS - 1), skip_group_check=True),
                                    reads=["identb", ("bm", bmslot)], writes=[sk])

                def B(sk, pt, cols=cols):
                    P.op("act", lambda e: e.activation(out=W["PT"][:, pt, cols], in_=bk(sk)[:, cols], func=AF.Exp,
                                                       bias=b31[:, h:h + 1]),
                         reads=[sk, "b31"], writes=[("PT", pt)])

                def C(pt, kt=kt, r0=r0, r1=r1, first=first):
                    for r in range(r0, r1 + 1):
                        P.op("pe", lambda e, r=r, st=(first and r == r0): e.matmul(
                            bk(Okey)[:, r * 65:(r + 1) * 65], lhsT=W["PT"][:, pt, r * 128:(r + 1) * 128], rhs=Vt[:, kt, g, :],
                            start=st, stop=False, skip_group_check=True),
                            reads=[("PT", pt), (Vkey, kt // 4), Vkey + "o"], writes=[Okey])
                T.A0, T.A1, T.B, T.C = A0, A1, B, C
                tiles.append(T)
            return tiles

        def transpose_out(P, W, c, wout_d, l):
            for r in range(4):
                for k in range(8):
                    P.op("pe", lambda e, r=r, k=k: e.transpose(out=bankT[:, k * 128:(k + 1) * 128],
                                                               in_=W["acc"][:, r, k * 128:(k + 1) * 128], identity=identb[:]),
                         reads=["acc", "identb"], writes=["bkT"])
                P.op("dve", lambda e, r=r: e.tensor_copy(out=W["oT"][:, :, r * 128:(r + 1) * 128],
                                                         in_=bankT[:, :].rearrange("p (k q) -> p k q", k=8)),
                     reads=["bkT"], writes=["oT"])
            cs = slice(c * 512, (c + 1) * 512)
            for half in range(2):
                slot = load_w(P, W, wout_d[l][:, half * 512:(half + 1) * 512], 512)
                for nn in range(4):
                    n = half * 4 + nn
                    b = bJ[n % 2]
                    for k in range(8):
                        P.op("pe", lambda e, k=k, nn=nn, b=b, slot=slot: e.matmul(
                            bk(b)[:, :], lhsT=W["w"][:, slot, k, nn * 128:(nn + 1) * 128], rhs=W["oT"][:, k, :],
                            start=(k == 0), stop=(k == 7)), reads=[("w", slot), "oT"], writes=[b])
                    P.op("dve", lambda e, n=n, b=b: e.tensor_tensor(out=xT[:, n, cs], in0=bk(b)[:, :], in1=xT[:, n, cs], op=ALU.add),
                         reads=[b, ("xT", c)], writes=[("xT", c)])

        for s in range(nseq):
            for l in range(n_a):
                with ExitStack() as ph:
                    def tsb(name, shape, dt):
                        return ph.enter_context(nc.sbuf_tensor(f"t_{name}_a{s}{l}", list(shape), dt))
                    wstate["n"] = 0
                    wl = []
                    for c_ in range(NCH):
                        wl.append((awin_d[l][:, 0:512], 512))
                        for H_ in range(2):
                            wl.append((awin_d[l][:, 512 + H_ * 1024:1024 + H_ * 1024], 512))
                            wl.append((awin_d[l][:, 1024 + H_ * 1024:1536 + H_ * 1024], 512))
                        wl.append((awout_d[l][:, 0:512], 512))
                        wl.append((awout_d[l][:, 512:1024], 512))
                    wstate["list"] = wl
                    W = dict(
                        xs=tsb("xs", (128, 2, 1024), F32), sq=tsb("sq", (128, 2, 512), BF16), rs=tsb("rs", (128, 3, 512), F32),
                        hT=tsb("hT", (128, 2, 8, 512), BF16), w=tsb("w", (128, 2, 8, 512), BF16), sqh=tsb("sqh", (128, 2, 512), BF16),
                        QT=tsb("QT", (128, 4, 512), BF16), zs=tsb("zs", (128, 4, 512), BF16), PT=tsb("PT", (128, 4, 512), BF16),
                        bm=tsb("bm", (128, 2, 2, 6, 128), BF16), acc=tsb("acc", (128, 4, 1024), BF16),
                        oT=tsb("oT", (128, 8, 512), BF16), den=tsb("den", (128, 2, 4), F32), tt=tsb("tt", (128, 2, 4, 64), F32),
                    )
                    if l == 0:
                        load_x(P, W, s)
                    for c in range(NCH):
                        cs = slice(c * 512, (c + 1) * 512)
                        if c == 0:
                            norm_chunk(P, W, c, l)
                        W["hsel"] = c % 2
                        slot = load_w(P, W, awin_d[l][:, 0:512], 512)
                        proj_norm_seq(P, W, slot, [(blk * 128, (lambda blk=blk, cs=cs: KT0[:, blk, cs]), ("KT0", c), 2 + l, 1.0)
                                                   for blk in range(2)])
                        v_tm(P, W, slot, 256, V0, "V0", c)
                        for H in range(2):
                            slotq = load_w(P, W)
                            proj_norm_seq(P, W, slotq, [(jj * 128, (lambda jj=jj: W["QT"][:, jj, :]), ("QT", jj), l, 0.125)
                                                        for jj in range(4)])
                            slotz = load_w(P, W)
                            for r in range(4):
                                b = bJ[r % 2]
                                for k in range(8):
                                    P.op("pe", lambda e, k=k, r=r, b=b, slotz=slotz, hs=W["hsel"]: e.matmul(
                                        bk(b)[:, :], lhsT=W["hT"][:, hs, k, r * 128:(r + 1) * 128], rhs=W["w"][:, slotz, k, :],
                                        start=(k == 0), stop=(k == 7)), reads=[("w", slotz), ("hT", W["hsel"], k)], writes=[b])
                                P.op("act", lambda e, r=r, b=b: e.activation(out=W["zs"][:, r, :], in_=bk(b)[:, :], func=AF.Silu),
                                     reads=[b], writes=["zs"])
                            tiles = []
                            bmslots = {0: load_bm(P, W, 4 * H)}
                            for jj in range(4):
                                j = 4 * H + jj
                                ntiles0 = len(tiles)
                                tls = []
                                for m in range(2):
                                    h = HP[2 * j + m]
                                    g = h // 4
                                    Okey = bO[ctr["O"] % 2]
                                    ctr["O"] += 1
                                    dsl = ctr["O"] % 2
                                    if jj not in bmslots:
                                        bmslots[jj] = (bmslots[jj - 1] + 1) % 2
                                    bmslot = bmslots[jj]
                                    tl = attn_tiles(P, W, h, 64 * m, W["QT"][:, jj, :], ("QT", jj), KT0, "KT0", V0, "V0", g // 2, g, c,
                                                    list(range(max(0, 4 * c - 1), 4 * c + 4)),
                                                    lambda kt, c=c: (max(0, kt - 4 * c), min(3, kt - 4 * c + 1)),
                                                    lambda dl: {0: ["D0"], 1: ["D1", "NM4"]}[dl], bmslot, m, Okey)

                                    def post(Okey=Okey, h=h, dsl=dsl, H=H, ll=l):
                                        Ov = bk(Okey)[:, 0:260].rearrange("p (r e) -> p r e", r=4)
                                        P.op("dve", lambda e: e.tensor_scalar(
                                            out=W["den"][:, dsl, :], in0=Ov[:, :, 64], scalar1=esink[:, ll, h:h + 1], scalar2=None, op0=ALU.add),
                                            reads=[Okey, "esink"], writes=[("den", dsl)])
                                        P.op("dve", lambda e: e.reciprocal(out=W["den"][:, dsl, :], in_=W["den"][:, dsl, :]),
                                             reads=[("den", dsl)], writes=[("den", dsl)])
                                        P.op("dve", lambda e: e.tensor_tensor(
                                            out=W["tt"][:, dsl, :, :], in0=Ov[:, :, 0:64],
                                            in1=W["den"][:, dsl, :].unsqueeze(2).to_broadcast([128, 4, 64]), op=ALU.mult),
                                            reads=[Okey, ("den", dsl)], writes=[("tt", dsl)])
                                        hh = h - 8 * H
                                        P.op("pool", lambda e: e.tensor_tensor(
                                            out=W["acc"][:, :, h * 64:(h + 1) * 64], in0=W["tt"][:, dsl, :, :],
                                            in1=W["zs"][:, :, hh * 64:(hh + 1) * 64], op=ALU.mult),
                                            reads=[("tt", dsl), "zs"], writes=["acc"])
                                    tl[-1].post = post
                                    tls.append(tl)
                                tiles.extend(interleave(tls[0], tls[1]))
                                if jj + 1 < 4:
                                    def pre(jn=jj + 1, H=H):
                                        load_bm(P, W, 4 * H + jn)
                                    tiles[ntiles0].pre = pre
                            if H == 1 and c + 1 < NCH:
                                def prenorm(cn=c + 1, ll=l):
                                    norm_chunk(P, W, cn, ll, bJ[0])
                                tiles[1].pre = prenorm
                            run_pipeline(P, tiles)
                        transpose_out(P, W, c, awout_d, l)
                    if dbg and l == n_a - 1 and not do_kv:
                        W["nxs"] = 2
                        store_x(P, W, s)
                    P.flush()
            if not do_kv:
                continue
            with ExitStack() as ph:
                def tsb(name, shape, dt):
                    return ph.enter_context(nc.sbuf_tensor(f"t_{name}_kv{s}", list(shape), dt))
                wstate["n"] = 0
                wstate["list"] = [(kvw_d[:, i * 512:(i + 1) * 512], 512) for _ in range(NCH) for i in range(3)]
                W = dict(
                    sq=tsb("sq", (128, 2, 512), BF16), rs=tsb("rs", (128, 3, 512), F32),
                    hT=tsb("hT", (128, 2, 8, 512), BF16), w=tsb("w", (128, 2, 8, 512), BF16), sqh=tsb("sqh", (128, 2, 512), BF16),
                    KC=[tsb("KC0", (128, 2, S), BF16), tsb("VC0", (128, 2, S), BF16)],
                    w1X=tsb("w1X", (128, 32, 256), BF16), w2d=tsb("w2d", (128, 2, 128), BF16),
                    H1s=tsb("H1s", (128, 2, 128), BF16), posb=tsb("posb", (128, 2), F32), posT=tsb("posT", (128, 32), BF16),
                )
                for c in range(NCH):
                    cs = slice(c * 512, (c + 1) * 512)
                    norm_chunk(P, W, c, 2)
                    W["hsel"] = c % 2
                    slot = load_w(P, W)
                    for kind in range(2):
                        for blk in range(2):
                            b = bJ[blk]
                            proj_fm(P, W, b, slot, kind * 256 + blk * 128)
                            P.op("act", lambda e, b=b, kind=kind, blk=blk, cs=cs: e.activation(
                                out=W["KC"][kind][:, blk, cs], in_=bk(b)[:, :], func=AF.Copy), reads=[b], writes=[("KC", kind)])
                    for br, (KTt, Kkey, Vt, Vkey, gcol) in enumerate([(KT0, "KT0", V0, "V0", 5), (KT1, "KT1", V1, "V1", 6)]):
                        slot = load_w(P, W)
                        proj_norm_seq(P, W, slot, [(blk * 128, (lambda blk=blk, cs=cs, KTt=KTt: KTt[:, blk, cs]), (Kkey, c), gcol, 1.0)
                                                   for blk in range(2)])
                        v_tm(P, W, slot, 256, Vt, Vkey, c)
                for kind, (w1_d, w2_d, pos_d) in enumerate([(ckw1_d, ckw2_d, posk_d), (cvw1_d, cvw2_d, posv_d)]):
                    SRC = W["KC"][kind]
                    for hf in range(2):
                        P.op("pool", lambda e, hf=hf, w1_d=w1_d: e.dma_start(
                            out=W["w1X"][hf * 64:(hf + 1) * 64, :, :], in_=w1_d.rearrange("(t d) n -> d t n", d=64)),
                            writes=["w1X"], dma="w1X")
                        P.op("pool", lambda e, hf=hf, w2_d=w2_d: e.dma_start(
                            out=W["w2d"][:, :, hf * 64:(hf + 1) * 64], in_=w2_d.rearrange("(hb p) n -> p hb n", p=128)),
                            writes=["w2d"], dma="w2d")
                    P.op("pool", lambda e, pos_d=pos_d: e.dma_start(out=W["posT"][:, :], in_=pos_d), writes=["posT"], dma="posT")
                    for hb in range(2):
                        for t in range(32):
                            P.op("pe", lambda e, hb=hb, t=t: e.matmul(
                                bk(bM)[:, hb:hb + 1], lhsT=W["w1X"][0:64, t, hb * 128:(hb + 1) * 128], rhs=W["posT"][0:64, t:t + 1],
                                start=(t == 0), stop=(t == 31)), reads=["w1X", "posT"], writes=[bM])
                    P.op("dve", lambda e: e.tensor_copy(out=W["posb"][:, :], in_=bk(bM)[:, 0:2]), reads=[bM], writes=["posb"])
                    for g in range(4):
                        pb = (g % 2) * 64
                        blk = g // 2
                        for hb in range(2):
                            b = bJ[hb]
                            for t in range(32):
                                P.op("pe", lambda e, hb=hb, t=t, b=b, pb=pb, blk=blk, SRC=SRC: e.matmul(
                                    bk(b)[:, 0:127], lhsT=W["w1X"][pb:pb + 64, t, hb * 128:(hb + 1) * 128],
                                    rhs=SRC[pb:pb + 64, blk, t:t + 2017:16], start=(t == 0), stop=(t == 31)),
                                    reads=["w1X", ("KC", kind)], writes=[b])
                            P.op("act", lambda e, hb=hb, b=b: e.activation(out=W["H1s"][:, hb, 0:127], in_=bk(b)[:, 0:127], func=AF.Silu,
                                                                           bias=W["posb"][:, hb:hb + 1]),
                                 reads=[b, "posb"], writes=["H1s"])
                        sk = bS[0]
                        if kind == 0:
                            for hb in range(2):
                                P.op("pe", lambda e, hb=hb, sk=sk: e.matmul(bk(sk)[:, 0:127], lhsT=W["w2d"][:, hb, :], rhs=W["H1s"][:, hb, 0:127],
                                                                          start=(hb == 0), stop=(hb == 1)),
                                     reads=["w2d", "H1s"], writes=[sk])
                            P.op("act", lambda e, sk=sk: e.activation(out=W["sqh"][:, 0, 0:127], in_=bk(sk)[:, 0:127], func=AF.Square),
                                 reads=[sk], writes=[("sqh", 0)])
                            P.op("pe", lambda e: e.matmul(bk(bM)[:, 0:127], lhsT=bonesb[:, :], rhs=W["sqh"][:, 0, 0:127], start=True, stop=True),
                                 reads=[("sqh", 0), "bonesb"], writes=[bM])
                            P.op("act", lambda e: e.activation(out=W["rs"][:, 1, 0:127], in_=bk(bM)[:, 0:127], func=AF.Ln, bias=EPS,
                                                               scale=1.0 / 64), reads=[bM], writes=[("rs", 1)])
                            P.op("act", lambda e: e.activation(out=W["rs"][:, 1, 0:127], in_=W["rs"][:, 1, 0:127], func=AF.Exp, scale=-0.5),
                                 reads=[("rs", 1)], writes=[("rs", 1)])
                            P.op("dve", lambda e, sk=sk, pb=pb, blk=blk: e.scalar_tensor_tensor(
                                out=KcT[pb:pb + 64, blk, 0:127], in0=bk(sk)[pb:pb + 64, 0:127], scalar=hg[pb:pb + 64, 4:5],
                                in1=W["rs"][pb:pb + 64, 1, 0:127], op0=ALU.mult, op1=ALU.mult),
                                reads=[sk, ("rs", 1), "hg"], writes=["KcT"])
                        else:
                            for hb in range(2):
                                P.op("pe", lambda e, hb=hb, sk=sk: e.matmul(bk(sk)[0:127, 0:64], lhsT=W["H1s"][:, hb, 0:127],
                                                                          rhs=W["w2d"][:, hb, 0:64], start=(hb == 0), stop=(hb == 1)),
                                     reads=["w2d", "H1s"], writes=[sk])
                            P.op("act", lambda e, sk=sk, g=g: e.activation(out=Vc[0:127, g, 0:64], in_=bk(sk)[0:127, 0:64], func=AF.Copy),
                                 reads=[sk], writes=["Vc"])
                P.flush()

            for l in range(n_b):
                with ExitStack() as ph:
                    def tsb(name, shape, dt):
                        return ph.enter_context(nc.sbuf_tensor(f"t_{name}_b{s}{l}", list(shape), dt))
                    wstate["n"] = 0
                    wl = []
                    for c_ in range(NCH):
                        for H_ in range(2):
                            wl.append((bwin_d[l][:, H_ * 2048:H_ * 2048 + 512], 512))
                            for jj_ in range(4):
                                o_ = H_ * 2048 + 512 + jj_ * 384
                                wl.append((bwin_d[l][:, o_:o_ + 384], 384))
                        wl.append((bwout_d[l][:, 0:512], 512))
                        wl.append((bwout_d[l][:, 512:1024], 512))
                    wstate["list"] = wl
                    W = dict(
                        sq=tsb("sq", (128, 2, 512), BF16), rs=tsb("rs", (128, 3, 512), F32),
                        hT=tsb("hT", (128, 2, 8, 512), BF16), w=tsb("w", (128, 2, 8, 512), BF16), sqh=tsb("sqh", (128, 2, 512), BF16),
                        QT=tsb("QT", (128, 4, 512), BF16), zs=tsb("zs", (128, 2, 4, 384), BF16), PT=tsb("PT", (128, 4, 512), BF16),
                        bm=tsb("bm", (128, 2, 2, 6, 128), BF16), acc=tsb("acc", (128, 4, 1024), BF16),
                        oT=tsb("oT", (128, 8, 512), BF16), den=tsb("den", (128, 4, 4), F32), den2=tsb("den2", (128, 4, 4), F32),
                        tt=tsb("tt", (128, 4, 4, 64), F32), U=tsb("U", (128, 2, 3, 4, 64), F32),
                        Wg=tsb("Wg", (128, 8, 48), BF16), sig=tsb("sig", (128, 4, 48), F32), ogc=tsb("ogc", (128, 8, 4, 64), F32),
                        impg=tsb("impg", (128, 2, 4, 32), F32), imt=tsb("imt", (128, 4, 32), F32), score=tsb("score", (128, 4, 32), F32),
                        top8=tsb("top8", (128, 8), F32), negsel=tsb("negsel", (128, 4, 32), BF16),
                        negselT=tsb("negselT", (64, 512), BF16),
                    )
                    W["xs"] = W["U"][:, :, :, :, :].rearrange("p a b c d -> p (a b c d)")[:, 0:1024].rearrange("p (s n) -> p s n", s=1)
                    P.op("pool", lambda e, l=l: e.dma_start(out=W["Wg"][:, :, :],
                                                            in_=bwin_d[l][:, 4096:4144].rearrange("(k p) n -> p k n", p=128)),
                         writes=["Wg"], dma="Wg")
                    for c in range(NCH):
                        if c == 0:
                            norm_chunk(P, W, c, 3 + l)
                        W["hsel"] = c % 2
                        for r in range(4):
                            for k in range(8):
                                P.op("pe", lambda e, r=r, k=k, hs=W["hsel"]: e.matmul(bk(bM)[:, r * 48:(r + 1) * 48],
                                                                        lhsT=W["hT"][:, hs, k, r * 128:(r + 1) * 128],
                                                                        rhs=W["Wg"][:, k, :], start=(k == 0), stop=(k == 7), skip_group_check=True),
                                     reads=[("hT", W["hsel"], k), "Wg"], writes=[bM])
                        P.op("act", lambda e: e.activation(out=W["sig"][:, :, :], in_=bk(bM)[:, 0:192].rearrange("p (r n) -> p r n", r=4),
                                                           func=AF.Sigmoid), reads=[bM], writes=["sig"])
                        for H in range(2):
                            slotq = load_w(P, W)
                            proj_norm_seq(P, W, slotq, [(jj * 128, (lambda jj=jj: W["QT"][:, jj, :]), ("QT", jj), 7 + l, 0.125)
                                                        for jj in range(4)])
                            seen = set()
                            tiles = []
                            bmslots = {0: load_bm(P, W, 4 * H)}
                            for jj in range(4):
                                j = 4 * H + jj
                                ntiles0 = len(tiles)
                                if jj not in bmslots:
                                    bmslots[jj] = (bmslots[jj - 1] + 1) % 2
                                bmslot = bmslots[jj]
                                tls = []
                                for m in range(2):
                                    h = HP[2 * j + m]
                                    g = h // 4
                                    gl = g - 2 * H
                                    hl = jj * 2 + m
                                    pb = 64 * m
                                    Okey = bO[ctr["O"] % 2]
                                    ctr["O"] += 1
                                    dsl = ctr["O"] % 2
                                    Ikey = bJ[hl % 2]
                                    tiles_h = []
                                    for r in range(4):
                                        qt = 4 * c + r
                                        ncq = 8 * qt + 8
                                        off = 120 - 8 * qt
                                        T = Tile()

                                        def A0(sk, ncq=ncq, off=off, pb=pb, g=g, jj=jj, r=r, bmslot=bmslot, m=m):
                                            P.op("pe", lambda e: e.matmul(
                                                bk(sk)[0:ncq, 0:128], lhsT=KcT[pb:pb + 64, g // 2, 0:ncq],
                                                rhs=W["QT"][pb:pb + 64, jj, r * 128:(r + 1) * 128],
                                                start=True, stop=False, skip_group_check=True), reads=["KcT", ("QT", jj)], writes=[sk])

                                        def A1(sk, ncq=ncq, off=off, pb=pb, g=g, jj=jj, r=r, bmslot=bmslot, m=m):
                                            for hl2 in range(2):
                                                P.op("pe", lambda e, hl2=hl2: e.matmul(
                                                    bk(sk)[0:ncq, 0:128], lhsT=identb[:, off:off + ncq], rhs=W["bm"][:, bmslot, m, 4 + hl2, :],
                                                    start=False, stop=(hl2 == 1), skip_group_check=True),
                                                    reads=["identb", ("bm", bmslot)], writes=[sk])

                                        def B(sk, pt, ncq=ncq, h=h):
                                            P.op("act", lambda e: e.activation(
                                                out=W["PT"][0:ncq, pt, 0:128], in_=bk(sk)[0:ncq, 0:128], func=AF.Exp, bias=b31[0:ncq, h:h + 1]),
                                                reads=[sk, "b31"], writes=[("PT", pt)])

                                        def C(pt, ncq=ncq, Okey=Okey, Ikey=Ikey, g=g, r=r):
                                            P.op("pe", lambda e: e.matmul(
                                                bk(Okey)[:, r * 65:(r + 1) * 65], lhsT=W["PT"][0:ncq, pt, 0:128], rhs=Vc[0:ncq, g, :],
                                                start=(r == 0), stop=False, skip_group_check=True), reads=[("PT", pt), "Vc"], writes=[Okey])
                                            P.op("pe", lambda e: e.matmul(
                                                bk(Ikey)[:, r * 33:(r + 1) * 33], lhsT=W["PT"][0:ncq, pt, 0:128], rhs=ovxb[0:ncq, :],
                                                start=(r == 0), stop=False, skip_group_check=True), reads=[("PT", pt), "ovxb"], writes=[Ikey])
                                        T.A0, T.A1, T.B, T.C = A0, A1, B, C
                                        tiles_h.append(T)
                                    first_g = gl not in seen
                                    seen.add(gl)

                                    def post(Okey=Okey, Ikey=Ikey, dsl=dsl, first_g=first_g, gl=gl, h=h, hl=hl):
                                        Ov = bk(Okey)[:, 0:260].rearrange("p (r e) -> p r e", r=4)
                                        Iv = bk(Ikey)[:, 0:132].rearrange("p (r e) -> p r e", r=4)
                                        P.op("dve", lambda e: e.tensor_scalar(
                                            out=W["den"][:, dsl, :], in0=Ov[:, :, 64], scalar1=1e-30, scalar2=None, op0=ALU.max),
                                            reads=[Okey], writes=[("den", dsl)])
                                        P.op("dve", lambda e: e.reciprocal(out=W["den"][:, dsl, :], in_=W["den"][:, dsl, :]),
                                             reads=[("den", dsl)], writes=[("den", dsl)])
                                        dst = W["impg"][:, gl, :, :] if first_g else W["imt"][:, :, :]
                                        P.op("dve", lambda e: e.tensor_tensor(
                                            out=dst, in0=Iv[:, :, 0:32], in1=W["den"][:, dsl, :].unsqueeze(2).to_broadcast([128, 4, 32]), op=ALU.mult),
                                            reads=[Ikey, ("den", dsl)], writes=[("impg", gl) if first_g else "imt"])
                                        if not first_g:
                                            P.op("dve", lambda e: e.tensor_tensor(out=W["impg"][:, gl, :, :], in0=W["impg"][:, gl, :, :],
                                                                                  in1=W["imt"][:, :, :], op=ALU.add),
                                                 reads=["imt", ("impg", gl)], writes=[("impg", gl)])
                                        P.op("dve", lambda e: e.tensor_tensor(
                                            out=W["den2"][:, dsl, :], in0=W["den"][:, dsl, :], in1=W["sig"][:, :, h], op=ALU.mult),
                                            reads=[("den", dsl), "sig"], writes=[("den2", dsl)])
                                        P.op("dve", lambda e: e.tensor_tensor(
                                            out=W["ogc"][:, hl, :, :], in0=Ov[:, :, 0:64],
                                            in1=W["den2"][:, dsl, :].unsqueeze(2).to_broadcast([128, 4, 64]), op=ALU.mult),
                                            reads=[Okey, ("den2", dsl)], writes=[("ogc", hl)])
                                    tiles_h[-1].post = post
                                    tls.append(tiles_h)
                                tiles.extend(interleave(tls[0], tls[1]))
                                if jj + 1 < 4:
                                    def pre(jn=jj + 1, H=H):
                                        load_bm(P, W, 4 * H + jn)
                                    tiles[ntiles0].pre = pre
                            run_pipeline(P, tiles, bS3)
                            def emit_z(jj, H=H):
                                slotz = load_w(P, W)
                                zsl = jj % 2
                                for r in range(4):
                                    b = bJ[r % 2]
                                    for k in range(8):
                                        P.op("pe", lambda e, k=k, r=r, b=b, slotz=slotz, hs=W["hsel"]: e.matmul(
                                            bk(b)[:, 0:384], lhsT=W["hT"][:, hs, k, r * 128:(r + 1) * 128], rhs=W["w"][:, slotz, k, 0:384],
                                            start=(k == 0), stop=(k == 7)), reads=[("w", slotz), ("hT", W["hsel"], k)], writes=[b])
                                    P.op("act", lambda e, r=r, b=b, zsl=zsl: e.activation(out=W["zs"][:, zsl, r, :], in_=bk(b)[:, 0:384], func=AF.Silu),
                                         reads=[b], writes=[("zs", zsl)])
                            emit_z(0)
                            for gl in range(2):
                                P.op("dve", lambda e, gl=gl, c=c: e.tensor_tensor(out=W["score"][:, :, :], in0=W["impg"][:, gl, :, :],
                                                                                in1=MF[:, 4 * c:4 * c + 4, :], op=ALU.add),
                                     reads=[("impg", gl), "MF"], writes=["score"])
                                for r in range(4):
                                    P.op("dve", lambda e, r=r: e.max(out=W["top8"][:, :], in_=W["score"][:, r, :]), reads=["score"], writes=["top8"])
                                    P.op("dve", lambda e, r=r: e.tensor_scalar(out=W["negsel"][:, r, :], in0=W["score"][:, r, :],
                                                                               scalar1=W["top8"][:, 7:8], scalar2=NEGM, op0=ALU.is_lt, op1=ALU.mult),
                                         reads=["score", "top8"], writes=["negsel"])
                                for r in range(4):
                                    P.op("pe", lambda e, r=r, gl=gl: e.matmul(bk(bM)[32 * gl:32 * gl + 32, r * 128:(r + 1) * 128], lhsT=W["negsel"][:, r, :], rhs=identb[:, :],
                                                                       start=True, stop=True, skip_group_check=True),
                                         reads=["negsel", "identb"], writes=[bM])
                                P.op("act", lambda e, gl=gl: e.activation(out=W["negselT"][32 * gl:32 * gl + 32, :], in_=bk(bM)[32 * gl:32 * gl + 32, :], func=AF.Copy),
                                     reads=[bM], writes=[("negselT", gl)])
                            bmslots = {0: load_bm(P, W, 4 * H)}
                            alltiles = []
                            for jj in range(4):
                                j = 4 * H + jj
                                zsl = jj % 2
                                if jj not in bmslots:
                                    bmslots[jj] = (bmslots[jj - 1] + 1) % 2
                                bmslot = bmslots[jj]
                                tiles = []
                                tlmb = {}
                                for m in range(2):
                                    h = HP[2 * j + m]
                                    g = h // 4
                                    gl = g - 2 * H
                                    hl = jj * 2 + m
                                    for br in range(2):
                                        Okey = bO[m]
                                        ctr["O"] += 1
                                        dsl = ctr["O"] % 4
                                        if br == 0:
                                            def extra(P_, sk, cols, kt, last, gl=gl):
                                                P_.op("pe", lambda e: e.matmul(
                                                    bk(sk)[:, cols], lhsT=indb[32 * gl:32 * gl + 32, kt * 128:(kt + 1) * 128], rhs=W["negselT"][32 * gl:32 * gl + 32, cols],
                                                    start=False, stop=last, skip_group_check=True),
                                                    reads=["indb", ("negselT", gl)], writes=[sk])
                                            tl = attn_tiles(P, W, h, 64 * m, W["QT"][:, jj, :], ("QT", jj), KT0, "KT0", V0, "V0", g // 2, g, c,
                                                            list(range(0, 4 * c + 4)), lambda kt, c=c: (max(0, kt - 4 * c), 3),
                                                            lambda dl: {0: ["D0"], 1: ["D1"]}.get(dl, []), bmslot, m, Okey, extra_fn=extra)
                                        else:
                                            tl = attn_tiles(P, W, h, 64 * m, W["QT"][:, jj, :], ("QT", jj), KT1, "KT1", V1, "V1", g // 2, g, c,
                                                            list(range(max(0, 4 * c - 4), 4 * c + 4)),
                                                            lambda kt, c=c: (max(0, kt - 4 * c), min(3, kt - 4 * c + 4)),
                                                            lambda dl: {0: ["D0"], 1: ["D1"], 4: ["NM4"]}.get(dl, []), bmslot, m, Okey)

                                        def post(Okey=Okey, dsl=dsl, h=h, br=br, m=m, zsl=zsl, hl=hl):
                                            Ov = bk(Okey)[:, 0:260].rearrange("p (r e) -> p r e", r=4)
                                            P.op("dve", lambda e: e.reciprocal(out=W["den"][:, dsl, :], in_=Ov[:, :, 64]),
                                                 reads=[Okey], writes=[("den", dsl)])
                                            P.op("dve", lambda e: e.tensor_tensor(
                                                out=W["den2"][:, dsl, :], in0=W["den"][:, dsl, :], in1=W["sig"][:, :, 16 * (br + 1) + h], op=ALU.mult),
                                                reads=[("den", dsl), "sig"], writes=[("den2", dsl)])
                                            P.op("dve", lambda e: e.tensor_tensor(
                                                out=W["tt"][:, dsl, :, :], in0=Ov[:, :, 0:64],
                                                in1=W["den2"][:, dsl, :].unsqueeze(2).to_broadcast([128, 4, 64]), op=ALU.mult),
                                                reads=[Okey, ("den2", dsl)], writes=[("tt", dsl)])
                                            zo = m * 192 + (br + 1) * 64
                                            P.op("pool", lambda e: e.tensor_tensor(
                                                out=W["U"][:, m, br, :, :], in0=W["tt"][:, dsl, :, :], in1=W["zs"][:, zsl, :, zo:zo + 64], op=ALU.mult),
                                                reads=[("tt", dsl), ("zs", zsl)], writes=[("U", m, br)])
                                            if br == 1:
                                                zo0 = m * 192
                                                P.op("pool", lambda e: e.tensor_tensor(
                                                    out=W["U"][:, m, 2, :, :], in0=W["ogc"][:, hl, :, :], in1=W["zs"][:, zsl, :, zo0:zo0 + 64], op=ALU.mult),
                                                    reads=[("ogc", hl), ("zs", zsl)], writes=[("U", m, 2)])
                                                P.op("pool", lambda e: e.tensor_tensor(out=W["U"][:, m, 0, :, :], in0=W["U"][:, m, 0, :, :],
                                                                                       in1=W["U"][:, m, 1, :, :], op=ALU.add),
                                                     reads=[("U", m, 0), ("U", m, 1)], writes=[("U", m, 0)])
                                                P.op("pool", lambda e: e.tensor_tensor(out=W["acc"][:, :, h * 64:(h + 1) * 64], in0=W["U"][:, m, 0, :, :],
                                                                                       in1=W["U"][:, m, 2, :, :], op=ALU.add),
                                                     reads=[("U", m, 0), ("U", m, 2)], writes=["acc"])
                                        tl[-1].post = post
                                        tlmb[(m, br)] = tl
                                tiles = interleave(tlmb[(0, 0)], tlmb[(1, 0)]) + interleave(tlmb[(0, 1)], tlmb[(1, 1)])
                                if jj + 1 < 4:
                                    def pre1(jn=jj + 1, H=H):
                                        load_bm(P, W, 4 * H + jn)
                                    tiles[0].pre = pre1

                                    def pre2(jn=jj + 1):
                                        emit_z(jn)
                                    tiles[len(tiles) // 2].pre = pre2
                                alltiles.extend(tiles)
                            if H == 1 and c + 1 < NCH:
                                def prenorm(cn=c + 1, ll=l):
                                    norm_chunk(P, W, cn, 3 + ll, bJ[0])
                                alltiles[1].pre = prenorm
                            run_pipeline(P, alltiles, bS4)
                        transpose_out(P, W, c, bwout_d, l)
                    if l == n_b - 1:
                        W["nxs"] = 1
                        store_x(P, W, s)
                    P.flush()
    return nc


_CACHE = {}


def kernel(**inputs):
    sh = _prep_shared(inputs)
    x = np.asarray(inputs["x"], np.float32)
    nc = build()
    in_maps = []
    for i in range(8):
        m = dict(sh)
        m["x"] = np.ascontiguousarray(x[2 * i:2 * i + 2])
        in_maps.append(m)
    res = run_bass_kernel_spmd(nc, in_maps, core_ids=list(range(8)))
    return np.concatenate([r["y"] for r in res.results], axis=0).astype(np.float32)
```

```python
import math
from contextlib import ExitStack
import numpy as np
import concourse.bass as bass
import concourse.mybir as mybir
from concourse.bass_utils import run_bass_kernel_spmd

F32 = mybir.dt.float32
BF16 = mybir.dt.bfloat16
AF = mybir.ActivationFunctionType
ALU = mybir.AluOpType

S = 2048
D = 1024
NCH = 4
EPS = 1e-6
NEGM = -30000.0
NOPREFETCH = False
NBIAS = 2
HP = [0, 4, 1, 5, 2, 6, 3, 7, 8, 12, 9, 13, 10, 14, 11, 15]


class _Op:
    __slots__ = ("eng", "fn", "deps", "signal", "tok", "isdma", "idx")


class Prog:
    ENGS = ("pe", "act", "dve", "pool", "sp")

    def __init__(self, nc, es):
        self.nc, self.es = nc, es
        self.sems = {("eng", e): es.enter_context(nc.semaphore(f"sem_e_{e}")) for e in self.ENGS}
        self.bar = es.enter_context(nc.semaphore("sem_bar"))
        self.eng_cnt = {e: 0 for e in self.ENGS}
        self.dma_cnt = {}
        self.nflush = 0
        self._reset()

    def _reset(self):
        self.ops = {e: [] for e in self.ENGS}
        self.last_w = {}
        self.readers = {}
        self.n = 0

    def op(self, eng, fn, reads=(), writes=(), dma=None):
        o = _Op()
        o.eng, o.fn, o.signal, o.isdma = eng, fn, False, dma is not None
        o.idx = self.n
        self.n += 1
        deps = []
        for b in reads:
            deps.extend(self.last_w.get(b, {}).values())
        for b in writes:
            deps.extend(self.last_w.get(b, {}).values())
            deps.extend(self.readers.get(b, {}).values())
        best = {}
        for d in deps:
            if d.isdma:
                k = ("dma", d.tok[0])
            else:
                if d.eng == eng and eng == "pe":
                    continue
                k = ("eng", d.eng)
            if k not in best or best[k].idx < d.idx:
                best[k] = d
        o.deps = list(best.values())
        for d in o.deps:
            d.signal = True
        wk = ("dma", dma) if o.isdma else ("eng", eng)
        for b in reads:
            self.readers.setdefault(b, {})[wk] = o
        for b in writes:
            self.last_w.setdefault(b, {})[wk] = o
            self.readers[b] = {}
        if o.isdma:
            if dma not in self.dma_cnt:
                self.dma_cnt[dma] = 0
                self.sems[dma] = self.es.enter_context(self.nc.semaphore(f"sem_d{len(self.dma_cnt)}"))
            self.dma_cnt[dma] += 16
            o.tok = (dma, self.dma_cnt[dma])
        self.ops[eng].append(o)
        return o

    def flush(self):
        nc, sems = self.nc, self.sems
        for e in self.ENGS:
            c = self.eng_cnt[e]
            for o in self.ops[e]:
                if not o.isdma:
                    if o.signal:
                        c += 1
                    o.tok = (("eng", e), c)
            self.eng_cnt[e] = c
        ops = self.ops
        self.nflush += 1
        bar_target = 5 * self.nflush
        bar = self.bar
        dma_tot = dict(self.dma_cnt)

        def run(engname):
            def body(eng):
                waited = {}
                mydma = set()
                for o in ops[engname]:
                    for d in o.deps:
                        key, val = d.tok
                        if waited.get(key, 0) < val:
                            eng.wait_ge(sems[key], val)
                            waited[key] = val
                    if o.fn is None:
                        continue
                    ins = o.fn(eng)
                    if o.isdma:
                        ins.then_inc(sems[o.tok[0]], 16)
                        mydma.add(o.tok[0])
                    elif o.signal:
                        ins.then_inc(sems[o.tok[0]], 1)
                eng.drain()
                for k in mydma:
                    eng.wait_ge(sems[k], dma_tot[k])
                eng.sem_inc(bar, 1)
                eng.wait_ge(bar, bar_target)
            return body

        with nc.Block() as blk:
            blk.tensor(run("pe"))
            blk.scalar(run("act"))
            blk.vector(run("dve"))
            blk.gpsimd(run("pool"))
            blk.sync(run("sp"))
        self._reset()


def _t5_bucket(d):
    d = np.maximum(d, 0)
    large = 16 + (np.log(np.maximum(d, 1).astype(np.float32) / np.float32(16)) / np.float32(math.log(8.0))
                  * np.float32(16)).astype(np.int32)
    large = np.minimum(large, 31)
    return np.where(d < 16, d, large)


def _prep_shared(inp):
    f = np.float32
    tab = np.asarray(inp["rel_table"], f)
    k = np.arange(128)[:, None]
    q = np.arange(128)[None, :]
    d0 = q - k
    d1 = 128 + q - k
    dc = q - 16 * k + 1889
    G = np.empty((128, 3, 16, 128), f)
    G[:, 0] = np.transpose(tab[_t5_bucket(d0)], (0, 2, 1))
    G[:, 1] = np.transpose(tab[_t5_bucket(d1)], (0, 2, 1))
    G[:, 2] = np.transpose(tab[_t5_bucket(dc)], (0, 2, 1))
    NM = np.zeros((128, 3, 128), f)
    NM[:, 0] = np.where(d0 >= 0, 0.0, NEGM)
    NM[:, 2] = np.where(dc >= 0, 0.0, NEGM)
    NM4 = np.where(q < k, 0.0, NEGM).astype(f)
    ident = np.eye(128, dtype=f)
    bones = np.kron(np.eye(2, dtype=f), np.ones((64, 64), f))
    ind = np.zeros((32, 2048), f)
    ind[np.arange(2048) // 64, np.arange(2048)] = 1.0
    pos = (np.arange(16)[None, :, None] * 128 + np.arange(128)[:, None, None])
    j = np.arange(32)[None, None, :]
    pb = pos // 64
    causal = j <= pb
    forced = (j == 0) | ((pb - j >= 0) & (pb - j < 2))
    MF = np.where(causal, np.where(forced, 1e6, 0.0), -1e30).astype(f)
    cs = np.arange(128)[:, None] * 16
    ss = np.arange(32)[None, :] * 64
    ov = ((cs < ss + 64) & (cs + 32 > ss)).astype(f)
    ov[127] = 0.0
    ovx = np.concatenate([ov, np.ones((128, 1), f)], axis=1)
    qcols = np.concatenate([np.arange(h * 64, (h + 1) * 64) for h in HP])
    acols = np.concatenate([np.arange(1024, 1536), qcols[:512], 1536 + np.arange(0, 512),
                            qcols[512:], 1536 + np.arange(512, 1024)])
    a_w_in = np.ascontiguousarray(np.asarray(inp["a_w_in"], f)[:, :, acols])
    bcols = []
    for H in range(2):
        bcols.append(qcols[H * 512:(H + 1) * 512])
        for jj in range(4):
            for m in range(2):
                h = HP[2 * (4 * H + jj) + m]
                for c in range(3):
                    bcols.append(1072 + c * 1024 + h * 64 + np.arange(64))
    bcols.append(1024 + np.arange(48))
    bcols = np.concatenate(bcols)
    b_w_in = np.ascontiguousarray(np.asarray(inp["b_w_in"], f)[:, :, bcols])

    def colmajor(v):
        return np.ascontiguousarray(np.asarray(v, f).reshape(8, 128).T)
    ng = np.stack([colmajor(inp["a_norm"][0]), colmajor(inp["a_norm"][1]), colmajor(inp["kv_norm"]),
                   colmajor(inp["b_norm"][0]), colmajor(inp["b_norm"][1])], axis=1)
    hgl = [inp["a_q_gain"][0], inp["a_q_gain"][1], inp["a_k_gain"][0], inp["a_k_gain"][1],
           inp["kv_k_gain"][0], inp["kv_k_gain"][1], inp["kv_k_gain"][2], inp["b_q_gain"][0], inp["b_q_gain"][1]]
    hg = np.stack([np.tile(np.asarray(v, f), 2) for v in hgl], axis=1)
    sinkrep = np.ascontiguousarray(np.broadcast_to(np.asarray(inp["a_sink"], f)[None], (128, 2, 16)))
    b31rep = np.ascontiguousarray(np.broadcast_to(tab[31][None], (128, 16)))
    posk = np.ascontiguousarray(np.tile(np.asarray(inp["cmp_k_pos"], f).T, (2, 1)))
    posv = np.ascontiguousarray(np.tile(np.asarray(inp["cmp_v_pos"], f).T, (2, 1)))
    return dict(
        G=G, NM=NM, NM4=NM4, ident=ident, bones=bones, ind=ind, MF=MF, ovx=ovx,
        a_w_in=a_w_in, a_w_out=np.asarray(inp["a_w_out"], f), b_w_in=b_w_in,
        b_w_out=np.asarray(inp["b_w_out"], f), kv_w=np.asarray(inp["kv_w"], f),
        ng=np.ascontiguousarray(ng), hg=np.ascontiguousarray(hg), sinkrep=sinkrep, b31rep=b31rep,
        posk=posk, posv=posv,
        ck_w1=np.asarray(inp["cmp_k_w1"], f), ck_w2=np.asarray(inp["cmp_k_w2"], f),
        cv_w1=np.asarray(inp["cmp_v_w1"], f), cv_w2=np.asarray(inp["cmp_v_w2"], f),
    )


def build(nseq=2, n_a=2, do_kv=True, n_b=2, dbg=False):
    nc = bass.Bass("TRN2", target_bir_lowering=False)

    def din(name, shape):
        return nc.dram_tensor(name, list(shape), F32, kind="ExternalInput").ap()

    x_d = din("x", (nseq, S, D))
    G_d = din("G", (128, 3, 16, 128))
    NM_d = din("NM", (128, 3, 128))
    NM4_d = din("NM4", (128, 128))
    ident_d = din("ident", (128, 128))
    bones_d = din("bones", (128, 128))
    ind_d = din("ind", (32, 2048))
    MF_d = din("MF", (128, 16, 32))
    ovx_d = din("ovx", (128, 33))
    awin_d = din("a_w_in", (2, D, 2560))
    awout_d = din("a_w_out", (2, D, D))
    bwin_d = din("b_w_in", (2, D, 4144))
    bwout_d = din("b_w_out", (2, D, D))
    kvw_d = din("kv_w", (D, 1536))
    ng_d = din("ng", (128, 5, 8))
    hg_d = din("hg", (128, 9))
    sink_d = din("sinkrep", (128, 2, 16))
    b31_d = din("b31rep", (128, 16))
    posk_d = din("posk", (128, 32))
    posv_d = din("posv", (128, 32))
    ckw1_d = din("ck_w1", (2048, 256))
    ckw2_d = din("ck_w2", (256, 64))
    cvw1_d = din("cv_w1", (2048, 256))
    cvw2_d = din("cv_w2", (256, 64))
    y_d = nc.dram_tensor("y", [nseq, S, D], F32, kind="ExternalOutput").ap()
    BM_d = nc.dram_tensor("BM", [16, 128, 6, 128], BF16, kind="Internal").ap()

    es = ExitStack()
    with es:
        def sb(name, shape, dt):
            return es.enter_context(nc.sbuf_tensor("s_" + name, list(shape), dt))

        def ps(name, shape, dt):
            return es.enter_context(nc.psum_tensor(name, list(shape), dt))

        xT = sb("xT", (128, 8, S), F32)
        KT0 = sb("KT0", (128, 2, S), BF16)
        KT1 = sb("KT1", (128, 2, S), BF16)
        V0 = sb("V0", (128, 16, 4, 65), BF16)
        V1 = sb("V1", (128, 16, 4, 65), BF16)
        KcT = sb("KcT", (128, 2, 128), BF16)
        Vc = sb("Vc", (128, 4, 65), BF16)
        identb = sb("identb", (128, 128), BF16)
        identf = sb("identf", (128, 128), F32)
        bonesb = sb("bonesb", (128, 128), BF16)
        onesb = sb("onesb", (128, 128), BF16)
        indb = sb("indb", (64, 2048), BF16)
        NM4b = sb("NM4b", (128, 128), BF16)
        MF = sb("MF", (128, 16, 32), F32)
        ovxb = sb("ovxb", (128, 33), BF16)
        ng = sb("ng", (128, 5, 8), F32)
        hg = sb("hg", (128, 9), F32)
        esink = sb("esink", (128, 2, 16), F32)
        b31 = sb("b31", (128, 16), F32)
        banks = [ps(f"bk{i}", (128, 512), F32) for i in range(7)]
        bankT = ps("bkT", (128, 1024), BF16)
        bS = [("bk", 0), ("bk", 1)]
        bO = [("bk", 2), ("bk", 3)]
        bJ = [("bk", 4), ("bk", 5)]
        bM = ("bk", 6)

        bankTf = bankT.bitcast(F32)

        def bk(key):
            if key == "bkT":
                return bankTf
            return banks[key[1]]

        P = Prog(nc, es)

        with ExitStack() as ph:
            def tsb(name, shape, dt):
                return ph.enter_context(nc.sbuf_tensor("su_" + name, list(shape), dt))
            Gs = tsb("Gs", (128, 16, 128), F32)
            NMs = tsb("NMs", (128, 3, 128), F32)
            Dh = tsb("Dh", (128, 16, 128), BF16)
            Dl = tsb("Dl", (128, 16, 128), BF16)
            sinks = tsb("sinks", (128, 2, 16), F32)
            P.op("sp", lambda e: e.dma_start(out=identf[:], in_=ident_d), writes=["identf"], dma="c0")
            P.op("sp", lambda e: e.dma_start(out=MF[:], in_=MF_d), writes=["MF"], dma="c1")
            P.op("sp", lambda e: e.dma_start(out=ng[:], in_=ng_d), writes=["ng"], dma="c2")
            P.op("sp", lambda e: e.dma_start(out=hg[:], in_=hg_d), writes=["hg"], dma="c3")
            P.op("sp", lambda e: e.dma_start(out=sinks[:], in_=sink_d), writes=["sinks"], dma="c4")
            P.op("sp", lambda e: e.dma_start(out=b31[:], in_=b31_d), writes=["b31"], dma="c5")
            P.op("sp", lambda e: e.dma_start(out=NMs[:], in_=NM_d), writes=["NMs"], dma="c6")
            P.op("pool", lambda e: e.dma_start(out=identb[:], in_=ident_d), writes=["identb"], dma="p0")
            P.op("pool", lambda e: e.dma_start(out=bonesb[:], in_=bones_d), writes=["bonesb"], dma="p1")
            P.op("pool", lambda e: e.dma_start(out=indb[0:32, :], in_=ind_d), writes=["indb"], dma="p2")
            P.op("pool", lambda e: e.dma_start(out=indb[32:64, :], in_=ind_d), writes=["indb"], dma="p2")
            P.op("pool", lambda e: e.dma_start(out=NM4b[:], in_=NM4_d), writes=["NM4b"], dma="p3")
            P.op("pool", lambda e: e.dma_start(out=ovxb[:], in_=ovx_d), writes=["ovxb"], dma="p4")
            P.op("dve", lambda e: e.memset(onesb[:], 1.0), writes=["onesb"])
            P.op("dve", lambda e: e.memset(V0[:, :, :, 64:65], 1.0), writes=["V0o"])
            P.op("dve", lambda e: e.memset(V1[:, :, :, 64:65], 1.0), writes=["V1o"])
            P.op("dve", lambda e: e.memset(Vc[:, :, 0:64], 0.0), writes=["Vc"])
            P.op("dve", lambda e: e.memset(Vc[:, :, 64:65], 1.0), writes=["Vco"])
            P.op("dve", lambda e: e.memset(KcT[:], 0.0), writes=["KcT"])
            P.op("dve", lambda e: e.tensor_tensor(out=sinks[:], in0=sinks[:], in1=b31[:].unsqueeze(1).to_broadcast([128, 2, 16]),
                                                  op=ALU.subtract), reads=["sinks", "b31"], writes=["sinks"])
            P.op("act", lambda e: e.activation(out=esink[:], in_=sinks[:], func=AF.Exp), reads=["sinks"], writes=["esink"])
            for t in range(3):
                P.op("sp", lambda e, t=t: e.dma_start(out=Gs[:], in_=G_d[:, t]), writes=["Gs"], dma="g")
                P.op("dve", lambda e: e.tensor_tensor(out=Gs[:], in0=Gs[:], in1=b31[:].unsqueeze(2).to_broadcast([128, 16, 128]),
                                                      op=ALU.subtract), reads=["Gs", "b31"], writes=["Gs"])
                if t != 1:
                    P.op("dve", lambda e, t=t: e.tensor_tensor(out=Gs[:], in0=Gs[:],
                                                               in1=NMs[:, t, :].unsqueeze(1).to_broadcast([128, 16, 128]),
                                                               op=ALU.add), reads=["Gs", "NMs"], writes=["Gs"])
                P.op("dve", lambda e: e.tensor_copy(out=Dh[:], in_=Gs[:]), reads=["Gs"], writes=["Dh"])
                P.op("dve", lambda e: e.tensor_tensor(out=Dl[:], in0=Gs[:], in1=Dh[:], op=ALU.subtract),
                     reads=["Gs", "Dh"], writes=["Dl"])
                P.op("sp", lambda e, t=t: e.dma_start(out=BM_d[:, :, 2 * t, :].rearrange("h k q -> k h q"), in_=Dh[:]),
                     reads=["Dh"], writes=["BM"], dma="bmw")
                P.op("sp", lambda e, t=t: e.dma_start(out=BM_d[:, :, 2 * t + 1, :].rearrange("h k q -> k h q"), in_=Dl[:]),
                     reads=["Dl"], writes=["BM"], dma="bmw")
            P.op("sp", None, reads=["BM"])
            P.flush()

        def load_x(P, W, s):
            for t in range(16):
                slot = t % 2
                P.op("sp", lambda e, t=t, slot=slot: e.dma_start(out=W["xs"][:, slot, :], in_=x_d[s, t * 128:(t + 1) * 128, :]),
                     writes=[("xs", slot)], dma=("xs", slot))
                for half in range(2):
                    b = bJ[half]
                    for kk in range(4):
                        k = half * 4 + kk
                        P.op("pe", lambda e, b=b, kk=kk, k=k, slot=slot: e.transpose(
                            out=bk(b)[:, kk * 128:(kk + 1) * 128], in_=W["xs"][:, slot, k * 128:(k + 1) * 128], identity=identf[:]),
                            reads=[("xs", slot), "identf"], writes=[b])
                    eng = "dve" if half == 0 else "act"
                    if eng == "dve":
                        P.op("dve", lambda e, b=b, half=half, t=t: e.tensor_copy(
                            out=xT[:, half * 4:half * 4 + 4, t * 128:(t + 1) * 128],
                            in_=bk(b)[:, :].rearrange("p (k q) -> p k q", k=4)), reads=[b], writes=[("xT", t // 4)])
                    else:
                        P.op("act", lambda e, b=b, half=half, t=t: e.activation(
                            out=xT[:, half * 4:half * 4 + 4, t * 128:(t + 1) * 128],
                            in_=bk(b)[:, :].rearrange("p (k q) -> p k q", k=4), func=AF.Copy), reads=[b], writes=[("xT", t // 4)])

        def store_x(P, W, s):
            nxs = W["nxs"]
            for t in range(16):
                slot = t % nxs
                for half in range(2):
                    b = bJ[half]
                    for kk in range(4):
                        k = half * 4 + kk
                        P.op("pe", lambda e, b=b, kk=kk, k=k, t=t: e.transpose(
                            out=bk(b)[:, kk * 128:(kk + 1) * 128], in_=xT[:, k, t * 128:(t + 1) * 128], identity=identf[:]),
                            reads=[("xT", t // 4), "identf"], writes=[b])
                    if half == 0:
                        P.op("dve", lambda e, b=b, half=half, slot=slot: e.tensor_copy(
                            out=W["xs"][:, slot, half * 512:(half + 1) * 512], in_=bk(b)[:, :]), reads=[b], writes=[("xs", slot)])
                    else:
                        P.op("act", lambda e, b=b, half=half, slot=slot: e.activation(
                            out=W["xs"][:, slot, half * 512:(half + 1) * 512], in_=bk(b)[:, :], func=AF.Copy),
                            reads=[b], writes=[("xs", slot)])
                P.op("sp", lambda e, t=t, slot=slot: e.dma_start(out=y_d[s, t * 128:(t + 1) * 128, :], in_=W["xs"][:, slot, :]),
                     reads=[("xs", slot)], writes=[("y", t)], dma=("ys", slot))
            P.op("sp", None, reads=[("y", t) for t in range(16)])

        def norm_chunk(P, W, c, gi, sbank=None):
            sbank = sbank or bM
            hp = c % 2
            cs = slice(c * 512, (c + 1) * 512)
            for k in range(8):
                sl = k % 2
                P.op("act", lambda e, k=k, sl=sl: e.activation(out=W["sq"][:, sl, :], in_=xT[:, k, cs], func=AF.Square),
                     reads=[("xT", c)], writes=[("sq", sl)])
                P.op("pe", lambda e, k=k, sl=sl: e.matmul(bk(sbank)[:, :], lhsT=onesb[:, :], rhs=W["sq"][:, sl, :],
                                                          start=(k == 0), stop=(k == 7)),
                     reads=[("sq", sl), "onesb"], writes=[sbank])
            P.op("act", lambda e: e.activation(out=W["rs"][:, 0, :], in_=bk(sbank)[:, :], func=AF.Ln, bias=EPS, scale=1.0 / D),
                 reads=[sbank], writes=[("rs", 0)])
            P.op("act", lambda e: e.activation(out=W["rs"][:, 0, :], in_=W["rs"][:, 0, :], func=AF.Exp, scale=-0.5),
                 reads=[("rs", 0)], writes=[("rs", 0)])
            for k in range(8):
                P.op("dve", lambda e, k=k: e.scalar_tensor_tensor(out=W["hT"][:, hp, k, :], in0=xT[:, k, cs], scalar=ng[:, gi, k:k + 1],
                                                                  in1=W["rs"][:, 0, :], op0=ALU.mult, op1=ALU.mult),
                     reads=[("xT", c), ("rs", 0), "ng"], writes=[("hT", hp, k)])

        wstate = {"n": 0}

        def _issue_w(P, W, i):
            src_ap, ncols = wstate["list"][i]
            slot = i % 2
            P.op("pool", lambda e, slot=slot, src_ap=src_ap, ncols=ncols: e.dma_start(
                out=W["w"][:, slot, :, 0:ncols], in_=src_ap.rearrange("(k p) n -> p k n", p=128)),
                writes=[("w", slot)], dma=("w", slot))

        def load_w(P, W, src_ap=None, ncols=None):
            i = wstate["n"]
            wstate["n"] += 1
            if NOPREFETCH:
                _issue_w(P, W, i)
                return i % 2
            if i == 0:
                _issue_w(P, W, 0)
            if i + 1 < len(wstate["list"]):
                _issue_w(P, W, i + 1)
            return i % 2

        def proj_fm(P, W, b, slot, co, ncols=128):
            hs = W["hsel"]
            for k in range(8):
                P.op("pe", lambda e, k=k: e.matmul(bk(b)[0:ncols, :], lhsT=W["w"][:, slot, k, co:co + ncols], rhs=W["hT"][:, hs, k, :],
                                                   start=(k == 0), stop=(k == 7)),
                     reads=[("w", slot), ("hT", hs, k)], writes=[b])

        def headnorm(P, W, b, dest_fn, dest_key, gcol, scale, idx=0):
            sl = idx % 2
            stb = [bM, "bkT"][sl]
            P.op("act", lambda e: e.activation(out=W["sqh"][:, sl, :], in_=bk(b)[:, :], func=AF.Square), reads=[b], writes=[("sqh", sl)])
            P.op("pe", lambda e: e.matmul(bk(stb)[:, :], lhsT=bonesb[:, :], rhs=W["sqh"][:, sl, :], start=True, stop=True),
                 reads=[("sqh", sl), "bonesb"], writes=[stb])
            P.op("act", lambda e: e.activation(out=W["rs"][:, 1 + sl, :], in_=bk(stb)[:, :], func=AF.Ln, bias=EPS, scale=1.0 / 64),
                 reads=[stb], writes=[("rs", 1 + sl)])
            P.op("act", lambda e: e.activation(out=W["rs"][:, 1 + sl, :], in_=W["rs"][:, 1 + sl, :], func=AF.Exp, scale=-0.5,
                                               bias=math.log(scale)), reads=[("rs", 1 + sl)], writes=[("rs", 1 + sl)])
            P.op("dve", lambda e: e.scalar_tensor_tensor(out=dest_fn(), in0=bk(b)[:, :], scalar=hg[:, gcol:gcol + 1],
                                                         in1=W["rs"][:, 1 + sl, :], op0=ALU.mult, op1=ALU.mult),
                 reads=[b, ("rs", 1 + sl), "hg"], writes=[dest_key])

        def proj_norm_seq(P, W, slot, items):
            pbanks = [bJ[0], bJ[1], ("bk", 0), ("bk", 1)]
            n = len(items)
            for i in range(min(2, n)):
                proj_fm(P, W, pbanks[i % 4], slot, items[i][0])
            for i in range(n):
                if i + 2 < n:
                    proj_fm(P, W, pbanks[(i + 2) % 4], slot, items[i + 2][0])
                co, dfn, dkey, gcol, sc = items[i]
                headnorm(P, W, pbanks[i % 4], dfn, dkey, gcol, sc, idx=i)

        def v_tm(P, W, slot, co, Vt, Vkey, c):
            for r in range(4):
                b = bJ[r % 2]
                for k in range(8):
                    P.op("pe", lambda e, k=k, r=r, b=b, hs=W["hsel"]: e.matmul(bk(b)[:, 0:256], lhsT=W["hT"][:, hs, k, r * 128:(r + 1) * 128],
                                                                 rhs=W["w"][:, slot, k, co:co + 256], start=(k == 0), stop=(k == 7)),
                         reads=[("w", slot), ("hT", W["hsel"], k)], writes=[b])
                P.op("act", lambda e, r=r, b=b: e.activation(out=Vt[:, 4 * c + r, :, 0:64],
                                                             in_=bk(b)[:, 0:256].rearrange("p (g d) -> p g d", g=4), func=AF.Copy),
                     reads=[b], writes=[(Vkey, c)])

        bmstate = {"n": 0}

        def load_bm(P, W, j):
            slot = bmstate["n"] % 2
            bmstate["n"] += 1
            for m in range(2):
                h = HP[2 * j + m]
                P.op("sp", lambda e, m=m, h=h, slot=slot: e.dma_start(out=W["bm"][:, slot, m, :, :], in_=BM_d[h]),
                     reads=["BM"], writes=[("bm", slot)], dma=("bm", slot))
            return slot

        ctr = {"S": 0, "O": 0, "PT": 0}

        bS3 = [("bk", 0), ("bk", 1), ("bk", 6)]
        bS4 = [("bk", 0), ("bk", 1), ("bk", 6), "bkT"]

        class Tile:
            __slots__ = ("A0", "A1", "B", "C", "pre", "post", "mate")

            def __init__(self):
                self.pre = None
                self.post = None
                self.mate = False

        def interleave(l1, l2):
            out = []
            for i in range(max(len(l1), len(l2))):
                if i < len(l1):
                    out.append(l1[i])
                    l1[i].mate = i < len(l2)
                if i < len(l2):
                    out.append(l2[i])
            return out

        def run_pipeline(P, tiles, sb=None):
            sb = bS4
            groups = []
            i = 0
            while i < len(tiles):
                if tiles[i].mate and i + 1 < len(tiles):
                    groups.append([tiles[i], tiles[i + 1]])
                    i += 2
                else:
                    groups.append([tiles[i]])
                    i += 1
            prev = None
            for grp in groups + [None]:
                cur = None
                if grp is not None:
                    base = (ctr["S"] % 2) * 2
                    ctr["S"] += 1
                    cur = []
                    for ti, t in enumerate(grp):
                        if t.pre is not None:
                            t.pre()
                        cur.append((t, sb[base + ti], base + ti))
                    for t, sk, pt in cur:
                        t.A0(sk)
                    for t, sk, pt in cur:
                        t.A1(sk)
                    for t, sk, pt in cur:
                        t.B(sk, pt)
                if prev is not None:
                    for t, sk, pt in prev:
                        t.C(pt)
                        if t.post is not None:
                            t.post()
                prev = cur

        def attn_tiles(P, W, h, pb, qsrc, qkey, Kt, Kkey, Vt, Vkey, kb, g, c, kts, rng_fn, bias_fn, bmslot, m, Okey,
                       extra_fn=None):
            tiles = []
            for ti_, kt in enumerate(kts):
                r0, r1 = rng_fn(kt)
                cols = slice(r0 * 128, (r1 + 1) * 128)
                adds = []
                for r in range(r0, r1 + 1):
                    for nm in bias_fn(4 * c + r - kt):
                        adds.append((r, nm))
                if extra_fn is not None:
                    adds = adds + [("x", None)]
                first = (ti_ == 0)
                T = Tile()

                def A0(sk, kt=kt, cols=cols, adds=adds):
                    P.op("pe", lambda e, last=(len(adds) == 0): e.matmul(
                        bk(sk)[:, cols], lhsT=Kt[pb:pb + 64, kb, kt * 128:(kt + 1) * 128], rhs=qsrc[pb:pb + 64, cols],
                        start=True, stop=last, skip_group_check=True),
                        reads=[(Kkey, kt // 4), qkey], writes=[sk])

                def A1(sk, kt=kt, cols=cols, adds=adds):
                    for i, (r, nm) in enumerate(adds):
                        last = (i == len(adds) - 1)
                        if r == "x":
                            extra_fn(P, sk, cols, kt, last)
                            continue
                        if nm == "NM4":
                            P.op("pe", lambda e, r=r, last=last: e.matmul(
                                bk(sk)[:, r * 128:(r + 1) * 128], lhsT=identb[:, :], rhs=NM4b[:, :], start=False, stop=last,
                                skip_group_check=True), reads=["identb", "NM4b"], writes=[sk])
                        else:
                            ti = {"D0": 0, "D1": 1}[nm]
                            for hl in range(NBIAS):
                                P.op("pe", lambda e, r=r, ti=ti, hl=hl, last=last: e.matmul(
                                    bk(sk)[:, r * 128:(r + 1) * 128], lhsT=identb[:, :], rhs=W["bm"][:, bmslot, m, 2 * ti + hl, :],
                                    start=False, stop=(last and hl == NBIAS - 1), skip_group_check=True),
                                    reads=["identb", ("bm", bmslot)], writes=[sk])

                def B(sk, pt, cols=cols):
                    P.op("act", lambda e: e.activation(out=W["PT"][:, pt, cols], in_=bk(sk)[:, cols], func=AF.Exp),
                         reads=[sk], writes=[("PT", pt)])

                def C(pt, kt=kt, r0=r0, r1=r1, first=first):
                    for r in range(r0, r1 + 1):
                        P.op("pe", lambda e, r=r, st=(first and r == r0): e.matmul(
                            bk(Okey)[:, r * 65:(r + 1) * 65], lhsT=W["PT"][:, pt, r * 128:(r + 1) * 128], rhs=Vt[:, kt, g, :],
                            start=st, stop=False, skip_group_check=True),
                            reads=[("PT", pt), (Vkey, kt // 4), Vkey + "o"], writes=[Okey])
                T.A0, T.A1, T.B, T.C = A0, A1, B, C
                tiles.append(T)
            return tiles

        def transpose_out(P, W, c, wout_d, l):
            for r in range(4):
                for k in range(8):
                    P.op("pe", lambda e, r=r, k=k: e.transpose(out=bankT[:, k * 128:(k + 1) * 128],
                                                               in_=W["acc"][:, r, k * 128:(k + 1) * 128], identity=identb[:]),
                         reads=["acc", "identb"], writes=["bkT"])
                P.op("dve", lambda e, r=r: e.tensor_copy(out=W["oT"][:, :, r * 128:(r + 1) * 128],
                                                         in_=bankT[:, :].rearrange("p (k q) -> p k q", k=8)),
                     reads=["bkT"], writes=["oT"])
            cs = slice(c * 512, (c + 1) * 512)
            for half in range(2):
                slot = load_w(P, W, wout_d[l][:, half * 512:(half + 1) * 512], 512)
                for nn in range(4):
                    n = half * 4 + nn
                    b = bJ[n % 2]
                    for k in range(8):
                        P.op("pe", lambda e, k=k, nn=nn, b=b, slot=slot: e.matmul(
                            bk(b)[:, :], lhsT=W["w"][:, slot, k, nn * 128:(nn + 1) * 128], rhs=W["oT"][:, k, :],
                            start=(k == 0), stop=(k == 7)), reads=[("w", slot), "oT"], writes=[b])
                    P.op("dve", lambda e, n=n, b=b: e.tensor_tensor(out=xT[:, n, cs], in0=bk(b)[:, :], in1=xT[:, n, cs], op=ALU.add),
                         reads=[b, ("xT", c)], writes=[("xT", c)])

        for s in range(nseq):
            for l in range(n_a):
                with ExitStack() as ph:
                    def tsb(name, shape, dt):
                        return ph.enter_context(nc.sbuf_tensor(f"t_{name}_a{s}{l}", list(shape), dt))
                    wstate["n"] = 0
                    wl = []
                    for c_ in range(NCH):
                        wl.append((awin_d[l][:, 0:512], 512))
                        for H_ in range(2):
                            wl.append((awin_d[l][:, 512 + H_ * 1024:1024 + H_ * 1024], 512))
                            wl.append((awin_d[l][:, 1024 + H_ * 1024:1536 + H_ * 1024], 512))
                        wl.append((awout_d[l][:, 0:512], 512))
                        wl.append((awout_d[l][:, 512:1024], 512))
                    wstate["list"] = wl
                    W = dict(
                        xs=tsb("xs", (128, 2, 1024), F32), sq=tsb("sq", (128, 2, 512), BF16), rs=tsb("rs", (128, 3, 512), F32),
                        hT=tsb("hT", (128, 2, 8, 512), BF16), w=tsb("w", (128, 2, 8, 512), BF16), sqh=tsb("sqh", (128, 2, 512), BF16),
                        QT=tsb("QT", (128, 4, 512), BF16), zs=tsb("zs", (128, 4, 512), BF16), PT=tsb("PT", (128, 4, 512), BF16),
                        bm=tsb("bm", (128, 2, 2, 6, 128), BF16), acc=tsb("acc", (128, 4, 1024), BF16),
                        oT=tsb("oT", (128, 8, 512), BF16), den=tsb("den", (128, 2, 4), F32), tt=tsb("tt", (128, 2, 4, 65), F32),
                    )
                    if l == 0:
                        load_x(P, W, s)
                    for c in range(NCH):
                        cs = slice(c * 512, (c + 1) * 512)
                        if c == 0:
                            norm_chunk(P, W, c, l)
                        W["hsel"] = c % 2
                        slot = load_w(P, W, awin_d[l][:, 0:512], 512)
                        proj_norm_seq(P, W, slot, [(blk * 128, (lambda blk=blk, cs=cs: KT0[:, blk, cs]), ("KT0", c), 2 + l, 1.0)
                                                   for blk in range(2)])
                        v_tm(P, W, slot, 256, V0, "V0", c)
                        for H in range(2):
                            slotq = load_w(P, W)
                            proj_norm_seq(P, W, slotq, [(jj * 128, (lambda jj=jj: W["QT"][:, jj, :]), ("QT", jj), l, 0.125)
                                                        for jj in range(4)])
                            slotz = load_w(P, W)
                            for r in range(4):
                                b = bJ[r % 2]
                                for k in range(8):
                                    P.op("pe", lambda e, k=k, r=r, b=b, slotz=slotz, hs=W["hsel"]: e.matmul(
                                        bk(b)[:, :], lhsT=W["hT"][:, hs, k, r * 128:(r + 1) * 128], rhs=W["w"][:, slotz, k, :],
                                        start=(k == 0), stop=(k == 7)), reads=[("w", slotz), ("hT", W["hsel"], k)], writes=[b])
                                P.op("act", lambda e, r=r, b=b: e.activation(out=W["zs"][:, r, :], in_=bk(b)[:, :], func=AF.Silu),
                                     reads=[b], writes=["zs"])
                            tiles = []
                            bmslots = {0: load_bm(P, W, 4 * H)}
                            for jj in range(4):
                                j = 4 * H + jj
                                ntiles0 = len(tiles)
                                tls = []
                                for m in range(2):
                                    h = HP[2 * j + m]
                                    g = h // 4
                                    Okey = bO[ctr["O"] % 2]
                                    ctr["O"] += 1
                                    dsl = ctr["O"] % 2
                                    if jj not in bmslots:
                                        bmslots[jj] = (bmslots[jj - 1] + 1) % 2
                                    bmslot = bmslots[jj]
                                    tl = attn_tiles(P, W, h, 64 * m, W["QT"][:, jj, :], ("QT", jj), KT0, "KT0", V0, "V0", g // 2, g, c,
                                                    list(range(max(0, 4 * c - 1), 4 * c + 4)),
                                                    lambda kt, c=c: (max(0, kt - 4 * c), min(3, kt - 4 * c + 1)),
                                                    lambda dl: {0: ["D0"], 1: ["D1", "NM4"]}[dl], bmslot, m, Okey)

                                    def post(Okey=Okey, h=h, dsl=dsl, H=H, ll=l):
                                        Ov = bk(Okey)[:, 0:260].rearrange("p (r e) -> p r e", r=4)
                                        P.op("dve", lambda e: e.tensor_copy(out=W["tt"][:, dsl, :, :], in_=Ov), reads=[Okey], writes=[("tt", dsl)])
                                        P.op("dve", lambda e: e.tensor_scalar(
                                            out=W["den"][:, dsl, :], in0=W["tt"][:, dsl, :, 64], scalar1=esink[:, ll, h:h + 1], scalar2=None,
                                            op0=ALU.add), reads=[("tt", dsl), "esink"], writes=[("den", dsl)])
                                        P.op("dve", lambda e: e.reciprocal(out=W["den"][:, dsl, :], in_=W["den"][:, dsl, :]),
                                             reads=[("den", dsl)], writes=[("den", dsl)])
                                        P.op("dve", lambda e: e.tensor_tensor(
                                            out=W["tt"][:, dsl, :, 0:64], in0=W["tt"][:, dsl, :, 0:64],
                                            in1=W["den"][:, dsl, :].unsqueeze(2).to_broadcast([128, 4, 64]), op=ALU.mult),
                                            reads=[("tt", dsl), ("den", dsl)], writes=[("tt", dsl)])
                                        hh = h - 8 * H
                                        P.op("pool", lambda e: e.tensor_tensor(
                                            out=W["acc"][:, :, h * 64:(h + 1) * 64], in0=W["tt"][:, dsl, :, 0:64],
                                            in1=W["zs"][:, :, hh * 64:(hh + 1) * 64], op=ALU.mult),
                                            reads=[("tt", dsl), "zs"], writes=["acc"])
                                    tl[-1].post = post
                                    tls.append(tl)
                                tiles.extend(interleave(tls[0], tls[1]))
                                if jj + 1 < 4:
                                    def pre(jn=jj + 1, H=H):
                                        load_bm(P, W, 4 * H + jn)
                                    tiles[ntiles0].pre = pre
                            if H == 1 and c + 1 < NCH:
                                def prenorm(cn=c + 1, ll=l):
                                    norm_chunk(P, W, cn, ll, bJ[0])
                                tiles[1].pre = prenorm
                            run_pipeline(P, tiles)
                        transpose_out(P, W, c, awout_d, l)
                    if dbg and l == n_a - 1 and not do_kv:
                        W["nxs"] = 2
                        store_x(P, W, s)
                    P.flush()
            if not do_kv:
                continue
            with ExitStack() as ph:
                def tsb(name, shape, dt):
                    return ph.enter_context(nc.sbuf_tensor(f"t_{name}_kv{s}", list(shape), dt))
                wstate["n"] = 0
                wstate["list"] = [(kvw_d[:, i * 512:(i + 1) * 512], 512) for _ in range(NCH) for i in range(3)]
                W = dict(
                    sq=tsb("sq", (128, 2, 512), BF16), rs=tsb("rs", (128, 3, 512), F32),
                    hT=tsb("hT", (128, 2, 8, 512), BF16), w=tsb("w", (128, 2, 8, 512), BF16), sqh=tsb("sqh", (128, 2, 512), BF16),
                    KC=[tsb("KC0", (128, 2, S), BF16), tsb("VC0", (128, 2, S), BF16)],
                    w1X=tsb("w1X", (128, 32, 256), BF16), w2d=tsb("w2d", (128, 2, 128), BF16),
                    H1s=tsb("H1s", (128, 2, 128), BF16), posb=tsb("posb", (128, 2), F32), posT=tsb("posT", (128, 32), BF16),
                )
                for c in range(NCH):
                    cs = slice(c * 512, (c + 1) * 512)
                    norm_chunk(P, W, c, 2)
                    W["hsel"] = c % 2
                    slot = load_w(P, W)
                    for kind in range(2):
                        for blk in range(2):
                            b = bJ[blk]
                            proj_fm(P, W, b, slot, kind * 256 + blk * 128)
                            P.op("act", lambda e, b=b, kind=kind, blk=blk, cs=cs: e.activation(
                                out=W["KC"][kind][:, blk, cs], in_=bk(b)[:, :], func=AF.Copy), reads=[b], writes=[("KC", kind)])
                    for br, (KTt, Kkey, Vt, Vkey, gcol) in enumerate([(KT0, "KT0", V0, "V0", 5), (KT1, "KT1", V1, "V1", 6)]):
                        slot = load_w(P, W)
                        proj_norm_seq(P, W, slot, [(blk * 128, (lambda blk=blk, cs=cs, KTt=KTt: KTt[:, blk, cs]), (Kkey, c), gcol, 1.0)
                                                   for blk in range(2)])
                        v_tm(P, W, slot, 256, Vt, Vkey, c)
                for kind, (w1_d, w2_d, pos_d) in enumerate([(ckw1_d, ckw2_d, posk_d), (cvw1_d, cvw2_d, posv_d)]):
                    SRC = W["KC"][kind]
                    for hf in range(2):
                        P.op("pool", lambda e, hf=hf, w1_d=w1_d: e.dma_start(
                            out=W["w1X"][hf * 64:(hf + 1) * 64, :, :], in_=w1_d.rearrange("(t d) n -> d t n", d=64)),
                            writes=["w1X"], dma="w1X")
                        P.op("pool", lambda e, hf=hf, w2_d=w2_d: e.dma_start(
                            out=W["w2d"][:, :, hf * 64:(hf + 1) * 64], in_=w2_d.rearrange("(hb p) n -> p hb n", p=128)),
                            writes=["w2d"], dma="w2d")
                    P.op("pool", lambda e, pos_d=pos_d: e.dma_start(out=W["posT"][:, :], in_=pos_d), writes=["posT"], dma="posT")
                    for hb in range(2):
                        for t in range(32):
                            P.op("pe", lambda e, hb=hb, t=t: e.matmul(
                                bk(bM)[:, hb:hb + 1], lhsT=W["w1X"][0:64, t, hb * 128:(hb + 1) * 128], rhs=W["posT"][0:64, t:t + 1],
                                start=(t == 0), stop=(t == 31)), reads=["w1X", "posT"], writes=[bM])
                    P.op("dve", lambda e: e.tensor_copy(out=W["posb"][:, :], in_=bk(bM)[:, 0:2]), reads=[bM], writes=["posb"])
                    for g in range(4):
                        pb = (g % 2) * 64
                        blk = g // 2
                        for hb in range(2):
                            b = bJ[hb]
                            for t in range(32):
                                P.op("pe", lambda e, hb=hb, t=t, b=b, pb=pb, blk=blk, SRC=SRC: e.matmul(
                                    bk(b)[:, 0:127], lhsT=W["w1X"][pb:pb + 64, t, hb * 128:(hb + 1) * 128],
                                    rhs=SRC[pb:pb + 64, blk, t:t + 2017:16], start=(t == 0), stop=(t == 31)),
                                    reads=["w1X", ("KC", kind)], writes=[b])
                            P.op("act", lambda e, hb=hb, b=b: e.activation(out=W["H1s"][:, hb, 0:127], in_=bk(b)[:, 0:127], func=AF.Silu,
                                                                           bias=W["posb"][:, hb:hb + 1]),
                                 reads=[b, "posb"], writes=["H1s"])
                        sk = bS[0]
                        if kind == 0:
                            for hb in range(2):
                                P.op("pe", lambda e, hb=hb, sk=sk: e.matmul(bk(sk)[:, 0:127], lhsT=W["w2d"][:, hb, :], rhs=W["H1s"][:, hb, 0:127],
                                                                          start=(hb == 0), stop=(hb == 1)),
                                     reads=["w2d", "H1s"], writes=[sk])
                            P.op("act", lambda e, sk=sk: e.activation(out=W["sqh"][:, 0, 0:127], in_=bk(sk)[:, 0:127], func=AF.Square),
                                 reads=[sk], writes=[("sqh", 0)])
                            P.op("pe", lambda e: e.matmul(bk(bM)[:, 0:127], lhsT=bonesb[:, :], rhs=W["sqh"][:, 0, 0:127], start=True, stop=True),
                                 reads=[("sqh", 0), "bonesb"], writes=[bM])
                            P.op("act", lambda e: e.activation(out=W["rs"][:, 1, 0:127], in_=bk(bM)[:, 0:127], func=AF.Ln, bias=EPS,
                                                               scale=1.0 / 64), reads=[bM], writes=[("rs", 1)])
                            P.op("act", lambda e: e.activation(out=W["rs"][:, 1, 0:127], in_=W["rs"][:, 1, 0:127], func=AF.Exp, scale=-0.5),
                                 reads=[("rs", 1)], writes=[("rs", 1)])
                            P.op("dve", lambda e, sk=sk, pb=pb, blk=blk: e.scalar_tensor_tensor(
                                out=KcT[pb:pb + 64, blk, 0:127], in0=bk(sk)[pb:pb + 64, 0:127], scalar=hg[pb:pb + 64, 4:5],
                                in1=W["rs"][pb:pb + 64, 1, 0:127], op0=ALU.mult, op1=ALU.mult),
                                reads=[sk, ("rs", 1), "hg"], writes=["KcT"])
                        else:
                            for hb in range(2):
                                P.op("pe", lambda e, hb=hb, sk=sk: e.matmul(bk(sk)[0:127, 0:64], lhsT=W["H1s"][:, hb, 0:127],
                                                                          rhs=W["w2d"][:, hb, 0:64], start=(hb == 0), stop=(hb == 1)),
                                     reads=["w2d", "H1s"], writes=[sk])
                            P.op("act", lambda e, sk=sk, g=g: e.activation(out=Vc[0:127, g, 0:64], in_=bk(sk)[0:127, 0:64], func=AF.Copy),
                                 reads=[sk], writes=["Vc"])
                P.flush()

            for l in range(n_b):
                with ExitStack() as ph:
                    def tsb(name, shape, dt):
                        return ph.enter_context(nc.sbuf_tensor(f"t_{name}_b{s}{l}", list(shape), dt))
                    wstate["n"] = 0
                    wl = []
                    for c_ in range(NCH):
                        for H_ in range(2):
                            wl.append((bwin_d[l][:, H_ * 2048:H_ * 2048 + 512], 512))
                            for jj_ in range(4):
                                o_ = H_ * 2048 + 512 + jj_ * 384
                                wl.append((bwin_d[l][:, o_:o_ + 384], 384))
                        wl.append((bwout_d[l][:, 0:512], 512))
                        wl.append((bwout_d[l][:, 512:1024], 512))
                    wstate["list"] = wl
                    W = dict(
                        sq=tsb("sq", (128, 2, 512), BF16), rs=tsb("rs", (128, 3, 512), F32),
                        hT=tsb("hT", (128, 2, 8, 512), BF16), w=tsb("w", (128, 2, 8, 512), BF16), sqh=tsb("sqh", (128, 2, 512), BF16),
                        QT=tsb("QT", (128, 4, 512), BF16), zs=tsb("zs", (128, 2, 4, 384), BF16), PT=tsb("PT", (128, 4, 512), BF16),
                        bm=tsb("bm", (128, 2, 2, 6, 128), BF16), acc=tsb("acc", (128, 4, 1024), BF16),
                        oT=tsb("oT", (128, 8, 512), BF16), den=tsb("den", (128, 4, 4), F32), den2=tsb("den2", (128, 4, 4), F32),
                        tt=tsb("tt", (128, 4, 4, 65), F32), U=tsb("U", (128, 2, 3, 4, 64), F32),
                        Wg=tsb("Wg", (128, 8, 48), BF16), sig=tsb("sig", (128, 4, 48), F32), ogc=tsb("ogc", (128, 8, 4, 64), F32),
                        impg=tsb("impg", (128, 2, 4, 32), F32), imt=tsb("imt", (128, 4, 32), F32), score=tsb("score", (128, 4, 32), F32),
                        top8=tsb("top8", (128, 8), F32), negsel=tsb("negsel", (128, 4, 32), BF16),
                        negselT=tsb("negselT", (64, 512), BF16),
                    )
                    W["xs"] = W["U"][:, :, :, :, :].rearrange("p a b c d -> p (a b c d)")[:, 0:1024].rearrange("p (s n) -> p s n", s=1)
                    P.op("pool", lambda e, l=l: e.dma_start(out=W["Wg"][:, :, :],
                                                            in_=bwin_d[l][:, 4096:4144].rearrange("(k p) n -> p k n", p=128)),
                         writes=["Wg"], dma="Wg")
                    for c in range(NCH):
                        if c == 0:
                            norm_chunk(P, W, c, 3 + l)
                        W["hsel"] = c % 2
                        for r in range(4):
                            for k in range(8):
                                P.op("pe", lambda e, r=r, k=k, hs=W["hsel"]: e.matmul(bk(bM)[:, r * 48:(r + 1) * 48],
                                                                        lhsT=W["hT"][:, hs, k, r * 128:(r + 1) * 128],
                                                                        rhs=W["Wg"][:, k, :], start=(k == 0), stop=(k == 7), skip_group_check=True),
                                     reads=[("hT", W["hsel"], k), "Wg"], writes=[bM])
                        P.op("act", lambda e: e.activation(out=W["sig"][:, :, :], in_=bk(bM)[:, 0:192].rearrange("p (r n) -> p r n", r=4),
                                                           func=AF.Sigmoid), reads=[bM], writes=["sig"])
                        for H in range(2):
                            slotq = load_w(P, W)
                            proj_norm_seq(P, W, slotq, [(jj * 128, (lambda jj=jj: W["QT"][:, jj, :]), ("QT", jj), 7 + l, 0.125)
                                                        for jj in range(4)])
                            seen = set()
                            tiles = []
                            bmslots = {0: load_bm(P, W, 4 * H)}
                            for jj in range(4):
                                j = 4 * H + jj
                                ntiles0 = len(tiles)
                                if jj not in bmslots:
                                    bmslots[jj] = (bmslots[jj - 1] + 1) % 2
                                bmslot = bmslots[jj]
                                tls = []
                                for m in range(2):
                                    h = HP[2 * j + m]
                                    g = h // 4
                                    gl = g - 2 * H
                                    hl = jj * 2 + m
                                    pb = 64 * m
                                    Okey = bO[ctr["O"] % 2]
                                    ctr["O"] += 1
                                    dsl = ctr["O"] % 2
                                    Ikey = bJ[hl % 2]
                                    tiles_h = []
                                    for r in range(4):
                                        qt = 4 * c + r
                                        ncq = 8 * qt + 8
                                        off = 120 - 8 * qt
                                        T = Tile()

                                        def A0(sk, ncq=ncq, off=off, pb=pb, g=g, jj=jj, r=r, bmslot=bmslot, m=m):
                                            P.op("pe", lambda e: e.matmul(
                                                bk(sk)[0:ncq, 0:128], lhsT=KcT[pb:pb + 64, g // 2, 0:ncq],
                                                rhs=W["QT"][pb:pb + 64, jj, r * 128:(r + 1) * 128],
                                                start=True, stop=False, skip_group_check=True), reads=["KcT", ("QT", jj)], writes=[sk])

                                        def A1(sk, ncq=ncq, off=off, pb=pb, g=g, jj=jj, r=r, bmslot=bmslot, m=m):
                                            for hl2 in range(2):
                                                P.op("pe", lambda e, hl2=hl2: e.matmul(
                                                    bk(sk)[0:ncq, 0:128], lhsT=identb[:, off:off + ncq], rhs=W["bm"][:, bmslot, m, 4 + hl2, :],
                                                    start=False, stop=(hl2 == 1), skip_group_check=True),
                                                    reads=["identb", ("bm", bmslot)], writes=[sk])

                                        def B(sk, pt, ncq=ncq, h=h):
                                            P.op("act", lambda e: e.activation(
                                                out=W["PT"][0:ncq, pt, 0:128], in_=bk(sk)[0:ncq, 0:128], func=AF.Exp),
                                                reads=[sk], writes=[("PT", pt)])

                                        def C(pt, ncq=ncq, Okey=Okey, Ikey=Ikey, g=g, r=r):
                                            P.op("pe", lambda e: e.matmul(
                                                bk(Okey)[:, r * 65:(r + 1) * 65], lhsT=W["PT"][0:ncq, pt, 0:128], rhs=Vc[0:ncq, g, :],
                                                start=(r == 0), stop=False, skip_group_check=True), reads=[("PT", pt), "Vc"], writes=[Okey])
                                            P.op("pe", lambda e: e.matmul(
                                                bk(Ikey)[:, r * 33:(r + 1) * 33], lhsT=W["PT"][0:ncq, pt, 0:128], rhs=ovxb[0:ncq, :],
                                                start=(r == 0), stop=False, skip_group_check=True), reads=[("PT", pt), "ovxb"], writes=[Ikey])
                                        T.A0, T.A1, T.B, T.C = A0, A1, B, C
                                        tiles_h.append(T)
                                    first_g = gl not in seen
                                    seen.add(gl)

                                    def post(Okey=Okey, Ikey=Ikey, dsl=dsl, first_g=first_g, gl=gl, h=h, hl=hl):
                                        Ov = bk(Okey)[:, 0:260].rearrange("p (r e) -> p r e", r=4)
                                        Iv = bk(Ikey)[:, 0:132].rearrange("p (r e) -> p r e", r=4)
                                        P.op("dve", lambda e: e.tensor_copy(out=W["tt"][:, dsl, :, :], in_=Ov), reads=[Okey], writes=[("tt", dsl)])
                                        P.op("dve", lambda e: e.tensor_scalar(
                                            out=W["den"][:, dsl, :], in0=W["tt"][:, dsl, :, 64], scalar1=1e-30, scalar2=None, op0=ALU.max),
                                            reads=[("tt", dsl)], writes=[("den", dsl)])
                                        P.op("dve", lambda e: e.reciprocal(out=W["den"][:, dsl, :], in_=W["den"][:, dsl, :]),
                                             reads=[("den", dsl)], writes=[("den", dsl)])
                                        dst = W["impg"][:, gl, :, :] if first_g else W["imt"][:, :, :]
                                        P.op("dve", lambda e: e.tensor_tensor(
                                            out=dst, in0=Iv[:, :, 0:32], in1=W["den"][:, dsl, :].unsqueeze(2).to_broadcast([128, 4, 32]), op=ALU.mult),
                                            reads=[Ikey, ("den", dsl)], writes=[("impg", gl) if first_g else "imt"])
                                        if not first_g:
                                            P.op("dve", lambda e: e.tensor_tensor(out=W["impg"][:, gl, :, :], in0=W["impg"][:, gl, :, :],
                                                                                  in1=W["imt"][:, :, :], op=ALU.add),
                                                 reads=["imt", ("impg", gl)], writes=[("impg", gl)])
                                        P.op("dve", lambda e: e.tensor_tensor(
                                            out=W["den2"][:, dsl, :], in0=W["den"][:, dsl, :], in1=W["sig"][:, :, h], op=ALU.mult),
                                            reads=[("den", dsl), "sig"], writes=[("den2", dsl)])
                                        P.op("dve", lambda e: e.tensor_tensor(
                                            out=W["ogc"][:, hl, :, :], in0=W["tt"][:, dsl, :, 0:64],
                                            in1=W["den2"][:, dsl, :].unsqueeze(2).to_broadcast([128, 4, 64]), op=ALU.mult),
                                            reads=[("tt", dsl), ("den2", dsl)], writes=[("ogc", hl)])
                                    tiles_h[-1].post = post
                                    tls.append(tiles_h)
                                tiles.extend(interleave(tls[0], tls[1]))
                                if jj + 1 < 4:
                                    def pre(jn=jj + 1, H=H):
                                        load_bm(P, W, 4 * H + jn)
                                    tiles[ntiles0].pre = pre
                            run_pipeline(P, tiles, bS3)
                            def emit_z(jj, H=H):
                                slotz = load_w(P, W)
                                zsl = jj % 2
                                for r in range(4):
                                    b = bJ[r % 2]
                                    for k in range(8):
                                        P.op("pe", lambda e, k=k, r=r, b=b, slotz=slotz, hs=W["hsel"]: e.matmul(
                                            bk(b)[:, 0:384], lhsT=W["hT"][:, hs, k, r * 128:(r + 1) * 128], rhs=W["w"][:, slotz, k, 0:384],
                                            start=(k == 0), stop=(k == 7)), reads=[("w", slotz), ("hT", W["hsel"], k)], writes=[b])
                                    P.op("act", lambda e, r=r, b=b, zsl=zsl: e.activation(out=W["zs"][:, zsl, r, :], in_=bk(b)[:, 0:384], func=AF.Silu),
                                         reads=[b], writes=[("zs", zsl)])
                            emit_z(0)
                            for gl in range(2):
                                P.op("dve", lambda e, gl=gl, c=c: e.tensor_tensor(out=W["score"][:, :, :], in0=W["impg"][:, gl, :, :],
                                                                                in1=MF[:, 4 * c:4 * c + 4, :], op=ALU.add),
                                     reads=[("impg", gl), "MF"], writes=["score"])
                                for r in range(4):
                                    P.op("dve", lambda e, r=r: e.max(out=W["top8"][:, :], in_=W["score"][:, r, :]), reads=["score"], writes=["top8"])
                                    P.op("dve", lambda e, r=r: e.tensor_scalar(out=W["negsel"][:, r, :], in0=W["score"][:, r, :],
                                                                               scalar1=W["top8"][:, 7:8], scalar2=NEGM, op0=ALU.is_lt, op1=ALU.mult),
                                         reads=["score", "top8"], writes=["negsel"])
                                for r in range(4):
                                    P.op("pe", lambda e, r=r, gl=gl: e.matmul(bk(bM)[32 * gl:32 * gl + 32, r * 128:(r + 1) * 128], lhsT=W["negsel"][:, r, :], rhs=identb[:, :],
                                                                       start=True, stop=True, skip_group_check=True),
                                         reads=["negsel", "identb"], writes=[bM])
                                P.op("act", lambda e, gl=gl: e.activation(out=W["negselT"][32 * gl:32 * gl + 32, :], in_=bk(bM)[32 * gl:32 * gl + 32, :], func=AF.Copy),
                                     reads=[bM], writes=[("negselT", gl)])
                            bmslots = {0: load_bm(P, W, 4 * H)}
                            alltiles = []
                            for jj in range(4):
                                j = 4 * H + jj
                                zsl = jj % 2
                                if jj not in bmslots:
                                    bmslots[jj] = (bmslots[jj - 1] + 1) % 2
                                bmslot = bmslots[jj]
                                tiles = []
                                tlmb = {}
                                for m in range(2):
                                    h = HP[2 * j + m]
                                    g = h // 4
                                    gl = g - 2 * H
                                    hl = jj * 2 + m
                                    for br in range(2):
                                        Okey = bO[m]
                                        ctr["O"] += 1
                                        dsl = ctr["O"] % 4
                                        if br == 0:
                                            def extra(P_, sk, cols, kt, last, gl=gl):
                                                P_.op("pe", lambda e: e.matmul(
                                                    bk(sk)[:, cols], lhsT=indb[32 * gl:32 * gl + 32, kt * 128:(kt + 1) * 128], rhs=W["negselT"][32 * gl:32 * gl + 32, cols],
                                                    start=False, stop=last, skip_group_check=True),
                                                    reads=["indb", ("negselT", gl)], writes=[sk])
                                            tl = attn_tiles(P, W, h, 64 * m, W["QT"][:, jj, :], ("QT", jj), KT0, "KT0", V0, "V0", g // 2, g, c,
                                                            list(range(0, 4 * c + 4)), lambda kt, c=c: (max(0, kt - 4 * c), 3),
                                                            lambda dl: {0: ["D0"], 1: ["D1"]}.get(dl, []), bmslot, m, Okey, extra_fn=extra)
                                        else:
                                            tl = attn_tiles(P, W, h, 64 * m, W["QT"][:, jj, :], ("QT", jj), KT1, "KT1", V1, "V1", g // 2, g, c,
                                                            list(range(max(0, 4 * c - 4), 4 * c + 4)),
                                                            lambda kt, c=c: (max(0, kt - 4 * c), min(3, kt - 4 * c + 4)),
                                                            lambda dl: {0: ["D0"], 1: ["D1"], 4: ["NM4"]}.get(dl, []), bmslot, m, Okey)

                                        def post(Okey=Okey, dsl=dsl, h=h, br=br, m=m, zsl=zsl, hl=hl):
                                            Ov = bk(Okey)[:, 0:260].rearrange("p (r e) -> p r e", r=4)
                                            P.op("dve", lambda e: e.tensor_copy(out=W["tt"][:, dsl, :, :], in_=Ov), reads=[Okey], writes=[("tt", dsl)])
                                            P.op("dve", lambda e: e.reciprocal(out=W["den"][:, dsl, :], in_=W["tt"][:, dsl, :, 64]),
                                                 reads=[("tt", dsl)], writes=[("den", dsl)])
                                            P.op("dve", lambda e: e.tensor_tensor(
                                                out=W["den2"][:, dsl, :], in0=W["den"][:, dsl, :], in1=W["sig"][:, :, 16 * (br + 1) + h], op=ALU.mult),
                                                reads=[("den", dsl), "sig"], writes=[("den2", dsl)])
                                            P.op("dve", lambda e: e.tensor_tensor(
                                                out=W["tt"][:, dsl, :, 0:64], in0=W["tt"][:, dsl, :, 0:64],
                                                in1=W["den2"][:, dsl, :].unsqueeze(2).to_broadcast([128, 4, 64]), op=ALU.mult),
                                                reads=[("tt", dsl), ("den2", dsl)], writes=[("tt", dsl)])
                                            zo = m * 192 + (br + 1) * 64
                                            P.op("pool", lambda e: e.tensor_tensor(
                                                out=W["U"][:, m, br, :, :], in0=W["tt"][:, dsl, :, 0:64], in1=W["zs"][:, zsl, :, zo:zo + 64], op=ALU.mult),
                                                reads=[("tt", dsl), ("zs", zsl)], writes=[("U", m, br)])
                                            if br == 1:
                                                zo0 = m * 192
                                                P.op("pool", lambda e: e.tensor_tensor(
                                                    out=W["U"][:, m, 2, :, :], in0=W["ogc"][:, hl, :, :], in1=W["zs"][:, zsl, :, zo0:zo0 + 64], op=ALU.mult),
                                                    reads=[("ogc", hl), ("zs", zsl)], writes=[("U", m, 2)])
                                                P.op("pool", lambda e: e.tensor_tensor(out=W["U"][:, m, 0, :, :], in0=W["U"][:, m, 0, :, :],
                                                                                       in1=W["U"][:, m, 1, :, :], op=ALU.add),
                                                     reads=[("U", m, 0), ("U", m, 1)], writes=[("U", m, 0)])
                                                P.op("pool", lambda e: e.tensor_tensor(out=W["acc"][:, :, h * 64:(h + 1) * 64], in0=W["U"][:, m, 0, :, :],
                                                                                       in1=W["U"][:, m, 2, :, :], op=ALU.add),
                                                     reads=[("U", m, 0), ("U", m, 2)], writes=["acc"])
                                        tl[-1].post = post
                                        tlmb[(m, br)] = tl
                                tiles = interleave(tlmb[(0, 0)], tlmb[(1, 0)]) + interleave(tlmb[(0, 1)], tlmb[(1, 1)])
                                if jj + 1 < 4:
                                    def pre1(jn=jj + 1, H=H):
                                        load_bm(P, W, 4 * H + jn)
                                    tiles[0].pre = pre1

                                    def pre2(jn=jj + 1):
                                        emit_z(jn)
                                    tiles[len(tiles) // 2].pre = pre2
                                alltiles.extend(tiles)
                            if H == 1 and c + 1 < NCH:
                                def prenorm(cn=c + 1, ll=l):
                                    norm_chunk(P, W, cn, 3 + ll, bJ[0])
                                alltiles[1].pre = prenorm
                            run_pipeline(P, alltiles, bS4)
                        transpose_out(P, W, c, bwout_d, l)
                    if l == n_b - 1:
                        W["nxs"] = 1
                        store_x(P, W, s)
                    P.flush()
    return nc


_CACHE = {}


def kernel(**inputs):
    sh = _prep_shared(inputs)
    x = np.asarray(inputs["x"], np.float32)
    nc = build()
    in_maps = []
    for i in range(8):
        m = dict(sh)
        m["x"] = np.ascontiguousarray(x[2 * i:2 * i + 2])
        in_maps.append(m)
    res = run_bass_kernel_spmd(nc, in_maps, core_ids=list(range(8)))
    return np.concatenate([r["y"] for r in res.results], axis=0).astype(np.float32)
```

```python
import math
from contextlib import ExitStack
import numpy as np
import concourse.bass as bass
import concourse.mybir as mybir
from concourse.bass_utils import run_bass_kernel_spmd

F32 = mybir.dt.float32
BF16 = mybir.dt.bfloat16
AF = mybir.ActivationFunctionType
ALU = mybir.AluOpType

S = 2048
D = 1024
NCH = 4
EPS = 1e-6
NEGM = -30000.0
NOPREFETCH = False
NBIAS = 2
HP = [0, 4, 1, 5, 2, 6, 3, 7, 8, 12, 9, 13, 10, 14, 11, 15]


class _Op:
    __slots__ = ("eng", "fn", "deps", "signal", "tok", "isdma", "idx")


class Prog:
    ENGS = ("pe", "act", "dve", "pool", "sp")

    def __init__(self, nc, es):
        self.nc, self.es = nc, es
        self.sems = {("eng", e): es.enter_context(nc.semaphore(f"sem_e_{e}")) for e in self.ENGS}
        self.bar = es.enter_context(nc.semaphore("sem_bar"))
        self.eng_cnt = {e: 0 for e in self.ENGS}
        self.dma_cnt = {}
        self.nflush = 0
        self._reset()

    def _reset(self):
        self.ops = {e: [] for e in self.ENGS}
        self.last_w = {}
        self.readers = {}
        self.n = 0

    def op(self, eng, fn, reads=(), writes=(), dma=None):
        o = _Op()
        o.eng, o.fn, o.signal, o.isdma = eng, fn, False, dma is not None
        o.idx = self.n
        self.n += 1
        deps = []
        for b in reads:
            deps.extend(self.last_w.get(b, {}).values())
        for b in writes:
            deps.extend(self.last_w.get(b, {}).values())
            deps.extend(self.readers.get(b, {}).values())
        best = {}
        for d in deps:
            if d.isdma:
                k = ("dma", d.tok[0])
            else:
                if d.eng == eng and eng == "pe":
                    continue
                k = ("eng", d.eng)
            if k not in best or best[k].idx < d.idx:
                best[k] = d
        o.deps = list(best.values())
        for d in o.deps:
            d.signal = True
        wk = ("dma", dma) if o.isdma else ("eng", eng)
        for b in reads:
            self.readers.setdefault(b, {})[wk] = o
        for b in writes:
            self.last_w.setdefault(b, {})[wk] = o
            self.readers[b] = {}
        if o.isdma:
            if dma not in self.dma_cnt:
                self.dma_cnt[dma] = 0
                self.sems[dma] = self.es.enter_context(self.nc.semaphore(f"sem_d{len(self.dma_cnt)}"))
            self.dma_cnt[dma] += 16
            o.tok = (dma, self.dma_cnt[dma])
        self.ops[eng].append(o)
        return o

    def flush(self):
        nc, sems = self.nc, self.sems
        for e in self.ENGS:
            c = self.eng_cnt[e]
            for o in self.ops[e]:
                if not o.isdma:
                    if o.signal:
                        c += 1
                    o.tok = (("eng", e), c)
            self.eng_cnt[e] = c
        ops = self.ops
        self.nflush += 1
        bar_target = 5 * self.nflush
        bar = self.bar
        dma_tot = dict(self.dma_cnt)

        def run(engname):
            def body(eng):
                waited = {}
                mydma = set()
                for o in ops[engname]:
                    for d in o.deps:
                        key, val = d.tok
                        if waited.get(key, 0) < val:
                            eng.wait_ge(sems[key], val)
                            waited[key] = val
                    if o.fn is None:
                        continue
                    ins = o.fn(eng)
                    if o.isdma:
                        ins.then_inc(sems[o.tok[0]], 16)
                        mydma.add(o.tok[0])
                    elif o.signal:
                        ins.then_inc(sems[o.tok[0]], 1)
                eng.drain()
                for k in mydma:
                    eng.wait_ge(sems[k], dma_tot[k])
                eng.sem_inc(bar, 1)
                eng.wait_ge(bar, bar_target)
            return body

        with nc.Block() as blk:
            blk.tensor(run("pe"))
            blk.scalar(run("act"))
            blk.vector(run("dve"))
            blk.gpsimd(run("pool"))
            blk.sync(run("sp"))
        self._reset()


def _t5_bucket(d):
    d = np.maximum(d, 0)
    large = 16 + (np.log(np.maximum(d, 1).astype(np.float32) / np.float32(16)) / np.float32(math.log(8.0))
                  * np.float32(16)).astype(np.int32)
    large = np.minimum(large, 31)
    return np.where(d < 16, d, large)


def _prep_shared(inp):
    f = np.float32
    tab = np.asarray(inp["rel_table"], f)
    k = np.arange(128)[:, None]
    q = np.arange(128)[None, :]
    d0 = q - k
    d1 = 128 + q - k
    dc = q - 16 * k + 1889
    G = np.empty((128, 3, 16, 128), f)
    G[:, 0] = np.transpose(tab[_t5_bucket(d0)], (0, 2, 1))
    G[:, 1] = np.transpose(tab[_t5_bucket(d1)], (0, 2, 1))
    G[:, 2] = np.transpose(tab[_t5_bucket(dc)], (0, 2, 1))
    NM = np.zeros((128, 3, 128), f)
    NM[:, 0] = np.where(d0 >= 0, 0.0, NEGM)
    NM[:, 2] = np.where(dc >= 0, 0.0, NEGM)
    NM4 = np.where(q < k, 0.0, NEGM).astype(f)
    ident = np.eye(128, dtype=f)
    bones = np.kron(np.eye(2, dtype=f), np.ones((64, 64), f))
    ind = np.zeros((32, 2048), f)
    ind[np.arange(2048) // 64, np.arange(2048)] = 1.0
    pos = (np.arange(16)[None, :, None] * 128 + np.arange(128)[:, None, None])
    j = np.arange(32)[None, None, :]
    pb = pos // 64
    causal = j <= pb
    forced = (j == 0) | ((pb - j >= 0) & (pb - j < 2))
    MF = np.where(causal, np.where(forced, 1e6, 0.0), -1e30).astype(f)
    cs = np.arange(128)[:, None] * 16
    ss = np.arange(32)[None, :] * 64
    ov = ((cs < ss + 64) & (cs + 32 > ss)).astype(f)
    ov[127] = 0.0
    ovx = np.concatenate([ov, np.ones((128, 1), f)], axis=1)
    qcols = np.concatenate([np.arange(h * 64, (h + 1) * 64) for h in HP])
    acols = np.concatenate([np.arange(1024, 1536), qcols[:512], 1536 + np.arange(0, 512),
                            qcols[512:], 1536 + np.arange(512, 1024)])
    a_w_in = np.ascontiguousarray(np.asarray(inp["a_w_in"], f)[:, :, acols])
    bcols = []
    for H in range(2):
        bcols.append(qcols[H * 512:(H + 1) * 512])
        for jj in range(4):
            for m in range(2):
                h = HP[2 * (4 * H + jj) + m]
                for c in range(3):
                    bcols.append(1072 + c * 1024 + h * 64 + np.arange(64))
    bcols.append(1024 + np.arange(48))
    bcols = np.concatenate(bcols)
    b_w_in = np.ascontiguousarray(np.asarray(inp["b_w_in"], f)[:, :, bcols])

    def colmajor(v):
        return np.ascontiguousarray(np.asarray(v, f).reshape(8, 128).T)
    ng = np.stack([colmajor(inp["a_norm"][0]), colmajor(inp["a_norm"][1]), colmajor(inp["kv_norm"]),
                   colmajor(inp["b_norm"][0]), colmajor(inp["b_norm"][1])], axis=1)
    hgl = [inp["a_q_gain"][0], inp["a_q_gain"][1], inp["a_k_gain"][0], inp["a_k_gain"][1],
           inp["kv_k_gain"][0], inp["kv_k_gain"][1], inp["kv_k_gain"][2], inp["b_q_gain"][0], inp["b_q_gain"][1]]
    hg = np.stack([np.tile(np.asarray(v, f), 2) for v in hgl], axis=1)
    sinkrep = np.ascontiguousarray(np.broadcast_to(np.asarray(inp["a_sink"], f)[None], (128, 2, 16)))
    b31rep = np.ascontiguousarray(np.broadcast_to(tab[31][None], (128, 16)))
    posk = np.ascontiguousarray(np.tile(np.asarray(inp["cmp_k_pos"], f).T, (2, 1)))
    posv = np.ascontiguousarray(np.tile(np.asarray(inp["cmp_v_pos"], f).T, (2, 1)))
    return dict(
        G=G, NM=NM, NM4=NM4, ident=ident, bones=bones, ind=ind, MF=MF, ovx=ovx,
        a_w_in=a_w_in, a_w_out=np.asarray(inp["a_w_out"], f), b_w_in=b_w_in,
        b_w_out=np.asarray(inp["b_w_out"], f), kv_w=np.asarray(inp["kv_w"], f),
        ng=np.ascontiguousarray(ng), hg=np.ascontiguousarray(hg), sinkrep=sinkrep, b31rep=b31rep,
        posk=posk, posv=posv,
        ck_w1=np.asarray(inp["cmp_k_w1"], f), ck_w2=np.asarray(inp["cmp_k_w2"], f),
        cv_w1=np.asarray(inp["cmp_v_w1"], f), cv_w2=np.asarray(inp["cmp_v_w2"], f),
    )


def build(nseq=2, n_a=2, do_kv=True, n_b=2, dbg=False):
    nc = bass.Bass("TRN2", target_bir_lowering=False)

    def din(name, shape):
        return nc.dram_tensor(name, list(shape), F32, kind="ExternalInput").ap()

    x_d = din("x", (nseq, S, D))
    G_d = din("G", (128, 3, 16, 128))
    NM_d = din("NM", (128, 3, 128))
    NM4_d = din("NM4", (128, 128))
    ident_d = din("ident", (128, 128))
    bones_d = din("bones", (128, 128))
    ind_d = din("ind", (32, 2048))
    MF_d = din("MF", (128, 16, 32))
    ovx_d = din("ovx", (128, 33))
    awin_d = din("a_w_in", (2, D, 2560))
    awout_d = din("a_w_out", (2, D, D))
    bwin_d = din("b_w_in", (2, D, 4144))
    bwout_d = din("b_w_out", (2, D, D))
    kvw_d = din("kv_w", (D, 1536))
    ng_d = din("ng", (128, 5, 8))
    hg_d = din("hg", (128, 9))
    sink_d = din("sinkrep", (128, 2, 16))
    b31_d = din("b31rep", (128, 16))
    posk_d = din("posk", (128, 32))
    posv_d = din("posv", (128, 32))
    ckw1_d = din("ck_w1", (2048, 256))
    ckw2_d = din("ck_w2", (256, 64))
    cvw1_d = din("cv_w1", (2048, 256))
    cvw2_d = din("cv_w2", (256, 64))
    y_d = nc.dram_tensor("y", [nseq, S, D], F32, kind="ExternalOutput").ap()
    BM_d = nc.dram_tensor("BM", [16, 128, 6, 128], BF16, kind="Internal").ap()

    es = ExitStack()
    with es:
        def sb(name, shape, dt):
            return es.enter_context(nc.sbuf_tensor("s_" + name, list(shape), dt))

        def ps(name, shape, dt):
            return es.enter_context(nc.psum_tensor(name, list(shape), dt))

        xT = sb("xT", (128, 8, S), F32)
        KT0 = sb("KT0", (128, 2, S), BF16)
        KT1 = sb("KT1", (128, 2, S), BF16)
        V0 = sb("V0", (128, 16, 4, 65), BF16)
        V1 = sb("V1", (128, 16, 4, 65), BF16)
        KcT = sb("KcT", (128, 2, 128), BF16)
        Vc = sb("Vc", (128, 4, 65), BF16)
        identb = sb("identb", (128, 128), BF16)
        identf = sb("identf", (128, 128), F32)
        bonesb = sb("bonesb", (128, 128), BF16)
        onesb = sb("onesb", (128, 128), BF16)
        indb = sb("indb", (64, 2048), BF16)
        NM4b = sb("NM4b", (128, 128), BF16)
        MF = sb("MF", (128, 16, 32), F32)
        ovxb = sb("ovxb", (128, 33), BF16)
        ng = sb("ng", (128, 5, 8), F32)
        hg = sb("hg", (128, 9), F32)
        esink = sb("esink", (128, 2, 16), F32)
        b31 = sb("b31", (128, 16), F32)
        banks = [ps(f"bk{i}", (128, 512), F32) for i in range(7)]
        bankT = ps("bkT", (128, 1024), BF16)
        bS = [("bk", 0), ("bk", 1)]
        bO = [("bk", 2), ("bk", 3)]
        bJ = [("bk", 4), ("bk", 5)]
        bM = ("bk", 6)

        bankTf = bankT.bitcast(F32)

        def bk(key):
            if key == "bkT":
                return bankTf
            return banks[key[1]]

        P = Prog(nc, es)

        with ExitStack() as ph:
            def tsb(name, shape, dt):
                return ph.enter_context(nc.sbuf_tensor("su_" + name, list(shape), dt))
            Gs = tsb("Gs", (128, 16, 128), F32)
            NMs = tsb("NMs", (128, 3, 128), F32)
            Dh = tsb("Dh", (128, 16, 128), BF16)
            Dl = tsb("Dl", (128, 16, 128), BF16)
            sinks = tsb("sinks", (128, 2, 16), F32)
            P.op("sp", lambda e: e.dma_start(out=identf[:], in_=ident_d), writes=["identf"], dma="c0")
            P.op("sp", lambda e: e.dma_start(out=MF[:], in_=MF_d), writes=["MF"], dma="c1")
            P.op("sp", lambda e: e.dma_start(out=ng[:], in_=ng_d), writes=["ng"], dma="c2")
            P.op("sp", lambda e: e.dma_start(out=hg[:], in_=hg_d), writes=["hg"], dma="c3")
            P.op("sp", lambda e: e.dma_start(out=sinks[:], in_=sink_d), writes=["sinks"], dma="c4")
            P.op("sp", lambda e: e.dma_start(out=b31[:], in_=b31_d), writes=["b31"], dma="c5")
            P.op("sp", lambda e: e.dma_start(out=NMs[:], in_=NM_d), writes=["NMs"], dma="c6")
            P.op("pool", lambda e: e.dma_start(out=identb[:], in_=ident_d), writes=["identb"], dma="p0")
            P.op("pool", lambda e: e.dma_start(out=bonesb[:], in_=bones_d), writes=["bonesb"], dma="p1")
            P.op("pool", lambda e: e.dma_start(out=indb[0:32, :], in_=ind_d), writes=["indb"], dma="p2")
            P.op("pool", lambda e: e.dma_start(out=indb[32:64, :], in_=ind_d), writes=["indb"], dma="p2")
            P.op("pool", lambda e: e.dma_start(out=NM4b[:], in_=NM4_d), writes=["NM4b"], dma="p3")
            P.op("pool", lambda e: e.dma_start(out=ovxb[:], in_=ovx_d), writes=["ovxb"], dma="p4")
            P.op("dve", lambda e: e.memset(onesb[:], 1.0), writes=["onesb"])
            P.op("dve", lambda e: e.memset(V0[:, :, :, 64:65], 1.0), writes=["V0o"])
            P.op("dve", lambda e: e.memset(V1[:, :, :, 64:65], 1.0), writes=["V1o"])
            P.op("dve", lambda e: e.memset(Vc[:, :, 0:64], 0.0), writes=["Vc"])
            P.op("dve", lambda e: e.memset(Vc[:, :, 64:65], 1.0), writes=["Vco"])
            P.op("dve", lambda e: e.memset(KcT[:], 0.0), writes=["KcT"])
            P.op("dve", lambda e: e.tensor_tensor(out=sinks[:], in0=sinks[:], in1=b31[:].unsqueeze(1).to_broadcast([128, 2, 16]),
                                                  op=ALU.subtract), reads=["sinks", "b31"], writes=["sinks"])
            P.op("act", lambda e: e.activation(out=esink[:], in_=sinks[:], func=AF.Exp), reads=["sinks"], writes=["esink"])
            for t in range(3):
                P.op("sp", lambda e, t=t: e.dma_start(out=Gs[:], in_=G_d[:, t]), writes=["Gs"], dma="g")
                P.op("dve", lambda e: e.tensor_tensor(out=Gs[:], in0=Gs[:], in1=b31[:].unsqueeze(2).to_broadcast([128, 16, 128]),
                                                      op=ALU.subtract), reads=["Gs", "b31"], writes=["Gs"])
                if t != 1:
                    P.op("dve", lambda e, t=t: e.tensor_tensor(out=Gs[:], in0=Gs[:],
                                                               in1=NMs[:, t, :].unsqueeze(1).to_broadcast([128, 16, 128]),
                                                               op=ALU.add), reads=["Gs", "NMs"], writes=["Gs"])
                P.op("dve", lambda e: e.tensor_copy(out=Dh[:], in_=Gs[:]), reads=["Gs"], writes=["Dh"])
                P.op("dve", lambda e: e.tensor_tensor(out=Dl[:], in0=Gs[:], in1=Dh[:], op=ALU.subtract),
                     reads=["Gs", "Dh"], writes=["Dl"])
                P.op("sp", lambda e, t=t: e.dma_start(out=BM_d[:, :, 2 * t, :].rearrange("h k q -> k h q"), in_=Dh[:]),
                     reads=["Dh"], writes=["BM"], dma="bmw")
                P.op("sp", lambda e, t=t: e.dma_start(out=BM_d[:, :, 2 * t + 1, :].rearrange("h k q -> k h q"), in_=Dl[:]),
                     reads=["Dl"], writes=["BM"], dma="bmw")
            P.op("sp", None, reads=["BM"])
            P.flush()

        def load_x(P, W, s):
            for t in range(16):
                slot = t % 2
                P.op("sp", lambda e, t=t, slot=slot: e.dma_start(out=W["xs"][:, slot, :], in_=x_d[s, t * 128:(t + 1) * 128, :]),
                     writes=[("xs", slot)], dma=("xs", slot))
                for half in range(2):
                    b = bJ[half]
                    for kk in range(4):
                        k = half * 4 + kk
                        P.op("pe", lambda e, b=b, kk=kk, k=k, slot=slot: e.transpose(
                            out=bk(b)[:, kk * 128:(kk + 1) * 128], in_=W["xs"][:, slot, k * 128:(k + 1) * 128], identity=identf[:]),
                            reads=[("xs", slot), "identf"], writes=[b])
                    eng = "dve" if half == 0 else "act"
                    if eng == "dve":
                        P.op("dve", lambda e, b=b, half=half, t=t: e.tensor_copy(
                            out=xT[:, half * 4:half * 4 + 4, t * 128:(t + 1) * 128],
                            in_=bk(b)[:, :].rearrange("p (k q) -> p k q", k=4)), reads=[b], writes=[("xT", t // 4)])
                    else:
                        P.op("act", lambda e, b=b, half=half, t=t: e.activation(
                            out=xT[:, half * 4:half * 4 + 4, t * 128:(t + 1) * 128],
                            in_=bk(b)[:, :].rearrange("p (k q) -> p k q", k=4), func=AF.Copy), reads=[b], writes=[("xT", t // 4)])

        def store_x(P, W, s):
            nxs = W["nxs"]
            for t in range(16):
                slot = t % nxs
                for half in range(2):
                    b = bJ[half]
                    for kk in range(4):
                        k = half * 4 + kk
                        P.op("pe", lambda e, b=b, kk=kk, k=k, t=t: e.transpose(
                            out=bk(b)[:, kk * 128:(kk + 1) * 128], in_=xT[:, k, t * 128:(t + 1) * 128], identity=identf[:]),
                            reads=[("xT", t // 4), "identf"], writes=[b])
                    if half == 0:
                        P.op("dve", lambda e, b=b, half=half, slot=slot: e.tensor_copy(
                            out=W["xs"][:, slot, half * 512:(half + 1) * 512], in_=bk(b)[:, :]), reads=[b], writes=[("xs", slot)])
                    else:
                        P.op("act", lambda e, b=b, half=half, slot=slot: e.activation(
                            out=W["xs"][:, slot, half * 512:(half + 1) * 512], in_=bk(b)[:, :], func=AF.Copy),
                            reads=[b], writes=[("xs", slot)])
                P.op("sp", lambda e, t=t, slot=slot: e.dma_start(out=y_d[s, t * 128:(t + 1) * 128, :], in_=W["xs"][:, slot, :]),
                     reads=[("xs", slot)], writes=[("y", t)], dma=("ys", slot))
            P.op("sp", None, reads=[("y", t) for t in range(16)])

        def norm_chunk(P, W, c, gi, sbank=None):
            sbank = sbank or bM
            hp = c % 2
            cs = slice(c * 512, (c + 1) * 512)
            for k in range(8):
                sl = k % 2
                P.op("act", lambda e, k=k, sl=sl: e.activation(out=W["sq"][:, sl, :], in_=xT[:, k, cs], func=AF.Square),
                     reads=[("xT", c)], writes=[("sq", sl)])
                P.op("pe", lambda e, k=k, sl=sl: e.matmul(bk(sbank)[:, :], lhsT=onesb[:, :], rhs=W["sq"][:, sl, :],
                                                          start=(k == 0), stop=(k == 7)),
                     reads=[("sq", sl), "onesb"], writes=[sbank])
            P.op("act", lambda e: e.activation(out=W["rs"][:, 0, :], in_=bk(sbank)[:, :], func=AF.Ln, bias=EPS, scale=1.0 / D),
                 reads=[sbank], writes=[("rs", 0)])
            P.op("act", lambda e: e.activation(out=W["rs"][:, 0, :], in_=W["rs"][:, 0, :], func=AF.Exp, scale=-0.5),
                 reads=[("rs", 0)], writes=[("rs", 0)])
            for k in range(8):
                P.op("dve", lambda e, k=k: e.scalar_tensor_tensor(out=W["hT"][:, hp, k, :], in0=xT[:, k, cs], scalar=ng[:, gi, k:k + 1],
                                                                  in1=W["rs"][:, 0, :], op0=ALU.mult, op1=ALU.mult),
                     reads=[("xT", c), ("rs", 0), "ng"], writes=[("hT", hp, k)])

        wstate = {"n": 0}

        def _issue_w(P, W, i):
            src_ap, ncols = wstate["list"][i]
            slot = i % 2
            P.op("pool", lambda e, slot=slot, src_ap=src_ap, ncols=ncols: e.dma_start(
                out=W["w"][:, slot, :, 0:ncols], in_=src_ap.rearrange("(k p) n -> p k n", p=128)),
                writes=[("w", slot)], dma=("w", slot))

        def load_w(P, W, src_ap=None, ncols=None):
            i = wstate["n"]
            wstate["n"] += 1
            if NOPREFETCH:
                _issue_w(P, W, i)
                return i % 2
            if i == 0:
                _issue_w(P, W, 0)
            if i + 1 < len(wstate["list"]):
                _issue_w(P, W, i + 1)
            return i % 2

        def proj_fm(P, W, b, slot, co, ncols=128):
            hs = W["hsel"]
            for k in range(8):
                P.op("pe", lambda e, k=k: e.matmul(bk(b)[0:ncols, :], lhsT=W["w"][:, slot, k, co:co + ncols], rhs=W["hT"][:, hs, k, :],
                                                   start=(k == 0), stop=(k == 7)),
                     reads=[("w", slot), ("hT", hs, k)], writes=[b])

        def headnorm(P, W, b, dest_fn, dest_key, gcol, scale, idx=0):
            sl = idx % 2
            stb = [bM, "bkT"][sl]
            P.op("act", lambda e: e.activation(out=W["sqh"][:, sl, :], in_=bk(b)[:, :], func=AF.Square), reads=[b], writes=[("sqh", sl)])
            P.op("pe", lambda e: e.matmul(bk(stb)[:, :], lhsT=bonesb[:, :], rhs=W["sqh"][:, sl, :], start=True, stop=True),
                 reads=[("sqh", sl), "bonesb"], writes=[stb])
            P.op("act", lambda e: e.activation(out=W["rs"][:, 1 + sl, :], in_=bk(stb)[:, :], func=AF.Ln, bias=EPS, scale=1.0 / 64),
                 reads=[stb], writes=[("rs", 1 + sl)])
            P.op("act", lambda e: e.activation(out=W["rs"][:, 1 + sl, :], in_=W["rs"][:, 1 + sl, :], func=AF.Exp, scale=-0.5,
                                               bias=math.log(scale)), reads=[("rs", 1 + sl)], writes=[("rs", 1 + sl)])
            P.op("dve", lambda e: e.scalar_tensor_tensor(out=dest_fn(), in0=bk(b)[:, :], scalar=hg[:, gcol:gcol + 1],
                                                         in1=W["rs"][:, 1 + sl, :], op0=ALU.mult, op1=ALU.mult),
                 reads=[b, ("rs", 1 + sl), "hg"], writes=[dest_key])

        def proj_norm_seq(P, W, slot, items):
            pbanks = [bJ[0], bJ[1], ("bk", 0), ("bk", 1)]
            n = len(items)
            for i in range(min(2, n)):
                proj_fm(P, W, pbanks[i % 4], slot, items[i][0])
            for i in range(n):
                if i + 2 < n:
                    proj_fm(P, W, pbanks[(i + 2) % 4], slot, items[i + 2][0])
                co, dfn, dkey, gcol, sc = items[i]
                headnorm(P, W, pbanks[i % 4], dfn, dkey, gcol, sc, idx=i)

        def v_tm(P, W, slot, co, Vt, Vkey, c):
            for r in range(4):
                b = bJ[r % 2]
                for k in range(8):
                    P.op("pe", lambda e, k=k, r=r, b=b, hs=W["hsel"]: e.matmul(bk(b)[:, 0:256], lhsT=W["hT"][:, hs, k, r * 128:(r + 1) * 128],
                                                                 rhs=W["w"][:, slot, k, co:co + 256], start=(k == 0), stop=(k == 7)),
                         reads=[("w", slot), ("hT", W["hsel"], k)], writes=[b])
                P.op("act", lambda e, r=r, b=b: e.activation(out=Vt[:, 4 * c + r, :, 0:64],
                                                             in_=bk(b)[:, 0:256].rearrange("p (g d) -> p g d", g=4), func=AF.Copy),
                     reads=[b], writes=[(Vkey, c)])

        bmstate = {"n": 0}

        def load_bm(P, W, j):
            slot = bmstate["n"] % 2
            bmstate["n"] += 1
            for m in range(2):
                h = HP[2 * j + m]
                P.op("sp", lambda e, m=m, h=h, slot=slot: e.dma_start(out=W["bm"][:, slot, m, :, :], in_=BM_d[h]),
                     reads=["BM"], writes=[("bm", slot)], dma=("bm", slot))
            return slot

        ctr = {"S": 0, "O": 0, "PT": 0}

        bS3 = [("bk", 0), ("bk", 1), ("bk", 6)]
        bS4 = [("bk", 0), ("bk", 1), ("bk", 6), "bkT"]

        class Tile:
            __slots__ = ("A0", "A1", "B", "C", "pre", "post", "mate")

            def __init__(self):
                self.pre = None
                self.post = None
                self.mate = False

        def interleave(l1, l2):
            out = []
            for i in range(max(len(l1), len(l2))):
                if i < len(l1):
                    out.append(l1[i])
                    l1[i].mate = i < len(l2)
                if i < len(l2):
                    out.append(l2[i])
            return out

        def run_pipeline(P, tiles, sb=None):
            sb = bS4
            groups = []
            i = 0
            while i < len(tiles):
                if tiles[i].mate and i + 1 < len(tiles):
                    groups.append([tiles[i], tiles[i + 1]])
                    i += 2
                else:
                    groups.append([tiles[i]])
                    i += 1
            prev = None
            for grp in groups + [None]:
                cur = None
                if grp is not None:
                    base = (ctr["S"] % 2) * 2
                    ctr["S"] += 1
                    cur = []
                    for ti, t in enumerate(grp):
                        if t.pre is not None:
                            t.pre()
                        cur.append((t, sb[base + ti], base + ti))
                    for t, sk, pt in cur:
                        t.A0(sk)
                    for t, sk, pt in cur:
                        t.A1(sk)
                    for t, sk, pt in cur:
                        t.B(sk, pt)
                if prev is not None:
                    for t, sk, pt in prev:
                        t.C(pt)
                        if t.post is not None:
                            t.post()
                prev = cur

        def attn_tiles(P, W, h, pb, qsrc, qkey, Kt, Kkey, Vt, Vkey, kb, g, c, kts, rng_fn, bias_fn, bmslot, m, Okey,
                       extra_fn=None):
            tiles = []
            for ti_, kt in enumerate(kts):
                r0, r1 = rng_fn(kt)
                cols = slice(r0 * 128, (r1 + 1) * 128)
                adds = []
                for r in range(r0, r1 + 1):
                    for nm in bias_fn(4 * c + r - kt):
                        adds.append((r, nm))
                if extra_fn is not None:
                    adds = adds + [("x", None)]
                first = (ti_ == 0)
                T = Tile()

                def A0(sk, kt=kt, cols=cols, adds=adds):
                    P.op("pe", lambda e, last=(len(adds) == 0): e.matmul(
                        bk(sk)[:, cols], lhsT=Kt[pb:pb + 64, kb, kt * 128:(kt + 1) * 128], rhs=qsrc[pb:pb + 64, cols],
                        start=True, stop=last, skip_group_check=True),
                        reads=[(Kkey, kt // 4), qkey], writes=[sk])

                def A1(sk, kt=kt, cols=cols, adds=adds):
                    for i, (r, nm) in enumerate(adds):
                        last = (i == len(adds) - 1)
                        if r == "x":
                            extra_fn(P, sk, cols, kt, last)
                            continue
                        if nm == "NM4":
                            P.op("pe", lambda e, r=r, last=last: e.matmul(
                                bk(sk)[:, r * 128:(r + 1) * 128], lhsT=identb[:, :], rhs=NM4b[:, :], start=False, stop=last,
                                skip_group_check=True), reads=["identb", "NM4b"], writes=[sk])
                        else:
                            ti = {"D0": 0, "D1": 1}[nm]
                            for hl in range(NBIAS):
                                P.op("pe", lambda e, r=r, ti=ti, hl=hl, last=last: e.matmul(
                                    bk(sk)[:, r * 128:(r + 1) * 128], lhsT=identb[:, :], rhs=W["bm"][:, bmslot, m, 2 * ti + hl, :],
                                    start=False, stop=(last and hl == NBIAS - 1), skip_group_check=True),
                                    reads=["identb", ("bm", bmslot)], writes=[sk])

                def B(sk, pt, cols=cols):
                    P.op("act", lambda e: e.activation(out=W["PT"][:, pt, cols], in_=bk(sk)[:, cols], func=AF.Exp),
                         reads=[sk], writes=[("PT", pt)])

                def C(pt, kt=kt, r0=r0, r1=r1, first=first):
                    for r in range(r0, r1 + 1):
                        P.op("pe", lambda e, r=r, st=(first and r == r0): e.matmul(
                            bk(Okey)[:, r * 65:(r + 1) * 65], lhsT=W["PT"][:, pt, r * 128:(r + 1) * 128], rhs=Vt[:, kt, g, :],
                            start=st, stop=False, skip_group_check=True),
                            reads=[("PT", pt), (Vkey, kt // 4), Vkey + "o"], writes=[Okey])
                T.A0, T.A1, T.B, T.C = A0, A1, B, C
                tiles.append(T)
            return tiles

        def transpose_out(P, W, c, wout_d, l):
            for r in range(4):
                for k in range(8):
                    P.op("pe", lambda e, r=r, k=k: e.transpose(out=bankT[:, k * 128:(k + 1) * 128],
                                                               in_=W["acc"][:, r, k * 128:(k + 1) * 128], identity=identb[:]),
                         reads=["acc", "identb"], writes=["bkT"])
                P.op("dve", lambda e, r=r: e.tensor_copy(out=W["oT"][:, :, r * 128:(r + 1) * 128],
                                                         in_=bankT[:, :].rearrange("p (k q) -> p k q", k=8)),
                     reads=["bkT"], writes=["oT"])
            cs = slice(c * 512, (c + 1) * 512)
            for half in range(2):
                slot = load_w(P, W, wout_d[l][:, half * 512:(half + 1) * 512], 512)
                for nn in range(4):
                    n = half * 4 + nn
                    b = bJ[n % 2]
                    for k in range(8):
                        P.op("pe", lambda e, k=k, nn=nn, b=b, slot=slot: e.matmul(
                            bk(b)[:, :], lhsT=W["w"][:, slot, k, nn * 128:(nn + 1) * 128], rhs=W["oT"][:, k, :],
                            start=(k == 0), stop=(k == 7)), reads=[("w", slot), "oT"], writes=[b])
                    P.op("dve", lambda e, n=n, b=b: e.tensor_tensor(out=xT[:, n, cs], in0=bk(b)[:, :], in1=xT[:, n, cs], op=ALU.add),
                         reads=[b, ("xT", c)], writes=[("xT", c)])

        for s in range(nseq):
            for l in range(n_a):
                with ExitStack() as ph:
                    def tsb(name, shape, dt):
                        return ph.enter_context(nc.sbuf_tensor(f"t_{name}_a{s}{l}", list(shape), dt))
                    wstate["n"] = 0
                    wl = []
                    for c_ in range(NCH):
                        wl.append((awin_d[l][:, 0:512], 512))
                        for H_ in range(2):
                            wl.append((awin_d[l][:, 512 + H_ * 1024:1024 + H_ * 1024], 512))
                            wl.append((awin_d[l][:, 1024 + H_ * 1024:1536 + H_ * 1024], 512))
                        wl.append((awout_d[l][:, 0:512], 512))
                        wl.append((awout_d[l][:, 512:1024], 512))
                    wstate["list"] = wl
                    W = dict(
                        xs=tsb("xs", (128, 2, 1024), F32), sq=tsb("sq", (128, 2, 512), BF16), rs=tsb("rs", (128, 3, 512), F32),
                        hT=tsb("hT", (128, 2, 8, 512), BF16), w=tsb("w", (128, 2, 8, 512), BF16), sqh=tsb("sqh", (128, 2, 512), BF16),
                        QT=tsb("QT", (128, 4, 512), BF16), zs=tsb("zs", (128, 4, 512), BF16), PT=tsb("PT", (128, 4, 512), BF16),
                        bm=tsb("bm", (128, 2, 2, 6, 128), BF16), acc=tsb("acc", (128, 4, 1024), BF16),
                        oT=tsb("oT", (128, 8, 512), BF16), den=tsb("den", (128, 2, 4), F32), tt=tsb("tt", (128, 2, 4, 65), F32),
                    )
                    if l == 0:
                        load_x(P, W, s)
                    for c in range(NCH):
                        cs = slice(c * 512, (c + 1) * 512)
                        if c == 0:
                            norm_chunk(P, W, c, l)
                        W["hsel"] = c % 2
                        slot = load_w(P, W, awin_d[l][:, 0:512], 512)
                        proj_norm_seq(P, W, slot, [(blk * 128, (lambda blk=blk, cs=cs: KT0[:, blk, cs]), ("KT0", c), 2 + l, 1.0)
                                                   for blk in range(2)])
                        v_tm(P, W, slot, 256, V0, "V0", c)
                        for H in range(2):
                            slotq = load_w(P, W)
                            proj_norm_seq(P, W, slotq, [(jj * 128, (lambda jj=jj: W["QT"][:, jj, :]), ("QT", jj), l, 0.125)
                                                        for jj in range(4)])
                            slotz = load_w(P, W)
                            for r in range(4):
                                b = bJ[r % 2]
                                for k in range(8):
                                    P.op("pe", lambda e, k=k, r=r, b=b, slotz=slotz, hs=W["hsel"]: e.matmul(
                                        bk(b)[:, :], lhsT=W["hT"][:, hs, k, r * 128:(r + 1) * 128], rhs=W["w"][:, slotz, k, :],
                                        start=(k == 0), stop=(k == 7)), reads=[("w", slotz), ("hT", W["hsel"], k)], writes=[b])
                                P.op("act", lambda e, r=r, b=b: e.activation(out=W["zs"][:, r, :], in_=bk(b)[:, :], func=AF.Silu),
                                     reads=[b], writes=["zs"])
                            tiles = []
                            bmslots = {0: load_bm(P, W, 4 * H)}
                            for jj in range(4):
                                j = 4 * H + jj
                                ntiles0 = len(tiles)
                                tls = []
                                for m in range(2):
                                    h = HP[2 * j + m]
                                    g = h // 4
                                    Okey = bO[ctr["O"] % 2]
                                    ctr["O"] += 1
                                    dsl = ctr["O"] % 2
                                    if jj not in bmslots:
                                        bmslots[jj] = (bmslots[jj - 1] + 1) % 2
                                    bmslot = bmslots[jj]
                                    tl = attn_tiles(P, W, h, 64 * m, W["QT"][:, jj, :], ("QT", jj), KT0, "KT0", V0, "V0", g // 2, g, c,
                                                    list(range(max(0, 4 * c - 1), 4 * c + 4)),
                                                    lambda kt, c=c: (max(0, kt - 4 * c), min(3, kt - 4 * c + 1)),
                                                    lambda dl: {0: ["D0"], 1: ["D1", "NM4"]}[dl], bmslot, m, Okey)

                                    def post(Okey=Okey, h=h, dsl=dsl, H=H, ll=l):
                                        Ov = bk(Okey)[:, 0:260].rearrange("p (r e) -> p r e", r=4)
                                        P.op("dve", lambda e: e.tensor_copy(out=W["tt"][:, dsl, :, :], in_=Ov), reads=[Okey], writes=[("tt", dsl)])
                                        P.op("dve", lambda e: e.tensor_scalar(
                                            out=W["den"][:, dsl, :], in0=W["tt"][:, dsl, :, 64], scalar1=esink[:, ll, h:h + 1], scalar2=None,
                                            op0=ALU.add), reads=[("tt", dsl), "esink"], writes=[("den", dsl)])
                                        P.op("dve", lambda e: e.reciprocal(out=W["den"][:, dsl, :], in_=W["den"][:, dsl, :]),
                                             reads=[("den", dsl)], writes=[("den", dsl)])
                                        P.op("dve", lambda e: e.tensor_tensor(
                                            out=W["tt"][:, dsl, :, 0:64], in0=W["tt"][:, dsl, :, 0:64],
                                            in1=W["den"][:, dsl, :].unsqueeze(2).to_broadcast([128, 4, 64]), op=ALU.mult),
                                            reads=[("tt", dsl), ("den", dsl)], writes=[("tt", dsl)])
                                        hh = h - 8 * H
                                        P.op("pool", lambda e: e.tensor_tensor(
                                            out=W["acc"][:, :, h * 64:(h + 1) * 64], in0=W["tt"][:, dsl, :, 0:64],
                                            in1=W["zs"][:, :, hh * 64:(hh + 1) * 64], op=ALU.mult),
                                            reads=[("tt", dsl), "zs"], writes=["acc"])
                                    tl[-1].post = post
                                    tls.append(tl)
                                tiles.extend(interleave(tls[0], tls[1]))
                                if jj + 1 < 4:
                                    def pre(jn=jj + 1, H=H):
                                        load_bm(P, W, 4 * H + jn)
                                    tiles[ntiles0].pre = pre
                            if H == 1 and c + 1 < NCH:
                                def prenorm(cn=c + 1, ll=l):
                                    norm_chunk(P, W, cn, ll, bJ[0])
                                tiles[1].pre = prenorm
                            run_pipeline(P, tiles)
                        transpose_out(P, W, c, awout_d, l)
                    if dbg and l == n_a - 1 and not do_kv:
                        W["nxs"] = 2
                        store_x(P, W, s)
                    P.flush()
            if not do_kv:
                continue
            with ExitStack() as ph:
                def tsb(name, shape, dt):
                    return ph.enter_context(nc.sbuf_tensor(f"t_{name}_kv{s}", list(shape), dt))
                wstate["n"] = 0
                wstate["list"] = [(kvw_d[:, i * 512:(i + 1) * 512], 512) for _ in range(NCH) for i in range(3)]
                W = dict(
                    sq=tsb("sq", (128, 2, 512), BF16), rs=tsb("rs", (128, 3, 512), F32),
                    hT=tsb("hT", (128, 2, 8, 512), BF16), w=tsb("w", (128, 2, 8, 512), BF16), sqh=tsb("sqh", (128, 2, 512), BF16),
                    KC=[tsb("KC0", (128, 2, S), BF16), tsb("VC0", (128, 2, S), BF16)],
                    w1X=tsb("w1X", (128, 32, 256), BF16), w2d=tsb("w2d", (128, 2, 128), BF16),
                    H1s=tsb("H1s", (128, 2, 2, 128), BF16), posb=tsb("posb", (128, 2), F32), posT=tsb("posT", (128, 32), BF16),
                )
                for c in range(NCH):
                    cs = slice(c * 512, (c + 1) * 512)
                    norm_chunk(P, W, c, 2)
                    W["hsel"] = c % 2
                    slot = load_w(P, W)
                    for kind in range(2):
                        for blk in range(2):
                            b = bJ[blk]
                            proj_fm(P, W, b, slot, kind * 256 + blk * 128)
                            P.op("act", lambda e, b=b, kind=kind, blk=blk, cs=cs: e.activation(
                                out=W["KC"][kind][:, blk, cs], in_=bk(b)[:, :], func=AF.Copy), reads=[b], writes=[("KC", kind)])
                    for br, (KTt, Kkey, Vt, Vkey, gcol) in enumerate([(KT0, "KT0", V0, "V0", 5), (KT1, "KT1", V1, "V1", 6)]):
                        slot = load_w(P, W)
                        proj_norm_seq(P, W, slot, [(blk * 128, (lambda blk=blk, cs=cs, KTt=KTt: KTt[:, blk, cs]), (Kkey, c), gcol, 1.0)
                                                   for blk in range(2)])
                        v_tm(P, W, slot, 256, Vt, Vkey, c)
                for kind, (w1_d, w2_d, pos_d) in enumerate([(ckw1_d, ckw2_d, posk_d), (cvw1_d, cvw2_d, posv_d)]):
                    SRC = W["KC"][kind]
                    for hf in range(2):
                        P.op("pool", lambda e, hf=hf, w1_d=w1_d: e.dma_start(
                            out=W["w1X"][hf * 64:(hf + 1) * 64, :, :], in_=w1_d.rearrange("(t d) n -> d t n", d=64)),
                            writes=["w1X"], dma="w1X")
                        P.op("pool", lambda e, hf=hf, w2_d=w2_d: e.dma_start(
                            out=W["w2d"][:, :, hf * 64:(hf + 1) * 64], in_=w2_d.rearrange("(hb p) n -> p hb n", p=128)),
                            writes=["w2d"], dma="w2d")
                    P.op("pool", lambda e, pos_d=pos_d: e.dma_start(out=W["posT"][:, :], in_=pos_d), writes=["posT"], dma="posT")
                    for hb in range(2):
                        for t in range(32):
                            P.op("pe", lambda e, hb=hb, t=t: e.matmul(
                                bk(bM)[:, hb:hb + 1], lhsT=W["w1X"][0:64, t, hb * 128:(hb + 1) * 128], rhs=W["posT"][0:64, t:t + 1],
                                start=(t == 0), stop=(t == 31)), reads=["w1X", "posT"], writes=[bM])
                    P.op("dve", lambda e: e.tensor_copy(out=W["posb"][:, :], in_=bk(bM)[:, 0:2]), reads=[bM], writes=["posb"])
                    for gp in (0, 2):
                        blk = gp // 2
                        fbanks = [[bJ[0], bJ[1]], [("bk", 0), ("bk", 1)]]
                        for hb in range(2):
                            for t in range(32):
                                for gi in range(2):
                                    pb = gi * 64
                                    b = fbanks[gi][hb]
                                    P.op("pe", lambda e, hb=hb, t=t, b=b, pb=pb, blk=blk, SRC=SRC: e.matmul(
                                        bk(b)[:, 0:127], lhsT=W["w1X"][pb:pb + 64, t, hb * 128:(hb + 1) * 128],
                                        rhs=SRC[pb:pb + 64, blk, t:t + 2017:16], start=(t == 0), stop=(t == 31)),
                                        reads=["w1X", ("KC", kind)], writes=[b])
                            for gi in range(2):
                                b = fbanks[gi][hb]
                                P.op("act", lambda e, hb=hb, b=b, gi=gi: e.activation(out=W["H1s"][:, gi, hb, 0:127], in_=bk(b)[:, 0:127],
                                                                                     func=AF.Silu, bias=W["posb"][:, hb:hb + 1]),
                                     reads=[b, "posb"], writes=[("H1s", gi)])
                        for gi in range(2):
                            g = gp + gi
                            pb = gi * 64
                            sk = "bkT"
                            if kind == 0:
                                for hb in range(2):
                                    P.op("pe", lambda e, hb=hb, sk=sk, gi=gi: e.matmul(bk(sk)[:, 0:127], lhsT=W["w2d"][:, hb, :], rhs=W["H1s"][:, gi, hb, 0:127],
                                                                              start=(hb == 0), stop=(hb == 1)),
                                         reads=["w2d", ("H1s", gi)], writes=[sk])
                                P.op("act", lambda e, sk=sk: e.activation(out=W["sqh"][:, 0, 0:127], in_=bk(sk)[:, 0:127], func=AF.Square),
                                     reads=[sk], writes=[("sqh", 0)])
                                P.op("pe", lambda e: e.matmul(bk(bM)[:, 0:127], lhsT=bonesb[:, :], rhs=W["sqh"][:, 0, 0:127], start=True, stop=True),
                                     reads=[("sqh", 0), "bonesb"], writes=[bM])
                                P.op("act", lambda e: e.activation(out=W["rs"][:, 1, 0:127], in_=bk(bM)[:, 0:127], func=AF.Ln, bias=EPS,
                                                                   scale=1.0 / 64), reads=[bM], writes=[("rs", 1)])
                                P.op("act", lambda e: e.activation(out=W["rs"][:, 1, 0:127], in_=W["rs"][:, 1, 0:127], func=AF.Exp, scale=-0.5),
                                     reads=[("rs", 1)], writes=[("rs", 1)])
                                P.op("dve", lambda e, sk=sk, pb=pb, blk=blk: e.scalar_tensor_tensor(
                                    out=KcT[pb:pb + 64, blk, 0:127], in0=bk(sk)[pb:pb + 64, 0:127], scalar=hg[pb:pb + 64, 4:5],
                                    in1=W["rs"][pb:pb + 64, 1, 0:127], op0=ALU.mult, op1=ALU.mult),
                                    reads=[sk, ("rs", 1), "hg"], writes=["KcT"])
                            else:
                                for hb in range(2):
                                    P.op("pe", lambda e, hb=hb, sk=sk, gi=gi: e.matmul(bk(sk)[0:127, 0:64], lhsT=W["H1s"][:, gi, hb, 0:127],
                                                                              rhs=W["w2d"][:, hb, 0:64], start=(hb == 0), stop=(hb == 1)),
                                         reads=["w2d", ("H1s", gi)], writes=[sk])
                                P.op("act", lambda e, sk=sk, g=g: e.activation(out=Vc[0:127, g, 0:64], in_=bk(sk)[0:127, 0:64], func=AF.Copy),
                                     reads=[sk], writes=["Vc"])
                P.flush()

            for l in range(n_b):
                with ExitStack() as ph:
                    def tsb(name, shape, dt):
                        return ph.enter_context(nc.sbuf_tensor(f"t_{name}_b{s}{l}", list(shape), dt))
                    wstate["n"] = 0
                    wl = []
                    for c_ in range(NCH):
                        for H_ in range(2):
                            wl.append((bwin_d[l][:, H_ * 2048:H_ * 2048 + 512], 512))
                            for jj_ in range(4):
                                o_ = H_ * 2048 + 512 + jj_ * 384
                                wl.append((bwin_d[l][:, o_:o_ + 384], 384))
                        wl.append((bwout_d[l][:, 0:512], 512))
                        wl.append((bwout_d[l][:, 512:1024], 512))
                    wstate["list"] = wl
                    W = dict(
                        sq=tsb("sq", (128, 2, 512), BF16), rs=tsb("rs", (128, 3, 512), F32),
                        hT=tsb("hT", (128, 2, 8, 512), BF16), w=tsb("w", (128, 2, 8, 512), BF16), sqh=tsb("sqh", (128, 2, 512), BF16),
                        QT=tsb("QT", (128, 4, 512), BF16), zs=tsb("zs", (128, 2, 4, 384), BF16), PT=tsb("PT", (128, 4, 512), BF16),
                        bm=tsb("bm", (128, 2, 2, 6, 128), BF16), acc=tsb("acc", (128, 4, 1024), BF16),
                        oT=tsb("oT", (128, 8, 512), BF16), den=tsb("den", (128, 4, 4), F32), den2=tsb("den2", (128, 4, 4), F32),
                        tt=tsb("tt", (128, 4, 4, 65), F32), U=tsb("U", (128, 2, 3, 4, 64), F32),
                        Wg=tsb("Wg", (128, 8, 48), BF16), sig=tsb("sig", (128, 4, 48), F32), ogc=tsb("ogc", (128, 8, 4, 64), F32),
                        impg=tsb("impg", (128, 2, 4, 32), F32), imt=tsb("imt", (128, 4, 32), F32), score=tsb("score", (128, 4, 32), F32),
                        top8=tsb("top8", (128, 8), F32), negsel=tsb("negsel", (128, 4, 32), BF16),
                        negselT=tsb("negselT", (64, 512), BF16),
                    )
                    W["xs"] = W["U"][:, :, :, :, :].rearrange("p a b c d -> p (a b c d)")[:, 0:1024].rearrange("p (s n) -> p s n", s=1)
                    P.op("pool", lambda e, l=l: e.dma_start(out=W["Wg"][:, :, :],
                                                            in_=bwin_d[l][:, 4096:4144].rearrange("(k p) n -> p k n", p=128)),
                         writes=["Wg"], dma="Wg")
                    for c in range(NCH):
                        if c == 0:
                            norm_chunk(P, W, c, 3 + l)
                        W["hsel"] = c % 2
                        for r in range(4):
                            for k in range(8):
                                P.op("pe", lambda e, r=r, k=k, hs=W["hsel"]: e.matmul(bk(bM)[:, r * 48:(r + 1) * 48],
                                                                        lhsT=W["hT"][:, hs, k, r * 128:(r + 1) * 128],
                                                                        rhs=W["Wg"][:, k, :], start=(k == 0), stop=(k == 7), skip_group_check=True),
                                     reads=[("hT", W["hsel"], k), "Wg"], writes=[bM])
                        P.op("act", lambda e: e.activation(out=W["sig"][:, :, :], in_=bk(bM)[:, 0:192].rearrange("p (r n) -> p r n", r=4),
                                                           func=AF.Sigmoid), reads=[bM], writes=["sig"])
                        for H in range(2):
                            slotq = load_w(P, W)
                            proj_norm_seq(P, W, slotq, [(jj * 128, (lambda jj=jj: W["QT"][:, jj, :]), ("QT", jj), 7 + l, 0.125)
                                                        for jj in range(4)])
                            seen = set()
                            tiles = []
                            bmslots = {0: load_bm(P, W, 4 * H)}
                            for jj in range(4):
                                j = 4 * H + jj
                                ntiles0 = len(tiles)
                                if jj not in bmslots:
                                    bmslots[jj] = (bmslots[jj - 1] + 1) % 2
                                bmslot = bmslots[jj]
                                tls = []
                                for m in range(2):
                                    h = HP[2 * j + m]
                                    g = h // 4
                                    gl = g - 2 * H
                                    hl = jj * 2 + m
                                    pb = 64 * m
                                    Okey = bO[ctr["O"] % 2]
                                    ctr["O"] += 1
                                    dsl = ctr["O"] % 2
                                    Ikey = bJ[hl % 2]
                                    tiles_h = []
                                    for r in range(4):
                                        qt = 4 * c + r
                                        ncq = 8 * qt + 8
                                        off = 120 - 8 * qt
                                        T = Tile()

                                        def A0(sk, ncq=ncq, off=off, pb=pb, g=g, jj=jj, r=r, bmslot=bmslot, m=m):
                                            P.op("pe", lambda e: e.matmul(
                                                bk(sk)[0:ncq, 0:128], lhsT=KcT[pb:pb + 64, g // 2, 0:ncq],
                                                rhs=W["QT"][pb:pb + 64, jj, r * 128:(r + 1) * 128],
                                                start=True, stop=False, skip_group_check=True), reads=["KcT", ("QT", jj)], writes=[sk])

                                        def A1(sk, ncq=ncq, off=off, pb=pb, g=g, jj=jj, r=r, bmslot=bmslot, m=m):
                                            for hl2 in range(2):
                                                P.op("pe", lambda e, hl2=hl2: e.matmul(
                                                    bk(sk)[0:ncq, 0:128], lhsT=identb[:, off:off + ncq], rhs=W["bm"][:, bmslot, m, 4 + hl2, :],
                                                    start=False, stop=(hl2 == 1), skip_group_check=True),
                                                    reads=["identb", ("bm", bmslot)], writes=[sk])

                                        def B(sk, pt, ncq=ncq, h=h):
                                            P.op("act", lambda e: e.activation(
                                                out=W["PT"][0:ncq, pt, 0:128], in_=bk(sk)[0:ncq, 0:128], func=AF.Exp),
                                                reads=[sk], writes=[("PT", pt)])

                                        def C(pt, ncq=ncq, Okey=Okey, Ikey=Ikey, g=g, r=r):
                                            P.op("pe", lambda e: e.matmul(
                                                bk(Okey)[:, r * 65:(r + 1) * 65], lhsT=W["PT"][0:ncq, pt, 0:128], rhs=Vc[0:ncq, g, :],
                                                start=(r == 0), stop=False, skip_group_check=True), reads=[("PT", pt), "Vc"], writes=[Okey])
                                            P.op("pe", lambda e: e.matmul(
                                                bk(Ikey)[:, r * 33:(r + 1) * 33], lhsT=W["PT"][0:ncq, pt, 0:128], rhs=ovxb[0:ncq, :],
                                                start=(r == 0), stop=False, skip_group_check=True), reads=[("PT", pt), "ovxb"], writes=[Ikey])
                                        T.A0, T.A1, T.B, T.C = A0, A1, B, C
                                        tiles_h.append(T)
                                    first_g = gl not in seen
                                    seen.add(gl)

                                    def post(Okey=Okey, Ikey=Ikey, dsl=dsl, first_g=first_g, gl=gl, h=h, hl=hl):
                                        Ov = bk(Okey)[:, 0:260].rearrange("p (r e) -> p r e", r=4)
                                        Iv = bk(Ikey)[:, 0:132].rearrange("p (r e) -> p r e", r=4)
                                        P.op("dve", lambda e: e.tensor_copy(out=W["tt"][:, dsl, :, :], in_=Ov), reads=[Okey], writes=[("tt", dsl)])
                                        P.op("dve", lambda e: e.tensor_scalar(
                                            out=W["den"][:, dsl, :], in0=W["tt"][:, dsl, :, 64], scalar1=1e-30, scalar2=None, op0=ALU.max),
                                            reads=[("tt", dsl)], writes=[("den", dsl)])
                                        P.op("dve", lambda e: e.reciprocal(out=W["den"][:, dsl, :], in_=W["den"][:, dsl, :]),
                                             reads=[("den", dsl)], writes=[("den", dsl)])
                                        dst = W["impg"][:, gl, :, :] if first_g else W["imt"][:, :, :]
                                        P.op("dve", lambda e: e.tensor_tensor(
                                            out=dst, in0=Iv[:, :, 0:32], in1=W["den"][:, dsl, :].unsqueeze(2).to_broadcast([128, 4, 32]), op=ALU.mult),
                                            reads=[Ikey, ("den", dsl)], writes=[("impg", gl) if first_g else "imt"])
                                        if not first_g:
                                            P.op("dve", lambda e: e.tensor_tensor(out=W["impg"][:, gl, :, :], in0=W["impg"][:, gl, :, :],
                                                                                  in1=W["imt"][:, :, :], op=ALU.add),
                                                 reads=["imt", ("impg", gl)], writes=[("impg", gl)])
                                        P.op("dve", lambda e: e.tensor_tensor(
                                            out=W["den2"][:, dsl, :], in0=W["den"][:, dsl, :], in1=W["sig"][:, :, h], op=ALU.mult),
                                            reads=[("den", dsl), "sig"], writes=[("den2", dsl)])
                                        P.op("dve", lambda e: e.tensor_tensor(
                                            out=W["ogc"][:, hl, :, :], in0=W["tt"][:, dsl, :, 0:64],
                                            in1=W["den2"][:, dsl, :].unsqueeze(2).to_broadcast([128, 4, 64]), op=ALU.mult),
                                            reads=[("tt", dsl), ("den2", dsl)], writes=[("ogc", hl)])
                                    tiles_h[-1].post = post
                                    tls.append(tiles_h)
                                tiles.extend(interleave(tls[0], tls[1]))
                                if jj + 1 < 4:
                                    def pre(jn=jj + 1, H=H):
                                        load_bm(P, W, 4 * H + jn)
                                    tiles[ntiles0].pre = pre
                            run_pipeline(P, tiles, bS3)
                            def emit_z(jj, H=H):
                                slotz = load_w(P, W)
                                zsl = jj % 2
                                for r in range(4):
                                    b = bJ[r % 2]
                                    for k in range(8):
                                        P.op("pe", lambda e, k=k, r=r, b=b, slotz=slotz, hs=W["hsel"]: e.matmul(
                                            bk(b)[:, 0:384], lhsT=W["hT"][:, hs, k, r * 128:(r + 1) * 128], rhs=W["w"][:, slotz, k, 0:384],
                                            start=(k == 0), stop=(k == 7)), reads=[("w", slotz), ("hT", W["hsel"], k)], writes=[b])
                                    P.op("act", lambda e, r=r, b=b, zsl=zsl: e.activation(out=W["zs"][:, zsl, r, :], in_=bk(b)[:, 0:384], func=AF.Silu),
                                         reads=[b], writes=[("zs", zsl)])
                            emit_z(0)
                            for gl in range(2):
                                P.op("dve", lambda e, gl=gl, c=c: e.tensor_tensor(out=W["score"][:, :, :], in0=W["impg"][:, gl, :, :],
                                                                                in1=MF[:, 4 * c:4 * c + 4, :], op=ALU.add),
                                     reads=[("impg", gl), "MF"], writes=["score"])
                                for r in range(4):
                                    P.op("dve", lambda e, r=r: e.max(out=W["top8"][:, :], in_=W["score"][:, r, :]), reads=["score"], writes=["top8"])
                                    P.op("dve", lambda e, r=r: e.tensor_scalar(out=W["negsel"][:, r, :], in0=W["score"][:, r, :],
                                                                               scalar1=W["top8"][:, 7:8], scalar2=NEGM, op0=ALU.is_lt, op1=ALU.mult),
                                         reads=["score", "top8"], writes=["negsel"])
                                for r in range(4):
                                    P.op("pe", lambda e, r=r, gl=gl: e.matmul(bk(bM)[32 * gl:32 * gl + 32, r * 128:(r + 1) * 128], lhsT=W["negsel"][:, r, :], rhs=identb[:, :],
                                                                       start=True, stop=True, skip_group_check=True),
                                         reads=["negsel", "identb"], writes=[bM])
                                P.op("act", lambda e, gl=gl: e.activation(out=W["negselT"][32 * gl:32 * gl + 32, :], in_=bk(bM)[32 * gl:32 * gl + 32, :], func=AF.Copy),
                                     reads=[bM], writes=[("negselT", gl)])
                            bmslots = {0: load_bm(P, W, 4 * H)}
                            alltiles = []
                            for jj in range(4):
                                j = 4 * H + jj
                                zsl = jj % 2
                                if jj not in bmslots:
                                    bmslots[jj] = (bmslots[jj - 1] + 1) % 2
                                bmslot = bmslots[jj]
                                tiles = []
                                tlmb = {}
                                for m in range(2):
                                    h = HP[2 * j + m]
                                    g = h // 4
                                    gl = g - 2 * H
                                    hl = jj * 2 + m
                                    for br in range(2):
                                        Okey = bO[m]
                                        ctr["O"] += 1
                                        dsl = ctr["O"] % 4
                                        if br == 0:
                                            def extra(P_, sk, cols, kt, last, gl=gl):
                                                P_.op("pe", lambda e: e.matmul(
                                                    bk(sk)[:, cols], lhsT=indb[32 * gl:32 * gl + 32, kt * 128:(kt + 1) * 128], rhs=W["negselT"][32 * gl:32 * gl + 32, cols],
                                                    start=False, stop=last, skip_group_check=True),
                                                    reads=["indb", ("negselT", gl)], writes=[sk])
                                            tl = attn_tiles(P, W, h, 64 * m, W["QT"][:, jj, :], ("QT", jj), KT0, "KT0", V0, "V0", g // 2, g, c,
                                                            list(range(0, 4 * c + 4)), lambda kt, c=c: (max(0, kt - 4 * c), 3),
                                                            lambda dl: {0: ["D0"], 1: ["D1"]}.get(dl, []), bmslot, m, Okey, extra_fn=extra)
                                        else:
                                            tl = attn_tiles(P, W, h, 64 * m, W["QT"][:, jj, :], ("QT", jj), KT1, "KT1", V1, "V1", g // 2, g, c,
                                                            list(range(max(0, 4 * c - 4), 4 * c + 4)),
                                                            lambda kt, c=c: (max(0, kt - 4 * c), min(3, kt - 4 * c + 4)),
                                                            lambda dl: {0: ["D0"], 1: ["D1"], 4: ["NM4"]}.get(dl, []), bmslot, m, Okey)

                                        def post(Okey=Okey, dsl=dsl, h=h, br=br, m=m, zsl=zsl, hl=hl):
                                            Ov = bk(Okey)[:, 0:260].rearrange("p (r e) -> p r e", r=4)
                                            P.op("dve", lambda e: e.tensor_copy(out=W["tt"][:, dsl, :, :], in_=Ov), reads=[Okey], writes=[("tt", dsl)])
                                            P.op("dve", lambda e: e.reciprocal(out=W["den"][:, dsl, :], in_=W["tt"][:, dsl, :, 64]),
                                                 reads=[("tt", dsl)], writes=[("den", dsl)])
                                            P.op("dve", lambda e: e.tensor_tensor(
                                                out=W["den2"][:, dsl, :], in0=W["den"][:, dsl, :], in1=W["sig"][:, :, 16 * (br + 1) + h], op=ALU.mult),
                                                reads=[("den", dsl), "sig"], writes=[("den2", dsl)])
                                            P.op("dve", lambda e: e.tensor_tensor(
                                                out=W["tt"][:, dsl, :, 0:64], in0=W["tt"][:, dsl, :, 0:64],
                                                in1=W["den2"][:, dsl, :].unsqueeze(2).to_broadcast([128, 4, 64]), op=ALU.mult),
                                                reads=[("tt", dsl), ("den2", dsl)], writes=[("tt", dsl)])
                                            zo = m * 192 + (br + 1) * 64
                                            P.op("pool", lambda e: e.tensor_tensor(
                                                out=W["U"][:, m, br, :, :], in0=W["tt"][:, dsl, :, 0:64], in1=W["zs"][:, zsl, :, zo:zo + 64], op=ALU.mult),
                                                reads=[("tt", dsl), ("zs", zsl)], writes=[("U", m, br)])
                                            if br == 1:
                                                zo0 = m * 192
                                                P.op("pool", lambda e: e.tensor_tensor(
                                                    out=W["U"][:, m, 2, :, :], in0=W["ogc"][:, hl, :, :], in1=W["zs"][:, zsl, :, zo0:zo0 + 64], op=ALU.mult),
                                                    reads=[("ogc", hl), ("zs", zsl)], writes=[("U", m, 2)])
                                                P.op("pool", lambda e: e.tensor_tensor(out=W["U"][:, m, 0, :, :], in0=W["U"][:, m, 0, :, :],
                                                                                       in1=W["U"][:, m, 1, :, :], op=ALU.add),
                                                     reads=[("U", m, 0), ("U", m, 1)], writes=[("U", m, 0)])
                                                P.op("pool", lambda e: e.tensor_tensor(out=W["acc"][:, :, h * 64:(h + 1) * 64], in0=W["U"][:, m, 0, :, :],
                                                                                       in1=W["U"][:, m, 2, :, :], op=ALU.add),
                                                     reads=[("U", m, 0), ("U", m, 2)], writes=["acc"])
                                        tl[-1].post = post
                                        tlmb[(m, br)] = tl
                                tiles = interleave(tlmb[(0, 0)], tlmb[(1, 0)]) + interleave(tlmb[(0, 1)], tlmb[(1, 1)])
                                if jj + 1 < 4:
                                    def pre1(jn=jj + 1, H=H):
                                        load_bm(P, W, 4 * H + jn)
                                    tiles[0].pre = pre1

                                    def pre2(jn=jj + 1):
                                        emit_z(jn)
                                    tiles[len(tiles) // 2].pre = pre2
                                alltiles.extend(tiles)
                            if H == 1 and c + 1 < NCH:
                                def prenorm(cn=c + 1, ll=l):
                                    norm_chunk(P, W, cn, 3 + ll, bJ[0])
                                alltiles[1].pre = prenorm
                            run_pipeline(P, alltiles, bS4)
                        transpose_out(P, W, c, bwout_d, l)
                    if l == n_b - 1:
                        W["nxs"] = 1
                        store_x(P, W, s)
                    P.flush()
    return nc


_CACHE = {}


def kernel(**inputs):
    sh = _prep_shared(inputs)
    x = np.asarray(inputs["x"], np.float32)
    nc = build()
    in_maps = []
    for i in range(8):
        m = dict(sh)
        m["x"] = np.ascontiguousarray(x[2 * i:2 * i + 2])
        in_maps.append(m)
    res = run_bass_kernel_spmd(nc, in_maps, core_ids=list(range(8)))
    return np.concatenate([r["y"] for r in res.results], axis=0).astype(np.float32)
```
